# Optimizing a Trainium2 kernel written in Bass

```python
import jax, jax.numpy as jnp
from jax import lax
import numpy as np

D_MODEL = 1024
BATCH = 8
SEQ = 2048
DEPTH = 1

HG_HEADS = 4
HG_KEY_DIM = 128
HG_VAL_DIM = 128
HG_KEY_WIDTH = HG_HEADS * HG_KEY_DIM
HG_VAL_WIDTH = HG_HEADS * HG_VAL_DIM
HG_CHUNK = 64
ATT_GROUPS = ((128, 1), (512, 4), (2048, 16))
N_ATT_GROUPS = 3
ATT_HEADS = 8
ATT_HEAD_DIM = 64
ATT_WIDTH = ATT_HEADS * ATT_HEAD_DIM
ATT_BLOCK = 128
ALIBI_MAX = 8.0
D_FF = 2816
IN_SIZES = (HG_KEY_WIDTH, HG_KEY_WIDTH, HG_VAL_WIDTH, HG_VAL_WIDTH, N_ATT_GROUPS * 3 * ATT_WIDTH, D_MODEL, D_MODEL)
IN_COLS = sum(IN_SIZES)
EPS = 1e-6
NEG_INF = -1e30

kernel_name = "hybrid_hgrn2_dilated_alibi_macaron"


def _rmsnorm(x, gain):
    xf = x.astype(jnp.float32)
    xf = xf * lax.rsqrt(jnp.mean(xf * xf, axis=-1, keepdims=True) + EPS)
    return xf.astype(x.dtype) * gain


def _swiglu(x, w_gate_up, w_down):
    a, b = jnp.split(x @ w_gate_up, 2, axis=-1)
    return (jax.nn.silu(a) * b) @ w_down


def _hgrn2(q, f_pre, i, og, lower_bound, out_gain):
    B, S, _ = q.shape
    dt = q.dtype
    H, K, V, C = HG_HEADS, HG_KEY_DIM, HG_VAL_DIM, HG_CHUNK
    lb = lower_bound.reshape(H, K)
    f = lb + (1.0 - lb) * jax.nn.sigmoid(f_pre.astype(jnp.float32).reshape(B, S, H, K))
    log_f = jnp.log(f)
    k = 1.0 - f
    qf = q.astype(jnp.float32).reshape(B, S, H, K)
    v = i.astype(jnp.float32).reshape(B, S, H, V)
    Sp = -(-S // C) * C
    pad = Sp - S

    def to_chunks(a):
        a = jnp.pad(a, ((0, 0), (0, pad), (0, 0), (0, 0)))
        return a.reshape(B, Sp // C, C, H, a.shape[-1]).transpose(1, 0, 3, 2, 4)

    qc, kc, vc, gc = to_chunks(qf), to_chunks(k), to_chunks(v), to_chunks(log_f)
    Gc = jnp.cumsum(gc, axis=3)
    causal = jnp.tril(jnp.ones((C, C), dtype=bool))[:, :, None]

    def step(state, inp):
        q_, k_, v_, G = inp
        diff = G[:, :, :, None, :] - G[:, :, None, :, :]
        decay = jnp.where(causal, jnp.exp(jnp.where(causal, diff, 0.0)), 0.0)
        scores = jnp.einsum('bhtk,bhsk,bhtsk->bhts', q_, k_, decay)
        o = (jnp.einsum('bhts,bhsv->bhtv', scores, v_)
             + jnp.einsum('bhtk,bhkv->bhtv', q_ * jnp.exp(G), state))
        G_last = G[:, :, -1:, :]
        new_state = (jnp.exp(G_last[:, :, 0, :])[..., None] * state
                     + jnp.einsum('bhsk,bhsv->bhkv', k_ * jnp.exp(G_last - G), v_))
        return new_state, o

    state0 = jnp.zeros((B, H, K, V), jnp.float32)
    _, oc = lax.scan(step, state0, (qc, kc, vc, Gc))
    o = oc.transpose(1, 0, 3, 2, 4).reshape(B, Sp, H, V)[:, :S]
    o = o * lax.rsqrt(jnp.mean(o * o, axis=-1, keepdims=True) + EPS)
    o = o.reshape(B, S, H * V).astype(dt) * out_gain
    return o * jax.nn.silu(og)


def _dilated_group(q, k, v, window, dilation, slopes):
    B, S, H, E = q.shape
    BLK = ATT_BLOCK
    span = window // dilation
    unit = dilation * BLK
    Sp = -(-S // unit) * unit
    L = Sp // dilation
    nb = L // BLK

    def to_blocks(a):
        a = jnp.pad(a, ((0, 0), (0, Sp - S), (0, 0), (0, 0)))
        return a.reshape(B, nb, BLK, dilation, H, E)

    def band(a):
        prev = jnp.pad(a, ((0, 0), (1, 0), (0, 0), (0, 0), (0, 0), (0, 0)))[:, :-1]
        return jnp.concatenate([prev, a], axis=2)

    qb = to_blocks(q)
    kband, vband = band(to_blocks(k)), band(to_blocks(v))
    scores = jnp.einsum('bnqrhe,bnkrhe->bnrhqk', qb, kband).astype(jnp.float32) * (E ** -0.5)
    qi = jnp.arange(BLK)[:, None]
    kj = jnp.arange(2 * BLK)[None, :]
    delta = qi + BLK - kj
    blk_idx = jnp.arange(nb)[:, None, None]
    valid = (delta >= 0) & (delta <= span) & ((blk_idx > 0) | (kj >= BLK))
    bias = -slopes[:, None, None] * (dilation * delta).astype(jnp.float32)
    scores = scores + bias[None, None, None]
    scores = jnp.where(valid[None, :, None, None], scores, NEG_INF)
    lse = jax.nn.logsumexp(scores, axis=-1)
    p = jnp.exp(scores - lse[..., None]).astype(v.dtype)
    o = jnp.einsum('bnrhqk,bnkrhe->bnqrhe', p, vband).reshape(B, Sp, H, E)[:, :S]
    lse = lse.transpose(0, 1, 4, 2, 3).reshape(B, Sp, H)[:, :S]
    return o, lse


def _mixer(u, w_in, lower_bound, hg_out_norm, w_branch_hg, w_branch_att, w_out):
    B, S, _ = u.shape
    dt = u.dtype
    points = [int(p) for p in np.cumsum(IN_SIZES)[:-1]]
    hg_q, hg_f, hg_i, hg_og, att, gate_hg, gate_att = jnp.split(u @ w_in, points, axis=-1)
    y_hg = _hgrn2(hg_q, hg_f, hg_i, hg_og, lower_bound, hg_out_norm)
    n_heads = N_ATT_GROUPS * ATT_HEADS
    slopes = jnp.exp2(-ALIBI_MAX * jnp.arange(1, n_heads + 1, dtype=jnp.float32) / n_heads)
    qkv = att.reshape(B, S, N_ATT_GROUPS, 3, ATT_HEADS, ATT_HEAD_DIM)
    outs, lses = [], []
    for g, (window, dilation) in enumerate(ATT_GROUPS):
        o, lse = _dilated_group(qkv[:, :, g, 0], qkv[:, :, g, 1], qkv[:, :, g, 2], window, dilation,
                                slopes[g * ATT_HEADS:(g + 1) * ATT_HEADS])
        outs.append(o)
        lses.append(lse)
    wts = jax.nn.softmax(jnp.stack(lses), axis=0).astype(dt)
    y_att = jnp.einsum('gbshe,gbsh->bshe', jnp.stack(outs), wts).reshape(B, S, ATT_WIDTH)
    merged = (jax.nn.sigmoid(gate_hg) * (y_hg @ w_branch_hg)
              + jax.nn.sigmoid(gate_att) * (y_att @ w_branch_att))
    return merged @ w_out


def setup_inputs(seed: int = 0) -> dict:
    key = jax.random.key(seed)
    ks = jax.random.split(key, 16)

    def nrm(k, shape, scale):
        return jax.random.normal(k, shape, jnp.float32) * scale

    return {
        "x": nrm(ks[0], (BATCH, SEQ, D_MODEL), 1.0),
        "ffn1_norm": 1.0 + nrm(ks[1], (DEPTH, D_MODEL), 0.02),
        "ffn1_w_gate_up": nrm(ks[2], (DEPTH, D_MODEL, 2 * D_FF), D_MODEL ** -0.5),
        "ffn1_w_down": nrm(ks[3], (DEPTH, D_FF, D_MODEL), D_FF ** -0.5),
        "mix_norm": 1.0 + nrm(ks[4], (DEPTH, D_MODEL), 0.02),
        "w_in": nrm(ks[5], (DEPTH, D_MODEL, IN_COLS), D_MODEL ** -0.5),
        "hg_lower_bounds": nrm(ks[6], (DEPTH + 1, HG_KEY_WIDTH), 0.1),
        "hg_out_norm": 1.0 + nrm(ks[7], (DEPTH, HG_VAL_WIDTH), 0.02),
        "w_branch_hg": nrm(ks[8], (DEPTH, HG_VAL_WIDTH, D_MODEL), HG_VAL_WIDTH ** -0.5),
        "w_branch_att": nrm(ks[9], (DEPTH, ATT_WIDTH, D_MODEL), ATT_WIDTH ** -0.5),
        "w_out": nrm(ks[10], (DEPTH, D_MODEL, D_MODEL), D_MODEL ** -0.5),
        "ffn2_norm": 1.0 + nrm(ks[11], (DEPTH, D_MODEL), 0.02),
        "ffn2_w_gate_up": nrm(ks[12], (DEPTH, D_MODEL, 2 * D_FF), D_MODEL ** -0.5),
        "ffn2_w_down": nrm(ks[13], (DEPTH, D_FF, D_MODEL), D_FF ** -0.5),
        "final_norm": 1.0 + nrm(ks[14], (D_MODEL,), 0.02),
    }


def reference(x, ffn1_norm, ffn1_w_gate_up, ffn1_w_down, mix_norm, w_in, hg_lower_bounds, hg_out_norm,
              w_branch_hg, w_branch_att, w_out, ffn2_norm, ffn2_w_gate_up, ffn2_w_down, final_norm):
    lower_bounds = jnp.cumsum(jax.nn.softmax(hg_lower_bounds.astype(jnp.float32), axis=0), axis=0)
    h = x
    for l in range(DEPTH):
        h = h + 0.5 * _swiglu(_rmsnorm(h, ffn1_norm[l]), ffn1_w_gate_up[l], ffn1_w_down[l])
        h = h + _mixer(_rmsnorm(h, mix_norm[l]), w_in[l], lower_bounds[l], hg_out_norm[l],
                       w_branch_hg[l], w_branch_att[l], w_out[l])
        h = h + 0.5 * _swiglu(_rmsnorm(h, ffn2_norm[l]), ffn2_w_gate_up[l], ffn2_w_down[l])
    return _rmsnorm(h, final_norm)
```

```python
import bisect
from contextlib import ExitStack

import numpy as np
import concourse.bass as bass
import concourse.mybir as mybir
from concourse.bass_utils import run_bass_kernel_spmd

F32 = mybir.dt.float32
BF16 = mybir.dt.bfloat16
AF = mybir.ActivationFunctionType
ALU = mybir.AluOpType

S = 2048
D = 1024
DFF = 2816
NJ = DFF // 128
EPS = 1e-6
NCORES = 8
ATT_GROUPS = ((128, 1), (512, 4), (2048, 16))
NEG = -30000.0
FFN_GROUPS = ((0, 1, 2, 3, 4, 5, 6, 7), (8, 9, 10, 11, 12, 13, 14), (15, 16, 17, 18, 19, 20, 21))
GMAX = 8
KNOBS = {"hg_mode": "rr", "hg_bw": 1.0, "hg_snap": "pool", "hg_dense": 2, "hg_a1_dense": 0, "hg_a2_dense": 0, "hg_pst": "act", "hg_order": 0, "flush": 0, "hg_mul": "pool", "at_uw": 2.0, "at_pw": 1.0}


class Tok:
    __slots__ = ("w", "r")

    def __init__(self):
        self.w = None
        self.r = {}


def toks(*shape):
    if len(shape) == 1:
        return [Tok() for _ in range(shape[0])]
    return [toks(*shape[1:]) for _ in range(shape[0])]


class Group:
    def __init__(self, kids):
        self.kids = kids


def _expand(ts):
    out = []
    for t in ts:
        if isinstance(t, Group):
            out.extend(t.kids)
        else:
            out.append(t)
    return out


class DSem:
    def __init__(self, h):
        self.h = h
        self.count = 0


class Prog:
    ENG = ("pe", "act", "dve", "pool", "sp")

    def __init__(self, nc, stack):
        self.nc = nc
        self.stack = stack
        self.ops = {e: [] for e in self.ENG}
        self.needed = {e: set() for e in self.ENG}
        self.need_sorted = {e: [] for e in self.ENG}
        self.evval = {e: {} for e in self.ENG}
        self.ecount = {e: 0 for e in self.ENG}
        self.flushed = {e: 0 for e in self.ENG}
        self.waited = {e: {} for e in self.ENG}
        self.esem = {e: stack.enter_context(nc.semaphore("es_" + e)) for e in self.ENG}
        self.nds = 0

    def dsem(self):
        self.nds += 1
        return DSem(self.stack.enter_context(self.nc.semaphore("ds%d" % self.nds)))

    def _waits(self, eng, reads, writes):
        ws = []
        for t in reads:
            if t.w is not None:
                ws.append(t.w)
        for t in writes:
            if t.w is not None:
                ws.append(t.w)
            for k, ev in t.r.items():
                if k == eng and eng == "pe":
                    continue
                ws.append(ev)
        out = []
        for ev in ws:
            if ev[0] == "E":
                if ev[1] == eng and eng == "pe":
                    continue
                if ev[2] >= self.flushed[ev[1]]:
                    self.needed[ev[1]].add(ev[2])
            out.append(ev)
        return out

    def _mark(self, ev, key, reads, writes):
        for t in reads:
            t.r[key] = ev
        for t in writes:
            t.w = ev
            t.r = {}

    def op(self, eng, fn, reads=(), writes=()):
        reads, writes = _expand(reads), _expand(writes)
        waits = self._waits(eng, reads, writes)
        ev = ("E", eng, len(self.ops[eng]))
        self.ops[eng].append(dict(waits=waits, fn=fn, kind="c"))
        self._mark(ev, eng, reads, writes)
        return ev

    def dma(self, q, fn, ds, reads=(), writes=()):
        reads, writes = _expand(reads), _expand(writes)
        waits = self._waits("dma", reads, writes)
        ds.count += 16
        ev = ("D", ds, ds.count)
        self.ops[q].append(dict(waits=waits, fn=fn, kind="d", ds=ds))
        self._mark(ev, ("D", id(ds)), reads, writes)
        return ev

    def dma_group(self, q, items, ds):
        final = ("D", ds, ds.count + 16 * len(items))
        prepared = []
        for fn, reads, writes in items:
            reads, writes = _expand(reads), _expand(writes)
            prepared.append((fn, reads, writes, self._waits("dma", reads, writes)))
        for fn, reads, writes, waits in prepared:
            ds.count += 16
            self.ops[q].append(dict(waits=waits, fn=fn, kind="d", ds=ds))
            self._mark(final, ("D", id(ds)), reads, writes)
        return final

    def wait_all(self, eng, evs):
        for ev in evs:
            if ev[0] == "E" and ev[2] >= self.flushed[ev[1]]:
                self.needed[ev[1]].add(ev[2])
        self.ops[eng].append(dict(waits=list(evs), fn=None, kind="w"))

    def barrier(self):
        last = {}
        for e in self.ENG:
            for i in range(len(self.ops[e]) - 1, -1, -1):
                if self.ops[e][i]["kind"] == "c":
                    last[e] = ("E", e, i)
                    break
        for e in self.ENG:
            self.wait_all(e, [ev for k, ev in last.items() if k != e])

    def _resolve(self, ev):
        e, idx = ev[1], ev[2]
        ns = self.need_sorted[e]
        k = bisect.bisect_left(ns, idx)
        return self.evval[e][ns[k]]

    def flush(self):
        for e in self.ENG:
            n = len(self.ops[e])
            for i in range(n - 1, self.flushed[e] - 1, -1):
                if self.ops[e][i]["kind"] == "c":
                    self.needed[e].add(i)
                    break
            c = self.ecount[e]
            for i in range(self.flushed[e], n):
                if i in self.needed[e]:
                    c += 1
                    self.evval[e][i] = c
                    self.need_sorted[e].append(i)
            self.ecount[e] = c
        prog = self

        def run(e, h):
            waited = prog.waited[e]
            for i in range(prog.flushed[e], len(prog.ops[e])):
                o = prog.ops[e][i]
                for ev in o["waits"]:
                    if ev[0] == "E":
                        sem = prog.esem[ev[1]]
                        val = prog._resolve(ev)
                        key = ("E", ev[1])
                    else:
                        sem = ev[1].h
                        val = ev[2]
                        key = ("D", id(ev[1]))
                    if waited.get(key, 0) >= val:
                        continue
                    waited[key] = val
                    h.wait_ge(sem, val)
                if o["fn"] is None:
                    continue
                inst = o["fn"](h)
                if o["kind"] == "d":
                    inst.then_inc(o["ds"].h, 16)
                elif i in prog.needed[e]:
                    inst.then_inc(prog.esem[e], 1)

        with self.nc.Block() as block:
            @block.tensor
            def _(h):
                run("pe", h)

            @block.scalar
            def _(h):
                run("act", h)

            @block.vector
            def _(h):
                run("dve", h)

            @block.gpsimd
            def _(h):
                run("pool", h)

            @block.sync
            def _(h):
                run("sp", h)

        for e in self.ENG:
            self.flushed[e] = len(self.ops[e])


def build_program(stage=99):
    nc = bass.Bass("TRN2", target_bir_lowering=False)

    def dram(name, shape, kind="ExternalInput"):
        return nc.dram_tensor(name, list(shape), F32, kind=kind).ap()

    xT_d = dram("xT", [D, S])
    wgu_d = [dram("wgu1", [NJ, 128, 2048]), dram("wgu2", [NJ, 128, 2048])]
    wd_d = [dram("wd1", [NJ, 128, 1024]), dram("wd2", [NJ, 128, 1024])]
    whg_d = dram("whg", [4, 128, 4096])
    watt_d = dram("watt", [4, 3, 128, 3072])
    wmA_d = dram("wmA", [8, 128, 1536])
    wmB_d = dram("wmB", [8, 128, 1536])
    wo_d = dram("wo", [2, 128, 4096])
    vecs_d = dram("vecs", [128, 48])
    biasT_d = dram("biasT", [4, 3, 128, 512])
    cst_d = dram("cst", [128, 640])
    outT_d = dram("outT", [D, S], kind="ExternalOutput")

    with ExitStack() as st:
        P = Prog(nc, st)

        def sbuf(stack, name, shape, dt):
            return stack.enter_context(nc.sbuf_tensor("s_" + name, list(shape), dt))

        def MM(out, lhsT, rhs, start, stop, reads, writes):
            return P.op("pe", lambda h: h.matmul(out, lhsT=lhsT, rhs=rhs, start=start, stop=stop), reads, writes)

        def TR(out, in_, ident, reads, writes):
            return P.op("pe", lambda h: h.transpose(out, in_, ident), reads, writes)

        def ACT(out, in_, func, reads, writes, **kw):
            return P.op("act", lambda h: h.activation(out=out, in_=in_, func=func, **kw), reads, writes)

        def TT(eng, out, in0, in1, op, reads, writes):
            return P.op(eng, lambda h: h.tensor_tensor(out=out, in0=in0, in1=in1, op=op), reads, writes)

        def STT(out, in0, scalar, in1, op0, op1, reads, writes):
            return P.op("dve", lambda h: h.scalar_tensor_tensor(out=out, in0=in0, scalar=scalar, in1=in1, op0=op0, op1=op1), reads, writes)

        def TS(eng, out, in0, s1, s2, op0, op1, reads, writes):
            return P.op(eng, lambda h: h.tensor_scalar(out=out, in0=in0, scalar1=s1, scalar2=s2, op0=op0, op1=op1), reads, writes)

        def CP(eng, out, in_, reads, writes):
            return P.op(eng, lambda h: h.tensor_copy(out=out, in_=in_), reads, writes)

        def MS(eng, ap, val, writes):
            return P.op(eng, lambda h: h.memset(ap, val), (), writes)

        def DMA(q, out, in_, ds, reads=(), writes=()):
            return P.dma(q, lambda h: h.dma_start(out=out, in_=in_), ds, reads, writes)

        def _dfn(out, in_):
            return lambda h: h.dma_start(out=out, in_=in_)

        def DMAG(q, items, ds):
            return P.dma_group(q, [(_dfn(o, i), r, w) for (o, i, r, w) in items], ds)

        hT = sbuf(st, "hT", [128, 8, S], F32)
        xnT = sbuf(st, "xnT", [128, 8, S], BF16)
        vecs = sbuf(st, "vecs", [128, 48], F32)
        cstb = sbuf(st, "cstb", [128, 640], BF16)
        ones_bf = sbuf(st, "ones_bf", [128, 128], BF16)
        ones_f = sbuf(st, "ones_f", [128, 64], F32)
        oml = sbuf(st, "oml", [128, 4], F32)
        lbd = sbuf(st, "lbd", [128, 4], F32)
        sq = [sbuf(st, "sq%d" % i, [128, 512], BF16) for i in range(2)]
        lnv = sbuf(st, "lnv", [128, 512], F32)
        rstd = sbuf(st, "rstd", [128, 512], F32)
        banks = [st.enter_context(nc.psum_tensor("bank%d" % i, [128, 512], F32)) for i in range(8)]

        hT_t = toks(8, 4)
        xn_t = toks(8, 4)
        vecs_t, cst_t, ones_t, rmask_t, oml_t, lbd_t, lnv_t, rstd_t = toks(8)
        sq_t = toks(2)
        bq = toks(8, 4)
        bt = [Group(bq[i]) for i in range(8)]
        ident = cstb[:, 0:128]
        mask4 = cstb[:, 128:640]

        def tsl(tt):
            return slice(tt * 512, (tt + 1) * 512)

        ds_x = P.dsem()
        ds_c = P.dsem()
        ds_v = P.dsem()
        ds_out = P.dsem()

        DMA("sp", vecs[:], vecs_d, ds_v, writes=[vecs_t])
        ds_xs = [ds_x] + [P.dsem() for _ in range(3)]
        for tt in range(4):
            dep = [hT_t[0][tt - 1]] if tt > 0 else []
            DMAG("sp", [(hT[:, c, tsl(tt)], xT_d[c * 128:(c + 1) * 128, tsl(tt)], dep, [hT_t[c][tt]]) for c in range(8)], ds_xs[tt])
        DMA("pool", cstb[:], cst_d, ds_c, writes=[cst_t])
        MS("dve", ones_bf[:], 1.0, [ones_t])
        MS("dve", ones_f[:], 1.0, [ones_t])
        TT("dve", lbd[:], vecs[:, 40:44], vecs[:, 36:40], ALU.subtract, [vecs_t], [lbd_t])
        ACT(oml[:], lbd[:], AF.Sigmoid, [lbd_t], [oml_t])

        def norm(gcol, out_fn, nfeat=D):
            for tt in range(4):
                norm_tile(tt, gcol, out_fn, nfeat)

        def norm_tile(tt, gcol, out_fn, nfeat=D):
            if True:
                ts = tsl(tt)
                sb_, rb_ = (7, 6) if tt % 2 == 0 else (4, 5)
                for c in range(8):
                    ACT(sq[c % 2][:], hT[:, c, ts], AF.Square, [hT_t[c][tt]], [sq_t[c % 2]])
                    MM(banks[sb_][:], ones_bf[:], sq[c % 2][:], c == 0, c == 7, [sq_t[c % 2], ones_t], [bt[sb_]])
                ACT(lnv[:], banks[sb_][:], AF.Ln, [bt[sb_]], [lnv_t], scale=1.0 / nfeat, bias=EPS)
                ACT(banks[rb_][:], lnv[:], AF.Exp, [lnv_t], [bt[rb_]], scale=-0.5)
                for c in range(8):
                    out_fn(c, tt, ts, gcol, rb_)

        def norm_to_xn(c, tt, ts, gcol, rb_):
            STT(xnT[:, c, ts], hT[:, c, ts], vecs[:, gcol + c:gcol + c + 1], banks[rb_][:], ALU.mult, ALU.mult,
                [hT_t[c][tt], bt[rb_], vecs_t], [xn_t[c][tt]])

        def norm_inplace(c, tt, ts, gcol, rb_):
            STT(hT[:, c, ts], hT[:, c, ts], vecs[:, gcol + c:gcol + c + 1], banks[rb_][:], ALU.mult, ALU.mult,
                [hT_t[c][tt], bt[rb_], vecs_t], [hT_t[c][tt]])

        def ffn(which, gcol):
            wgu, wd = wgu_d[which], wd_d[which]
            with ExitStack() as ph:
                actT = sbuf(ph, "actT%d" % which, [128, GMAX, S], BF16)
                wgu_sb = [sbuf(ph, "wgu_sb%d_%d" % (which, i), [128, 2048], BF16) for i in range(3)]
                wd_sb = sbuf(ph, "wd_sb%d" % which, [128, GMAX, 1024], BF16)
                sg = [sbuf(ph, "sg%d_%d" % (which, i), [128, 512], F32) for i in range(2)]
                act_t = toks(GMAX, 4)
                wgu_t = toks(3)
                wd_t = toks(GMAX)
                sg_t = toks(2)
                ds_wgu = [P.dsem() for _ in range(3)]
                ds_wd = P.dsem()

                def load_wgu(j):
                    DMA("pool", wgu_sb[j % 3][:], wgu[j], ds_wgu[j % 3], writes=[wgu_t[j % 3]])

                load_wgu(0)
                load_wgu(1)
                for grp in FFN_GROUPS:
                    for jj, j in enumerate(grp):
                        if j + 2 < NJ:
                            load_wgu(j + 2)
                        if jj == 1:
                            DMAG("pool", [(wd_sb[:, jj2, :], wd[j2], (), [wd_t[jj2]]) for jj2, j2 in enumerate(grp)], ds_wd)
                        w = wgu_sb[j % 3]
                        for tt in range(4):
                            if j == 0:
                                norm_tile(tt, gcol, norm_to_xn)
                            ts = tsl(tt)
                            gb, ub = tt % 2, 2 + tt % 2
                            for kc in range(8):
                                MM(banks[gb][:], w[:, kc * 128:(kc + 1) * 128], xnT[:, kc, ts], kc == 0, kc == 7,
                                   [wgu_t[j % 3], xn_t[kc][tt]], [bt[gb]])
                            for kc in range(8):
                                MM(banks[ub][:], w[:, (8 + kc) * 128:(9 + kc) * 128], xnT[:, kc, ts], kc == 0, kc == 7,
                                   [wgu_t[j % 3], xn_t[kc][tt]], [bt[ub]])
                            ACT(sg[tt % 2][:], banks[gb][:], AF.Silu, [bt[gb]], [sg_t[tt % 2]])
                            TT("dve", actT[:, jj, ts], sg[tt % 2][:], banks[ub][:], ALU.mult, [sg_t[tt % 2], bt[ub]], [act_t[jj][tt]])
                    n = len(grp)
                    for m in range(8):
                        for tt in range(4):
                            ts = tsl(tt)
                            db = 4 + (m * 4 + tt) % 2
                            for jj in range(n):
                                MM(banks[db][:], wd_sb[:, jj, m * 128:(m + 1) * 128], actT[:, jj, ts], jj == 0, jj == n - 1,
                                   [wd_t[jj], act_t[jj][tt]], [bt[db]])
                            STT(hT[:, m, ts], banks[db][:], 0.5, hT[:, m, ts], ALU.mult, ALU.add,
                                [bt[db], hT_t[m][tt]], [hT_t[m][tt]])
                P.barrier()
                if KNOBS["flush"]:
                    P.flush()

        def merge(wm_d, yT, y_t, tag):
            with ExitStack() as ph:
                wm_sb = [sbuf(ph, "wm_sb%s%d" % (tag, i), [128, 1536], BF16) for i in range(2)]
                wo_sb = sbuf(ph, "wo_sb" + tag, [128, 4096], BF16)
                sA = [sbuf(ph, "sA%s%d" % (tag, i), [128, 512], F32) for i in range(2)]
                mT = sbuf(ph, "mT" + tag, [128, 4, S], BF16)
                wm_t = toks(2)
                wo_t, = toks(1)
                sA_t = toks(2)
                mT_t = toks(4, 4)
                ds_wm = [P.dsem() for _ in range(2)]
                ds_wo = P.dsem()

                def load_wm(dm):
                    DMA("pool", wm_sb[dm % 2][:], wm_d[dm], ds_wm[dm % 2], writes=[wm_t[dm % 2]])

                load_wm(0)
                for dmg in range(2):
                    for dmi in range(4):
                        dm = dmg * 4 + dmi
                        if dm + 1 < 8:
                            load_wm(dm + 1)
                        if dmi == 1:
                            DMAG("pool", [(wo_sb[:, 0:2048], wo_d[dmg, :, 0:2048], (), [wo_t]),
                                          (wo_sb[:, 2048:4096], wo_d[dmg, :, 2048:4096], (), [wo_t])], ds_wo)
                        w = wm_sb[dm % 2]
                        for tt in range(4):
                            ts = tsl(tt)
                            gb, bb = tt % 2, 2 + tt % 2
                            for kc in range(8):
                                MM(banks[gb][:], w[:, kc * 128:(kc + 1) * 128], xnT[:, kc, ts], kc == 0, kc == 7,
                                   [wm_t[dm % 2], xn_t[kc][tt]], [bt[gb]])
                            for c in range(4):
                                MM(banks[bb][:], w[:, (8 + c) * 128:(9 + c) * 128], yT[:, c, ts], c == 0, c == 3,
                                   [wm_t[dm % 2], y_t[c][tt]], [bt[bb]])
                            ACT(sA[tt % 2][:], banks[gb][:], AF.Sigmoid, [bt[gb]], [sA_t[tt % 2]])
                            TT("dve", mT[:, dmi, ts], sA[tt % 2][:], banks[bb][:], ALU.mult, [sA_t[tt % 2], bt[bb]], [mT_t[dmi][tt]])
                    for m in range(8):
                        for tt in range(4):
                            ts = tsl(tt)
                            db = 4 + (m * 4 + tt) % 2
                            for dmi in range(4):
                                MM(banks[db][:], wo_sb[:, dmi * 1024 + m * 128: dmi * 1024 + (m + 1) * 128], mT[:, dmi, ts],
                                   dmi == 0, dmi == 3, [wo_t, mT_t[dmi][tt]], [bt[db]])
                            TT("dve", hT[:, m, ts], banks[db][:], hT[:, m, ts], ALU.add, [bt[db], hT_t[m][tt]], [hT_t[m][tt]])
                P.barrier()
                if KNOBS["flush"]:
                    P.flush()

        def hgrn2():
            with ExitStack() as ph:
                y_hgT = sbuf(ph, "y_hgT", [128, 4, S], BF16)
                yhg_t = toks(4, 4)
                with ExitStack() as wk:
                    whg_sb = [sbuf(wk, "whg_sb%d" % i, [128, 4096], BF16) for i in range(2)]
                    rmask = sbuf(wk, "rmask", [128, 512], F32)
                    lnoml = sbuf(wk, "lnoml", [128, 4], F32)
                    T1 = sbuf(wk, "T1", [128, 512], F32)
                    T2 = sbuf(wk, "T2", [128, 512], F32)
                    T3 = sbuf(wk, "T3", [128, 512], F32)
                    T4 = sbuf(wk, "T4", [128, 512], F32)
                    T5 = sbuf(wk, "T5", [128, 512], F32)
                    T7 = sbuf(wk, "T7", [128, 512], F32)
                    dcs = [sbuf(wk, "dcs%d" % i, [128, 32], F32) for i in range(2)]
                    qgT = [sbuf(wk, "qgT%d" % i, [128, S], BF16) for i in range(2)]
                    kgT = [sbuf(wk, "kgT%d" % i, [128, S], BF16) for i in range(2)]
                    v_tok = [sbuf(wk, "v_tok%d" % i, [128, S], BF16) for i in range(2)]
                    sog = [sbuf(wk, "sog%d" % i, [128, S], BF16) for i in range(2)]
                    kg_tok = sbuf(wk, "kg_tok", [128, S], BF16)
                    scm = sbuf(wk, "scm", [128, S], BF16)
                    S_bf = sbuf(wk, "S_bf", [128, 32, 128], BF16)
                    S32 = [sbuf(wk, "S32_%d" % i, [128, 128], F32) for i in range(3)]
                    Pp = [sbuf(wk, "Pp%d" % i, [128, 128], F32) for i in range(3)]
                    Pst = [sbuf(wk, "Pst%d" % i, [128, 512], F32) for i in range(2)]
                    osq, t1 = sq[0], lnv
                    whg_t = toks(2)
                    T1_t, T2_t, T3_t, T4_t, T5_t, T7_t, lnoml_t = toks(7)
                    osq_t, t1_t = sq_t[0], lnv_t
                    Pst_t = toks(2)
                    dcs_t = toks(2, 4)
                    qg_t = toks(2, 4)
                    kg_t = toks(2, 4)
                    v_t = toks(2, 4)
                    sog_t = toks(2, 4)
                    kgk_t = toks(4)
                    scm_t = toks(4)
                    Sbf_t = toks(32)
                    S32_t = toks(3)
                    Pp_t = toks(3)
                    ds_whg = [P.dsem() for _ in range(2)]
                    cnt = {"a": 0}
                    A_BANKS = (0, 1, 2)

                    def abank():
                        bk = A_BANKS[cnt["a"] % 3]
                        cnt["a"] += 1
                        return bk

                    MS("dve", rmask[:], 1.0, [rmask_t])
                    MS("dve", rmask[:, 0:512:64], 0.0, [rmask_t])
                    ACT(lnoml[:], oml[:], AF.Ln, [oml_t], [lnoml_t])

                    def load_whg(hh):
                        b = hh % 2
                        DMAG("pool", [(whg_sb[b][:, 0:2048], whg_d[hh, :, 0:2048], (), [whg_t[b]]),
                                      (whg_sb[b][:, 2048:4096], whg_d[hh, :, 2048:4096], (), [whg_t[b]])], ds_whg[b])

                    def stage_a1(hh):
                        b = hh % 2
                        w = whg_sb[b]
                        wt = whg_t[b]

                        def wcol(s_, kc):
                            return w[:, (s_ * 8 + kc) * 128:(s_ * 8 + kc + 1) * 128]

                        for tt in range(4):
                            ts = tsl(tt)
                            fb = abank()
                            for kc in range(8):
                                MM(banks[fb][:], wcol(1, kc), xnT[:, kc, ts], kc == 0, kc == 7, [wt, xn_t[kc][tt]], [bt[fb]])
                            ACT(T1[:], banks[fb][:], AF.Exp, [bt[fb]], [T1_t])
                            if KNOBS["hg_a1_dense"] == 0:
                                yield
                            ACT(T1[:], T1[:], AF.Ln, [T1_t], [T1_t], bias=1.0)
                            if KNOBS["hg_a1_dense"] == 0:
                                yield
                            ACT(T2[:], T1[:], AF.Exp, [T1_t, lnoml_t], [T2_t], scale=-1.0, bias=lnoml[:, hh:hh + 1])
                            if KNOBS["hg_a1_dense"] == 0:
                                yield
                            ACT(T3[:], T2[:], AF.Ln, [T2_t], [T3_t], scale=-1.0, bias=1.0)
                            if KNOBS["hg_a1_dense"] == 0:
                                yield
                            P.op("dve", lambda h: h.tensor_tensor_scan(out=T4[:], data0=rmask[:], data1=T3[:], initial=0.0,
                                                                        op0=ALU.mult, op1=ALU.add),
                                 [rmask_t, T3_t], [T4_t])
                            qb = abank()
                            for kc in range(8):
                                MM(banks[qb][:], wcol(0, kc), xnT[:, kc, ts], kc == 0, kc == 7, [wt, xn_t[kc][tt]], [bt[qb]])
                            if KNOBS["hg_a1_dense"] == 0:
                                yield
                            ACT(T3[:], T4[:], AF.Exp, [T4_t], [T3_t])
                            if KNOBS["hg_a1_dense"] == 0:
                                yield
                            ACT(T1[:], T4[:], AF.Exp, [T4_t], [T1_t], scale=-1.0)
                            if KNOBS["hg_a1_dense"] == 0:
                                yield
                            TT("dve", qgT[b][:, ts], banks[qb][:], T3[:], ALU.mult, [bt[qb], T3_t], [qg_t[b][tt]])
                            CP("dve", dcs[b][:, tt * 8:(tt + 1) * 8], T3[:, 63:512:64], [T3_t], [dcs_t[b][tt]])
                            if KNOBS["hg_a1_dense"] == 0:
                                yield
                            TT(KNOBS["hg_mul"], kgT[b][:, ts], T2[:], T1[:], ALU.mult, [T2_t, T1_t], [kg_t[b][tt]])
                            yield

                    def stage_a2(hh):
                        b = hh % 2
                        w = whg_sb[b]
                        wt = whg_t[b]

                        def wcol(s_, kc):
                            return w[:, (s_ * 8 + kc) * 128:(s_ * 8 + kc + 1) * 128]

                        for i4 in range(4):
                            vb = abank()
                            for q4 in range(4):
                                i = i4 * 4 + q4
                                for kc in range(8):
                                    MM(banks[vb][:, q4 * 128:(q4 + 1) * 128], xnT[:, kc, i * 128:(i + 1) * 128], wcol(2, kc), kc == 0, kc == 7,
                                       [wt, xn_t[kc][i4]], [bt[vb]])
                                if KNOBS["hg_a2_dense"] == 0:
                                    yield
                            CP("dve", v_tok[b][:, i4 * 512:(i4 + 1) * 512], banks[vb][:], [bt[vb]], [v_t[b][i4]])
                            if KNOBS["hg_a2_dense"] == 0:
                                yield
                        for tt in range(4):
                            ts = tsl(tt)
                            gb = abank()
                            for kc in range(8):
                                MM(banks[gb][:], wcol(3, kc), xnT[:, kc, ts], kc == 0, kc == 7, [wt, xn_t[kc][tt]], [bt[gb]])
                            CP("dve", T7[:], banks[gb][:], [bt[gb]], [T7_t])
                            if KNOBS["hg_a2_dense"] == 0:
                                yield
                            ACT(T5[:], T7[:], AF.Exp, [T7_t], [T5_t], scale=-1.0)
                            if KNOBS["hg_a2_dense"] == 0:
                                yield
                            ACT(T5[:], T5[:], AF.Ln, [T5_t], [T5_t], bias=1.0)
                            yield
                            ACT(T5[:], T5[:], AF.Exp, [T5_t], [T5_t], scale=-1.0)
                            if KNOBS["hg_a2_dense"] == 0:
                                yield
                            TT(KNOBS["hg_mul"], sog[b][:, ts], T5[:], T7[:], ALU.mult, [T5_t, T7_t], [sog_t[b][tt]])
                            yield

                    def stage_b(hh):
                        b = hh % 2
                        bankbf3 = banks[3].bitcast(BF16)
                        for i4 in range(4):
                            tt = i4
                            for q4 in range(4):
                                i = i4 * 4 + q4
                                TR(bankbf3[:, q4 * 128:(q4 + 1) * 128], kgT[b][:, i * 128:(i + 1) * 128], ident, [kg_t[b][tt], cst_t], [bt[3]])
                            CP("dve", kg_tok[:, i4 * 512:(i4 + 1) * 512], bankbf3[:, 0:512], [bt[3]], [kgk_t[i4]])
                            if KNOBS["hg_dense"] < 3:
                                yield
                            for q4 in range(4):
                                i = i4 * 4 + q4
                                tl = slice(i * 128, (i + 1) * 128)
                                MM(banks[4][:, q4 * 128:(q4 + 1) * 128], kgT[b][:, tl], qgT[b][:, tl], True, True, [kg_t[b][tt], qg_t[b][tt]], [bt[4]])
                            TT("dve", scm[:, i4 * 512:(i4 + 1) * 512], banks[4][:], mask4, ALU.mult, [bt[4], cst_t], [scm_t[i4]])
                            if KNOBS["hg_dense"] < 3:
                                yield
                            pbs = (5, 6)
                            for q4 in range(4):
                                i = i4 * 4 + q4
                                for half in range(2):
                                    hs = slice(half * 64, (half + 1) * 64)
                                    MM(banks[pbs[half]][:, q4 * 128:(q4 + 1) * 128], kg_tok[hs, i * 128:(i + 1) * 128], v_tok[b][hs, i * 128:(i + 1) * 128],
                                       True, True, [kgk_t[i4], v_t[b][i4]], [bt[pbs[half]]])
                            if KNOBS["hg_dense"] < 3:
                                yield
                            for half in range(2):
                                if KNOBS["hg_pst"] == "act":
                                    ACT(Pst[half][:], banks[pbs[half]][:], AF.Copy, [bt[pbs[half]]], [Pst_t[half]])
                                else:
                                    CP("dve", Pst[half][:], banks[pbs[half]][:], [bt[pbs[half]]], [Pst_t[half]])
                            if KNOBS["hg_dense"] < 3:
                                yield
                            for q4 in range(4):
                                i = i4 * 4 + q4
                                for half in range(2):
                                    c = 2 * i + half
                                    if c >= 31:
                                        continue
                                    pc = Pst[half][:, q4 * 128:(q4 + 1) * 128]
                                    if c == 0:
                                        CP("dve", S32[0][:], pc, [Pst_t[half]], [S32_t[0]])
                                    else:
                                        STT(S32[c % 3][:], S32[(c - 1) % 3][:], dcs[b][:, c - 1:c], pc, ALU.mult, ALU.add,
                                            [S32_t[(c - 1) % 3], Pst_t[half], dcs_t[b][(c - 1) // 8]], [S32_t[c % 3]])
                                    if KNOBS["hg_dense"] < 2:
                                        yield
                                    if KNOBS["hg_snap"] == "pool":
                                        TS("pool", S_bf[:, c, :], S32[c % 3][:], dcs[b][:, c:c + 1], 1.0, ALU.mult, ALU.mult,
                                           [S32_t[c % 3], dcs_t[b][c // 8]], [Sbf_t[c]])
                                    elif KNOBS["hg_snap"] == "act":
                                        ACT(S_bf[:, c, :], S32[c % 3][:], AF.Copy, [S32_t[c % 3], dcs_t[b][c // 8]], [Sbf_t[c]], scale=dcs[b][:, c:c + 1])
                                    else:
                                        TS("dve", S_bf[:, c, :], S32[c % 3][:], dcs[b][:, c:c + 1], None, ALU.mult, ALU.bypass,
                                           [S32_t[c % 3], dcs_t[b][c // 8]], [Sbf_t[c]])
                                    if KNOBS["hg_dense"] < 1:
                                        yield
                            if KNOBS["hg_dense"] >= 1:
                                yield
                        for tt in range(4):
                            ts = tsl(tt)
                            ob = 7
                            for q4 in range(4):
                                i = tt * 4 + q4
                                oreg = banks[ob][:, q4 * 128:(q4 + 1) * 128]
                                c0, c1 = 2 * i, 2 * i + 1
                                MM(oreg, v_tok[b][:, i * 128:(i + 1) * 128], scm[:, i * 128:(i + 1) * 128], True, False, [v_t[b][tt], scm_t[tt]], [bt[ob]])
                                if c0 >= 1:
                                    MM(banks[ob][:, q4 * 128:q4 * 128 + 64], S_bf[:, c0 - 1, :], qgT[b][:, c0 * 64:(c0 + 1) * 64], False, False,
                                       [Sbf_t[c0 - 1], qg_t[b][tt]], [bt[ob]])
                                MM(banks[ob][:, q4 * 128 + 64:(q4 + 1) * 128], S_bf[:, c1 - 1, :], qgT[b][:, c1 * 64:(c1 + 1) * 64], False, True,
                                   [Sbf_t[c1 - 1], qg_t[b][tt]], [bt[ob]])
                            ACT(osq[:], banks[ob][:], AF.Square, [bt[ob]], [osq_t])
                            yield
                            MM(banks[3][:], ones_bf[:], osq[:], True, True, [osq_t, ones_t], [bt[3]])
                            ACT(lnv[:], banks[3][:], AF.Ln, [bt[3]], [lnv_t], scale=1.0 / 128, bias=EPS)
                            yield
                            ACT(rstd[:], lnv[:], AF.Exp, [lnv_t], [rstd_t], scale=-0.5)
                            yield
                            TT("dve", t1[:], banks[ob][:], rstd[:], ALU.mult, [bt[ob], rstd_t], [t1_t])
                            yield
                            STT(y_hgT[:, hh, ts], t1[:], vecs[:, 32 + hh:33 + hh], sog[b][:, ts], ALU.mult, ALU.mult,
                                [t1_t, sog_t[b][tt], vecs_t], [yhg_t[hh][tt]])
                            yield

                    def run_rr(gens, weights=None):
                        gens = list(gens)
                        if KNOBS["hg_mode"] == "seq":
                            for g_ in gens[1:] + gens[:1]:
                                for _ in g_:
                                    pass
                            return
                        weights = list(weights) if weights else [1.0] * len(gens)
                        credit = [0.0] * len(gens)
                        alive = [True] * len(gens)
                        while any(alive):
                            for i_, g_ in enumerate(gens):
                                if not alive[i_]:
                                    continue
                                credit[i_] += weights[i_]
                                while credit[i_] >= 1.0 and alive[i_]:
                                    credit[i_] -= 1.0
                                    try:
                                        next(g_)
                                    except StopIteration:
                                        alive[i_] = False

                    load_whg(0)
                    load_whg(1)
                    norm(8, norm_to_xn)
                    run_rr([stage_a1(0), stage_a2(0)])
                    for hh in range(4):
                        gens = [stage_b(hh)]
                        wts = [KNOBS["hg_bw"]]
                        if hh + 1 < 4:
                            if KNOBS["hg_order"] == 0:
                                gens += [stage_a1(hh + 1), stage_a2(hh + 1)]
                            elif KNOBS["hg_order"] == 1:
                                gens = [stage_a1(hh + 1), stage_a2(hh + 1)] + gens
                            else:
                                gens = [stage_a1(hh + 1)] + gens + [stage_a2(hh + 1)]
                            wts = [1.0] * len(gens)
                        if hh + 2 < 4:
                            load_whg(hh + 2)
                        run_rr(gens, wts)
                    P.barrier()
                    if KNOBS["flush"]:
                        P.flush()
                if stage >= 3:
                    merge(wmA_d, y_hgT, yhg_t, "A")

        def attention():
            with ExitStack() as ph:
                y_attT = sbuf(ph, "y_attT", [128, 4, S], BF16)
                yatt_t = toks(4, 4)
                with ExitStack() as wk:
                    watt_sb = [sbuf(wk, "watt_sb%d" % i, [128, 3072], BF16) for i in range(2)]
                    bias_sb = [sbuf(wk, "bias_sb%d" % i, [128, 2, 256], F32) for i in range(2)]
                    qR = [sbuf(wk, "qR%d" % i, [128, S], BF16) for i in range(2)]
                    kR = [[sbuf(wk, "kR%d_%d" % (i, a), [128, S], BF16) for a in range(2)] for i in range(2)]
                    vT = sbuf(wk, "vT", [128, S], BF16)
                    vaug = [sbuf(wk, "vaug%d" % i, [128, 16, 2, 65], BF16) for i in range(2)]
                    acc = sbuf(wk, "acc", [65, 2, S], F32)
                    tmp = [sbuf(wk, "tmp%d" % i, [128, 512], F32) for i in range(2)]
                    pT = [sbuf(wk, "pT%d" % i, [128, 3968], BF16) for i in range(2)]
                    watt_t = toks(2)
                    bias_t = toks(2)
                    qT_t = toks(2, 4)
                    kT_t = toks(2, 4)
                    vT_t = toks(4)
                    kz_t, = toks(1)
                    vaug_t = toks(2, 16)
                    acc_t = toks(2)
                    tmp_t = toks(2)
                    pT_t = toks(2, 8)
                    vone_t = toks(2)
                    ds_watt = [P.dsem() for _ in range(2)]
                    ds_bias = [P.dsem() for _ in range(2)]
                    cnt = {"s": 0, "p": 0, "x": 0, "j": 0}
                    PROJ_BANKS = (0, 1, 4)

                    def pbank():
                        bk = PROJ_BANKS[cnt["j"] % 3]
                        cnt["j"] += 1
                        return bk

                    def load_watt(k):
                        hp, g = k // 3, k % 3
                        b = k % 2
                        DMAG("pool", [(watt_sb[b][:, 0:2048], watt_d[hp, g, :, 0:2048], (), [watt_t[b]]),
                                      (watt_sb[b][:, 2048:3072], watt_d[hp, g, :, 2048:3072], (), [watt_t[b]])], ds_watt[b])

                    def load_bias(k):
                        hp, g = k // 3, k % 3
                        DMA("sp", bias_sb[k % 2][:], biasT_d[hp, g], ds_bias[k % 2], writes=[bias_t[k % 2]])

                    def geom(g):
                        window, d = ATT_GROUPS[g]
                        nb = S // (128 * d)
                        units = [(r, nk) for r in range(d) for nk in range(nb)]
                        nq = [256 if nk + 1 < nb else 128 for (r, nk) in units]
                        col0 = [0] * 16
                        for u in range(1, 16):
                            col0[u] = col0[u - 1] + nq[u - 1]
                        groups = []
                        cur, w_ = [], 0
                        for u in range(16):
                            if w_ + nq[u] > 512:
                                groups.append(cur)
                                cur, w_ = [], 0
                            cur.append(u)
                            w_ += nq[u]
                        groups.append(cur)
                        return d, nb, units, nq, col0, groups

                    def res_out(t, d, tt):
                        L4 = 512 // d
                        return t.rearrange("p (r l) -> p r l", r=d)[:, :, tt * L4:(tt + 1) * L4]

                    def res_in(bank_ap, d):
                        return bank_ap.rearrange("p (l r) -> p r l", r=d)

                    def proj_gen(k):
                        hp, g = k // 3, k % 3
                        b = k % 2
                        if k + 1 < 12:
                            load_watt(k + 1)
                        w = watt_sb[b]
                        wt = watt_t[b]
                        d, nb, units, nq, col0, groups = geom(g)

                        def wcol(s_, kc):
                            return w[:, (s_ * 8 + kc) * 128:(s_ * 8 + kc + 1) * 128]

                        for tt in range(4):
                            ts = tsl(tt)
                            bk = pbank()
                            for kc in range(8):
                                MM(banks[bk][:], wcol(0, kc), xnT[:, kc, ts], kc == 0, kc == 7, [wt, xn_t[kc][tt]], [bt[bk]])
                            ACT(res_out(qR[b][:, :], d, tt), res_in(banks[bk][:, :], d), AF.Copy, [bt[bk]], [qT_t[b][tt]], scale=0.125)
                            yield
                            bk = pbank()
                            for kc in range(8):
                                MM(banks[bk][:], wcol(1, kc), xnT[:, kc, ts], kc == 0, kc == 7, [wt, xn_t[kc][tt]], [bt[bk]])
                            ACT(res_out(kR[b][0][0:64, :], d, tt), res_in(banks[bk][0:64, :], d), AF.Copy, [bt[bk], kz_t], [kT_t[b][tt]])
                            ACT(res_out(kR[b][1][64:128, :], d, tt), res_in(banks[bk][64:128, :], d), AF.Copy, [bt[bk], kz_t], [kT_t[b][tt]])
                            yield
                        for tt in range(4):
                            ts = tsl(tt)
                            bk = pbank()
                            for kc in range(8):
                                MM(banks[bk][:], wcol(2, kc), xnT[:, kc, ts], kc == 0, kc == 7, [wt, xn_t[kc][tt]], [bt[bk]])
                            ACT(res_out(vT[:, :], d, tt), res_in(banks[bk][:, :], d), AF.Copy, [bt[bk]], [vT_t[tt]])
                            yield
                        for u4 in range(4):
                            bk = pbank()
                            bkbf = banks[bk].bitcast(BF16)
                            for q4 in range(4):
                                u = u4 * 4 + q4
                                TR(bkbf[:, q4 * 128:(q4 + 1) * 128], vT[:, u * 128:(u + 1) * 128], ident, vT_t + [cst_t], [bt[bk]])
                            P.op("act", lambda h, bkbf=bkbf, u4=u4, b=b: h.activation(
                                out=vaug[b][:, u4 * 4:(u4 + 1) * 4, :, 0:64],
                                in_=bkbf[:, 0:512].rearrange("p (u a e) -> p u a e", u=4, a=2),
                                func=AF.Copy), [bt[bk]], vaug_t[b][u4 * 4:(u4 + 1) * 4])
                            yield

                    def units_gen(k):
                        hp, g = k // 3, k % 3
                        b = k % 2
                        bsb, bst = bias_sb[b], bias_t[b]
                        d, nb, units, nq, col0, groups = geom(g)
                        gof = {}
                        for m, grp in enumerate(groups):
                            for u in grp:
                                gof[u] = m
                        if g == 0:
                            MS("dve", acc[:], 0.0, acc_t)
                        if k + 1 < 12:
                            load_bias(k + 1)

                        def bias_ap(a, width):
                            base = bsb[:, a, 0:1]
                            pstep = base.ap[0][0]
                            if d == 16:
                                return bass.AP(bsb, base.offset, [[pstep, 128], [0, width // 128], [1, 128]])
                            assert width in (512, 384)
                            if width == 512:
                                return bass.AP(bsb, base.offset, [[pstep, 128], [0, 2], [1, 256]])
                            return None

                        def score_group(a, m):
                            grp = groups[m]
                            sbk = 5 + cnt["s"] % 3
                            x2 = cnt["x"] % 2
                            cnt["s"] += 1
                            cnt["x"] += 1
                            off = 0
                            for u in grp:
                                r, nk = units[u]
                                t0 = nk * 128 * d + r
                                c_ = u * 128
                                ktts = sorted(set([t0 // 512, (t0 + 127 * d) // 512]))
                                qtts = sorted(set(range(t0 // 512, (t0 + (nq[u] - 1) * d) // 512 + 1)))
                                MM(banks[sbk][:, off:off + nq[u]], kR[b][a][:, c_:c_ + 128], qR[b][:, c_:c_ + nq[u]], True, True,
                                   [kT_t[b][t] for t in ktts] + [qT_t[b][t] for t in qtts] + [kz_t], [bt[sbk]])
                                off += nq[u]
                            c0 = col0[grp[0]]
                            bap = bias_ap(a, off)
                            if bap is not None:
                                if d == 16:
                                    o_ap = tmp[x2][:, 0:off].rearrange("p (u i) -> p u i", i=128)
                                    i_ap = banks[sbk][:, 0:off].rearrange("p (u i) -> p u i", i=128)
                                else:
                                    o_ap = tmp[x2][:, 0:off].rearrange("p (u i) -> p u i", i=256)
                                    i_ap = banks[sbk][:, 0:off].rearrange("p (u i) -> p u i", i=256)
                                TT("dve", o_ap, i_ap, bap, ALU.add, [bt[sbk], bst], [tmp_t[x2]])
                            else:
                                TT("dve", tmp[x2][:, 0:256], banks[sbk][:, 0:256], bsb[:, a, 0:256], ALU.add, [bt[sbk], bst], [tmp_t[x2]])
                                TT("dve", tmp[x2][:, 256:384], banks[sbk][:, 256:384], bsb[:, a, 0:128], ALU.add, [bt[sbk], bst], [tmp_t[x2]])
                            ACT(pT[a][:, c0:c0 + off], tmp[x2][:, 0:off], AF.Exp, [tmp_t[x2]], [pT_t[a][m]])

                        def pv_group(a, j):
                            pvb = 2 + cnt["p"] % 2
                            cnt["p"] += 1
                            for q4 in range(4):
                                u = 4 * j + q4
                                r, n = units[u]
                                reg = banks[pvb][0:65, q4 * 128:(q4 + 1) * 128]
                                has_prev = n > 0
                                MM(reg, vaug[b][:, u, a, :], pT[a][:, col0[u]:col0[u] + 128], True, not has_prev,
                                   [vaug_t[b][u], vone_t[b], pT_t[a][gof[u]]], [bt[pvb]])
                                if has_prev:
                                    MM(reg, vaug[b][:, u - 1, a, :], pT[a][:, col0[u - 1] + 128:col0[u - 1] + 256], False, True,
                                       [vaug_t[b][u - 1], vone_t[b], pT_t[a][gof[u - 1]]], [bt[pvb]])
                            if d == 1:
                                dst = acc[:, a, j * 512:(j + 1) * 512]
                                src = banks[pvb][0:65, :]
                            elif d == 4:
                                dst = acc[:, a, j:j + 4 * 511 + 1:4]
                                src = banks[pvb][0:65, :]
                            else:
                                dst = acc[:, a, :].rearrange("p (i r) -> p r i", r=16)[:, 4 * j:4 * j + 4, :]
                                src = banks[pvb][0:65, :].rearrange("p (r i) -> p r i", r=4)
                            TT("dve", dst, dst, src, ALU.add, [bt[pvb], acc_t[a]], [acc_t[a]])

                        sched = []
                        emitted = -1
                        for j in range(4):
                            need = min(gof[4 * j + 3] + 1, len(groups) - 1)
                            for m in range(emitted + 1, need + 1):
                                sched.append(("S", m))
                            emitted = max(emitted, need)
                            sched.append(("P", j))
                        for kind, idx in sched:
                            for a in range(2):
                                if kind == "S":
                                    score_group(a, idx)
                                else:
                                    pv_group(a, idx)
                                yield
                        if g == 2:
                            for a in range(2):
                                for tt in range(4):
                                    ts = tsl(tt)
                                    nbk = 2 + (a * 4 + tt) % 2
                                    MM(banks[nbk][0:64, :], ones_f[64:65, 0:64], acc[64:65, a, ts], True, True, [acc_t[a], ones_t], [bt[nbk]])
                                    ACT(lnv[0:64, :], banks[nbk][0:64, :], AF.Ln, [bt[nbk]], [lnv_t])
                                    ACT(rstd[0:64, :], lnv[0:64, :], AF.Exp, [lnv_t], [rstd_t], scale=-1.0)
                                    TT("dve", y_attT[a * 64:(a + 1) * 64, hp, ts], acc[0:64, a, ts], rstd[0:64, :], ALU.mult, [acc_t[a], rstd_t], [yatt_t[hp][tt]])
                                    yield

                    for b in range(2):
                        MS("dve", vaug[b][:, :, :, 64:65], 1.0, vaug_t[b] + [vone_t[b]])
                        MS("dve", kR[b][0][64:128, :], 0.0, [kz_t])
                        MS("dve", kR[b][1][0:64, :], 0.0, [kz_t])
                    load_watt(0)
                    load_bias(0)
                    for _ in proj_gen(0):
                        pass
                    for k in range(12):
                        gu = units_gen(k)
                        gp = proj_gen(k + 1) if k + 1 < 12 else iter(())
                        alive_u, alive_p = True, True
                        cu, cp_ = 0.0, 0.0
                        while alive_u or alive_p:
                            cu += KNOBS["at_uw"]
                            while alive_u and cu >= 1.0:
                                cu -= 1.0
                                try:
                                    next(gu)
                                except StopIteration:
                                    alive_u = False
                            cp_ += KNOBS["at_pw"]
                            while alive_p and cp_ >= 1.0:
                                cp_ -= 1.0
                                try:
                                    next(gp)
                                except StopIteration:
                                    alive_p = False
                    P.barrier()
                    if KNOBS["flush"]:
                        P.flush()
                if stage >= 0:
                    merge(wmB_d, y_attT, yatt_t, "B")

        if stage >= 0:
            ffn(0, 0)
        if stage >= 2 or stage == -1:
            hgrn2()
        if stage == -2:
            norm(8, norm_to_xn)
        if stage >= 4 or stage == -2:
            attention()
        if stage >= 5:
            ffn(1, 16)
        if stage >= 6:
            norm(24, norm_inplace)
        ds_outs = [ds_out] + [P.dsem() for _ in range(3)]
        ev_out = [DMAG("sp", [(outT_d[c * 128:(c + 1) * 128, tsl(tt)], hT[:, c, tsl(tt)], [hT_t[c][tt]], ()) for c in range(8)], ds_outs[tt])
                  for tt in range(4)]
        P.wait_all("sp", ev_out)
        P.flush()
    return nc


def _const_tables():
    cst = np.zeros((128, 640), np.float32)
    cst[:, 0:128] = np.eye(128, dtype=np.float32)
    s = np.arange(128)[:, None]
    t = np.arange(128)[None, :]
    for q in range(4):
        cst[:, 128 + q * 128:256 + q * 128] = ((s <= t) & (s // 64 == t // 64)).astype(np.float32)
    n_heads = 24
    slopes = np.exp2(-8.0 * np.arange(1, n_heads + 1, dtype=np.float32) / n_heads).astype(np.float32)
    biasT = np.zeros((4, 3, 128, 2, 256), np.float32)
    j = np.arange(128)[:, None].astype(np.float32)
    i = np.arange(128)[None, :].astype(np.float32)
    for hp in range(4):
        for g, (window, d) in enumerate(ATT_GROUPS):
            for a in range(2):
                sl = slopes[g * 8 + hp * 2 + a]
                biasT[hp, g, :, a, 0:128] = np.where(i >= j, -sl * (d * (i - j)), NEG)
                biasT[hp, g, :, a, 128:256] = np.where(i <= j, -sl * (d * (i + 128.0 - j)), NEG)
    return cst, biasT.reshape(4, 3, 128, 512)


def _prep_weights(inp):
    f = lambda a: np.ascontiguousarray(a, dtype=np.float32)
    out = {}
    for tag, kgu, kd in (("1", "ffn1_w_gate_up", "ffn1_w_down"), ("2", "ffn2_w_gate_up", "ffn2_w_down")):
        W = np.asarray(inp[kgu])[0]
        W5 = W.reshape(8, 128, 2, NJ, 128)
        out["wgu" + tag] = f(W5.transpose(3, 1, 2, 0, 4).reshape(NJ, 128, 2048))
        out["wd" + tag] = f(np.asarray(inp[kd])[0].reshape(NJ, 128, 1024))
    Win = np.asarray(inp["w_in"])[0]
    Whg = Win[:, 0:2048].reshape(8, 128, 4, 4, 128)
    out["whg"] = f(Whg.transpose(3, 1, 2, 0, 4).reshape(4, 128, 4096))
    Watt = Win[:, 2048:6656].reshape(8, 128, 3, 3, 4, 128)
    out["watt"] = f(Watt.transpose(4, 2, 1, 3, 0, 5).reshape(4, 3, 128, 3072))
    for tag, c0, kb in (("A", 6656, "w_branch_hg"), ("B", 7680, "w_branch_att")):
        Wg = Win[:, c0:c0 + 1024].reshape(8, 128, 8, 128)
        Wb = np.asarray(inp[kb])[0].reshape(4, 128, 8, 128)
        wm = np.concatenate([Wg.transpose(2, 1, 0, 3).reshape(8, 128, 1024), Wb.transpose(2, 1, 0, 3).reshape(8, 128, 512)], axis=2)
        out["wm" + tag] = f(wm)
    Wo = np.asarray(inp["w_out"])[0].reshape(2, 4, 128, 1024)
    out["wo"] = f(Wo.transpose(0, 2, 1, 3).reshape(2, 128, 4096))
    vecs = np.zeros((128, 48), np.float32)
    vecs[:, 0:8] = np.asarray(inp["ffn1_norm"])[0].reshape(8, 128).T
    vecs[:, 8:16] = np.asarray(inp["mix_norm"])[0].reshape(8, 128).T
    vecs[:, 16:24] = np.asarray(inp["ffn2_norm"])[0].reshape(8, 128).T
    vecs[:, 24:32] = np.asarray(inp["final_norm"]).reshape(8, 128).T
    vecs[:, 32:36] = np.asarray(inp["hg_out_norm"])[0].reshape(4, 128).T
    lbs = np.asarray(inp["hg_lower_bounds"])
    vecs[:, 36:40] = lbs[0].reshape(4, 128).T
    vecs[:, 40:44] = lbs[1].reshape(4, 128).T
    out["vecs"] = vecs
    cst, biasT = _const_tables()
    out["cst"] = cst
    out["biasT"] = biasT
    return out


_NC_CACHE = {}


def _get_nc(stage=99):
    if stage not in _NC_CACHE:
        _NC_CACHE[stage] = build_program(stage)
    return _NC_CACHE[stage]


def kernel(**inputs):
    x = np.asarray(inputs["x"], dtype=np.float32)
    shared = _prep_weights(inputs)
    nc = _get_nc()
    in_maps = []
    for b in range(NCORES):
        m = dict(shared)
        m["xT"] = np.ascontiguousarray(x[b].T)
        in_maps.append(m)
    res = run_bass_kernel_spmd(nc, in_maps, core_ids=list(range(NCORES)))
    out = np.stack([np.ascontiguousarray(r["outT"].T) for r in res.results], axis=0)
    return out.astype(np.float32)
```

```python
import bisect
from contextlib import ExitStack

import numpy as np
import concourse.bass as bass
import concourse.mybir as mybir
from concourse.bass_utils import run_bass_kernel_spmd

F32 = mybir.dt.float32
BF16 = mybir.dt.bfloat16
AF = mybir.ActivationFunctionType
ALU = mybir.AluOpType

S = 2048
D = 1024
DFF = 2816
NJ = DFF // 128
EPS = 1e-6
NCORES = 8
ATT_GROUPS = ((128, 1), (512, 4), (2048, 16))
NEG = -30000.0
FFN_GROUPS = ((0, 1, 2, 3, 4, 5, 6, 7), (8, 9, 10, 11, 12, 13, 14), (15, 16, 17, 18, 19, 20, 21))
GMAX = 8
KNOBS = {"hg_mode": "rr", "hg_bw": 1.0, "hg_snap": "pool", "hg_dense": 2, "hg_a1_dense": 0, "hg_a2_dense": 0, "hg_pst": "act", "hg_order": 0, "flush": 0, "hg_mul": "pool", "at_uw": 2.0, "at_pw": 1.0}


class Tok:
    __slots__ = ("w", "r")

    def __init__(self):
        self.w = None
        self.r = {}


def toks(*shape):
    if len(shape) == 1:
        return [Tok() for _ in range(shape[0])]
    return [toks(*shape[1:]) for _ in range(shape[0])]


class Group:
    def __init__(self, kids):
        self.kids = kids


def _expand(ts):
    out = []
    for t in ts:
        if isinstance(t, Group):
            out.extend(t.kids)
        else:
            out.append(t)
    return out


class DSem:
    def __init__(self, h):
        self.h = h
        self.count = 0


class Prog:
    ENG = ("pe", "act", "dve", "pool", "sp")

    def __init__(self, nc, stack):
        self.nc = nc
        self.stack = stack
        self.ops = {e: [] for e in self.ENG}
        self.needed = {e: set() for e in self.ENG}
        self.need_sorted = {e: [] for e in self.ENG}
        self.evval = {e: {} for e in self.ENG}
        self.ecount = {e: 0 for e in self.ENG}
        self.flushed = {e: 0 for e in self.ENG}
        self.waited = {e: {} for e in self.ENG}
        self.esem = {e: stack.enter_context(nc.semaphore("es_" + e)) for e in self.ENG}
        self.nds = 0

    def dsem(self):
        self.nds += 1
        return DSem(self.stack.enter_context(self.nc.semaphore("ds%d" % self.nds)))

    def _waits(self, eng, reads, writes):
        ws = []
        for t in reads:
            if t.w is not None:
                ws.append(t.w)
        for t in writes:
            if t.w is not None:
                ws.append(t.w)
            for k, ev in t.r.items():
                if k == eng and eng == "pe":
                    continue
                ws.append(ev)
        out = []
        for ev in ws:
            if ev[0] == "E":
                if ev[1] == eng and eng == "pe":
                    continue
                if ev[2] >= self.flushed[ev[1]]:
                    self.needed[ev[1]].add(ev[2])
            out.append(ev)
        return out

    def _mark(self, ev, key, reads, writes):
        for t in reads:
            t.r[key] = ev
        for t in writes:
            t.w = ev
            t.r = {}

    def op(self, eng, fn, reads=(), writes=()):
        reads, writes = _expand(reads), _expand(writes)
        waits = self._waits(eng, reads, writes)
        ev = ("E", eng, len(self.ops[eng]))
        self.ops[eng].append(dict(waits=waits, fn=fn, kind="c"))
        self._mark(ev, eng, reads, writes)
        return ev

    def dma(self, q, fn, ds, reads=(), writes=()):
        reads, writes = _expand(reads), _expand(writes)
        waits = self._waits("dma", reads, writes)
        ds.count += 16
        ev = ("D", ds, ds.count)
        self.ops[q].append(dict(waits=waits, fn=fn, kind="d", ds=ds))
        self._mark(ev, ("D", id(ds)), reads, writes)
        return ev

    def dma_group(self, q, items, ds):
        final = ("D", ds, ds.count + 16 * len(items))
        prepared = []
        for fn, reads, writes in items:
            reads, writes = _expand(reads), _expand(writes)
            prepared.append((fn, reads, writes, self._waits("dma", reads, writes)))
        for fn, reads, writes, waits in prepared:
            ds.count += 16
            self.ops[q].append(dict(waits=waits, fn=fn, kind="d", ds=ds))
            self._mark(final, ("D", id(ds)), reads, writes)
        return final

    def wait_all(self, eng, evs):
        for ev in evs:
            if ev[0] == "E" and ev[2] >= self.flushed[ev[1]]:
                self.needed[ev[1]].add(ev[2])
        self.ops[eng].append(dict(waits=list(evs), fn=None, kind="w"))

    def barrier(self):
        last = {}
        for e in self.ENG:
            for i in range(len(self.ops[e]) - 1, -1, -1):
                if self.ops[e][i]["kind"] == "c":
                    last[e] = ("E", e, i)
                    break
        for e in self.ENG:
            self.wait_all(e, [ev for k, ev in last.items() if k != e])

    def _resolve(self, ev):
        e, idx = ev[1], ev[2]
        ns = self.need_sorted[e]
        k = bisect.bisect_left(ns, idx)
        return self.evval[e][ns[k]]

    def flush(self):
        for e in self.ENG:
            n = len(self.ops[e])
            for i in range(n - 1, self.flushed[e] - 1, -1):
                if self.ops[e][i]["kind"] == "c":
                    self.needed[e].add(i)
                    break
            c = self.ecount[e]
            for i in range(self.flushed[e], n):
                if i in self.needed[e]:
                    c += 1
                    self.evval[e][i] = c
                    self.need_sorted[e].append(i)
            self.ecount[e] = c
        prog = self

        def run(e, h):
            waited = prog.waited[e]
            for i in range(prog.flushed[e], len(prog.ops[e])):
                o = prog.ops[e][i]
                for ev in o["waits"]:
                    if ev[0] == "E":
                        sem = prog.esem[ev[1]]
                        val = prog._resolve(ev)
                        key = ("E", ev[1])
                    else:
                        sem = ev[1].h
                        val = ev[2]
                        key = ("D", id(ev[1]))
                    if waited.get(key, 0) >= val:
                        continue
                    waited[key] = val
                    h.wait_ge(sem, val)
                if o["fn"] is None:
                    continue
                inst = o["fn"](h)
                if o["kind"] == "d":
                    inst.then_inc(o["ds"].h, 16)
                elif i in prog.needed[e]:
                    inst.then_inc(prog.esem[e], 1)

        with self.nc.Block() as block:
            @block.tensor
            def _(h):
                run("pe", h)

            @block.scalar
            def _(h):
                run("act", h)

            @block.vector
            def _(h):
                run("dve", h)

            @block.gpsimd
            def _(h):
                run("pool", h)

            @block.sync
            def _(h):
                run("sp", h)

        for e in self.ENG:
            self.flushed[e] = len(self.ops[e])


def build_program(stage=99):
    nc = bass.Bass("TRN2", target_bir_lowering=False)

    def dram(name, shape, kind="ExternalInput"):
        return nc.dram_tensor(name, list(shape), F32, kind=kind).ap()

    xT_d = dram("xT", [D, S])
    wgu_d = [dram("wgu1", [NJ, 128, 2048]), dram("wgu2", [NJ, 128, 2048])]
    wd_d = [dram("wd1", [NJ, 128, 1024]), dram("wd2", [NJ, 128, 1024])]
    whg_d = dram("whg", [4, 128, 4096])
    watt_d = dram("watt", [4, 3, 128, 3072])
    wmA_d = dram("wmA", [8, 128, 1536])
    wmB_d = dram("wmB", [8, 128, 1536])
    wo_d = dram("wo", [2, 128, 4096])
    vecs_d = dram("vecs", [128, 48])
    biasT_d = dram("biasT", [4, 3, 128, 512])
    cst_d = dram("cst", [128, 640])
    outT_d = dram("outT", [D, S], kind="ExternalOutput")

    with ExitStack() as st:
        P = Prog(nc, st)

        def sbuf(stack, name, shape, dt):
            return stack.enter_context(nc.sbuf_tensor("s_" + name, list(shape), dt))

        def MM(out, lhsT, rhs, start, stop, reads, writes):
            return P.op("pe", lambda h: h.matmul(out, lhsT=lhsT, rhs=rhs, start=start, stop=stop), reads, writes)

        def TR(out, in_, ident, reads, writes):
            return P.op("pe", lambda h: h.transpose(out, in_, ident), reads, writes)

        def ACT(out, in_, func, reads, writes, **kw):
            return P.op("act", lambda h: h.activation(out=out, in_=in_, func=func, **kw), reads, writes)

        def TT(eng, out, in0, in1, op, reads, writes):
            return P.op(eng, lambda h: h.tensor_tensor(out=out, in0=in0, in1=in1, op=op), reads, writes)

        def STT(out, in0, scalar, in1, op0, op1, reads, writes):
            return P.op("dve", lambda h: h.scalar_tensor_tensor(out=out, in0=in0, scalar=scalar, in1=in1, op0=op0, op1=op1), reads, writes)

        def TS(eng, out, in0, s1, s2, op0, op1, reads, writes):
            return P.op(eng, lambda h: h.tensor_scalar(out=out, in0=in0, scalar1=s1, scalar2=s2, op0=op0, op1=op1), reads, writes)

        def CP(eng, out, in_, reads, writes):
            return P.op(eng, lambda h: h.tensor_copy(out=out, in_=in_), reads, writes)

        def MS(eng, ap, val, writes):
            return P.op(eng, lambda h: h.memset(ap, val), (), writes)

        def DMA(q, out, in_, ds, reads=(), writes=()):
            return P.dma(q, lambda h: h.dma_start(out=out, in_=in_), ds, reads, writes)

        def _dfn(out, in_):
            return lambda h: h.dma_start(out=out, in_=in_)

        def DMAG(q, items, ds):
            return P.dma_group(q, [(_dfn(o, i), r, w) for (o, i, r, w) in items], ds)

        hT = sbuf(st, "hT", [128, 8, S], F32)
        xnT = sbuf(st, "xnT", [128, 8, S], BF16)
        vecs = sbuf(st, "vecs", [128, 48], F32)
        cstb = sbuf(st, "cstb", [128, 640], BF16)
        ones_bf = sbuf(st, "ones_bf", [128, 128], BF16)
        ones_f = sbuf(st, "ones_f", [128, 64], F32)
        oml = sbuf(st, "oml", [128, 4], F32)
        lbd = sbuf(st, "lbd", [128, 4], F32)
        sq = [sbuf(st, "sq%d" % i, [128, 512], BF16) for i in range(2)]
        lnv = sbuf(st, "lnv", [128, 512], F32)
        rstd = sbuf(st, "rstd", [128, 512], F32)
        banks = [st.enter_context(nc.psum_tensor("bank%d" % i, [128, 512], F32)) for i in range(8)]

        hT_t = toks(8, 4)
        xn_t = toks(8, 4)
        vecs_t, cst_t, ones_t, rmask_t, oml_t, lbd_t, lnv_t, rstd_t = toks(8)
        sq_t = toks(2)
        bq = toks(8, 4)
        bt = [Group(bq[i]) for i in range(8)]
        ident = cstb[:, 0:128]
        mask4 = cstb[:, 128:640]

        def tsl(tt):
            return slice(tt * 512, (tt + 1) * 512)

        ds_x = P.dsem()
        ds_c = P.dsem()
        ds_v = P.dsem()
        ds_out = P.dsem()

        DMA("sp", vecs[:], vecs_d, ds_v, writes=[vecs_t])
        ds_xs = [ds_x] + [P.dsem() for _ in range(3)]
        for tt in range(4):
            dep = [hT_t[0][tt - 1]] if tt > 0 else []
            DMAG("sp", [(hT[:, c, tsl(tt)], xT_d[c * 128:(c + 1) * 128, tsl(tt)], dep, [hT_t[c][tt]]) for c in range(8)], ds_xs[tt])
        DMA("pool", cstb[:], cst_d, ds_c, writes=[cst_t])
        MS("dve", ones_bf[:], 1.0, [ones_t])
        MS("dve", ones_f[:], 1.0, [ones_t])
        TT("dve", lbd[:], vecs[:, 40:44], vecs[:, 36:40], ALU.subtract, [vecs_t], [lbd_t])
        ACT(oml[:], lbd[:], AF.Sigmoid, [lbd_t], [oml_t])

        def norm(gcol, out_fn, nfeat=D):
            for tt in range(4):
                norm_tile(tt, gcol, out_fn, nfeat)

        def norm_tile(tt, gcol, out_fn, nfeat=D):
            if True:
                ts = tsl(tt)
                sb_, rb_ = (7, 6) if tt % 2 == 0 else (4, 5)
                for c in range(8):
                    ACT(sq[c % 2][:], hT[:, c, ts], AF.Square, [hT_t[c][tt]], [sq_t[c % 2]])
                    MM(banks[sb_][:], ones_bf[:], sq[c % 2][:], c == 0, c == 7, [sq_t[c % 2], ones_t], [bt[sb_]])
                ACT(lnv[:], banks[sb_][:], AF.Ln, [bt[sb_]], [lnv_t], scale=1.0 / nfeat, bias=EPS)
                ACT(banks[rb_][:], lnv[:], AF.Exp, [lnv_t], [bt[rb_]], scale=-0.5)
                for c in range(8):
                    out_fn(c, tt, ts, gcol, rb_)

        def norm_to_xn(c, tt, ts, gcol, rb_):
            STT(xnT[:, c, ts], hT[:, c, ts], vecs[:, gcol + c:gcol + c + 1], banks[rb_][:], ALU.mult, ALU.mult,
                [hT_t[c][tt], bt[rb_], vecs_t], [xn_t[c][tt]])

        def norm_inplace(c, tt, ts, gcol, rb_):
            STT(hT[:, c, ts], hT[:, c, ts], vecs[:, gcol + c:gcol + c + 1], banks[rb_][:], ALU.mult, ALU.mult,
                [hT_t[c][tt], bt[rb_], vecs_t], [hT_t[c][tt]])

        def ffn(which, gcol):
            wgu, wd = wgu_d[which], wd_d[which]
            with ExitStack() as ph:
                actT = sbuf(ph, "actT%d" % which, [128, GMAX, S], BF16)
                wgu_sb = [sbuf(ph, "wgu_sb%d_%d" % (which, i), [128, 2048], BF16) for i in range(3)]
                wd_sb = sbuf(ph, "wd_sb%d" % which, [128, GMAX, 1024], BF16)
                sg = [sbuf(ph, "sg%d_%d" % (which, i), [128, 512], F32) for i in range(2)]
                act_t = toks(GMAX, 4)
                wgu_t = toks(3)
                wd_t = toks(GMAX)
                sg_t = toks(2)
                ds_wgu = [P.dsem() for _ in range(3)]
                ds_wd = P.dsem()

                def load_wgu(j):
                    dep = [hT_t[0][0]] if (which == 0 and j < 3) else []
                    DMA("pool", wgu_sb[j % 3][:], wgu[j], ds_wgu[j % 3], reads=dep, writes=[wgu_t[j % 3]])

                load_wgu(0)
                load_wgu(1)
                for grp in FFN_GROUPS:
                    for jj, j in enumerate(grp):
                        if j + 2 < NJ:
                            load_wgu(j + 2)
                        if jj == 1:
                            DMAG("pool", [(wd_sb[:, jj2, :], wd[j2], (), [wd_t[jj2]]) for jj2, j2 in enumerate(grp)], ds_wd)
                        w = wgu_sb[j % 3]
                        for tt in range(4):
                            if j == 0:
                                norm_tile(tt, gcol, norm_to_xn)
                            ts = tsl(tt)
                            gb, ub = tt % 2, 2 + tt % 2
                            for kc in range(8):
                                MM(banks[gb][:], w[:, kc * 128:(kc + 1) * 128], xnT[:, kc, ts], kc == 0, kc == 7,
                                   [wgu_t[j % 3], xn_t[kc][tt]], [bt[gb]])
                            for kc in range(8):
                                MM(banks[ub][:], w[:, (8 + kc) * 128:(9 + kc) * 128], xnT[:, kc, ts], kc == 0, kc == 7,
                                   [wgu_t[j % 3], xn_t[kc][tt]], [bt[ub]])
                            ACT(sg[tt % 2][:], banks[gb][:], AF.Silu, [bt[gb]], [sg_t[tt % 2]])
                            TT("dve", actT[:, jj, ts], sg[tt % 2][:], banks[ub][:], ALU.mult, [sg_t[tt % 2], bt[ub]], [act_t[jj][tt]])
                    n = len(grp)
                    for m in range(8):
                        for tt in range(4):
                            ts = tsl(tt)
                            db = 4 + (m * 4 + tt) % 2
                            for jj in range(n):
                                MM(banks[db][:], wd_sb[:, jj, m * 128:(m + 1) * 128], actT[:, jj, ts], jj == 0, jj == n - 1,
                                   [wd_t[jj], act_t[jj][tt]], [bt[db]])
                            STT(hT[:, m, ts], banks[db][:], 0.5, hT[:, m, ts], ALU.mult, ALU.add,
                                [bt[db], hT_t[m][tt]], [hT_t[m][tt]])
                P.barrier()
                if KNOBS["flush"]:
                    P.flush()

        def merge(wm_d, yT, y_t, tag):
            with ExitStack() as ph:
                wm_sb = [sbuf(ph, "wm_sb%s%d" % (tag, i), [128, 1536], BF16) for i in range(2)]
                wo_sb = sbuf(ph, "wo_sb" + tag, [128, 4096], BF16)
                sA = [sbuf(ph, "sA%s%d" % (tag, i), [128, 512], F32) for i in range(2)]
                mT = sbuf(ph, "mT" + tag, [128, 4, S], BF16)
                wm_t = toks(2)
                wo_t, = toks(1)
                sA_t = toks(2)
                mT_t = toks(4, 4)
                ds_wm = [P.dsem() for _ in range(2)]
                ds_wo = P.dsem()

                def load_wm(dm):
                    DMA("pool", wm_sb[dm % 2][:], wm_d[dm], ds_wm[dm % 2], writes=[wm_t[dm % 2]])

                load_wm(0)
                for dmg in range(2):
                    for dmi in range(4):
                        dm = dmg * 4 + dmi
                        if dm + 1 < 8:
                            load_wm(dm + 1)
                        if dmi == 1:
                            DMAG("pool", [(wo_sb[:, 0:2048], wo_d[dmg, :, 0:2048], (), [wo_t]),
                                          (wo_sb[:, 2048:4096], wo_d[dmg, :, 2048:4096], (), [wo_t])], ds_wo)
                        w = wm_sb[dm % 2]
                        for tt in range(4):
                            ts = tsl(tt)
                            gb, bb = tt % 2, 2 + tt % 2
                            for kc in range(8):
                                MM(banks[gb][:], w[:, kc * 128:(kc + 1) * 128], xnT[:, kc, ts], kc == 0, kc == 7,
                                   [wm_t[dm % 2], xn_t[kc][tt]], [bt[gb]])
                            for c in range(4):
                                MM(banks[bb][:], w[:, (8 + c) * 128:(9 + c) * 128], yT[:, c, ts], c == 0, c == 3,
                                   [wm_t[dm % 2], y_t[c][tt]], [bt[bb]])
                            ACT(sA[tt % 2][:], banks[gb][:], AF.Sigmoid, [bt[gb]], [sA_t[tt % 2]])
                            TT("dve", mT[:, dmi, ts], sA[tt % 2][:], banks[bb][:], ALU.mult, [sA_t[tt % 2], bt[bb]], [mT_t[dmi][tt]])
                    for m in range(8):
                        for tt in range(4):
                            ts = tsl(tt)
                            db = 4 + (m * 4 + tt) % 2
                            for dmi in range(4):
                                MM(banks[db][:], wo_sb[:, dmi * 1024 + m * 128: dmi * 1024 + (m + 1) * 128], mT[:, dmi, ts],
                                   dmi == 0, dmi == 3, [wo_t, mT_t[dmi][tt]], [bt[db]])
                            TT("dve", hT[:, m, ts], banks[db][:], hT[:, m, ts], ALU.add, [bt[db], hT_t[m][tt]], [hT_t[m][tt]])
                P.barrier()
                if KNOBS["flush"]:
                    P.flush()

        def hgrn2():
            with ExitStack() as ph:
                y_hgT = sbuf(ph, "y_hgT", [128, 4, S], BF16)
                yhg_t = toks(4, 4)
                with ExitStack() as wk:
                    whg_sb = [sbuf(wk, "whg_sb%d" % i, [128, 4096], BF16) for i in range(2)]
                    rmask = sbuf(wk, "rmask", [128, 512], F32)
                    lnoml = sbuf(wk, "lnoml", [128, 4], F32)
                    T1 = sbuf(wk, "T1", [128, 512], F32)
                    T2 = sbuf(wk, "T2", [128, 512], F32)
                    T3 = sbuf(wk, "T3", [128, 512], F32)
                    T4 = sbuf(wk, "T4", [128, 512], F32)
                    T5 = sbuf(wk, "T5", [128, 512], F32)
                    T7 = sbuf(wk, "T7", [128, 512], F32)
                    dcs = [sbuf(wk, "dcs%d" % i, [128, 32], F32) for i in range(2)]
                    qgT = [sbuf(wk, "qgT%d" % i, [128, S], BF16) for i in range(2)]
                    kgT = [sbuf(wk, "kgT%d" % i, [128, S], BF16) for i in range(2)]
                    v_tok = [sbuf(wk, "v_tok%d" % i, [128, S], BF16) for i in range(2)]
                    sog = [sbuf(wk, "sog%d" % i, [128, S], BF16) for i in range(2)]
                    kg_tok = sbuf(wk, "kg_tok", [128, S], BF16)
                    scm = sbuf(wk, "scm", [128, S], BF16)
                    S_bf = sbuf(wk, "S_bf", [128, 32, 128], BF16)
                    S32 = [sbuf(wk, "S32_%d" % i, [128, 128], F32) for i in range(3)]
                    Pp = [sbuf(wk, "Pp%d" % i, [128, 128], F32) for i in range(3)]
                    Pst = [sbuf(wk, "Pst%d" % i, [128, 512], F32) for i in range(2)]
                    osq, t1 = sq[0], lnv
                    whg_t = toks(2)
                    T1_t, T2_t, T3_t, T4_t, T5_t, T7_t, lnoml_t = toks(7)
                    osq_t, t1_t = sq_t[0], lnv_t
                    Pst_t = toks(2)
                    dcs_t = toks(2, 4)
                    qg_t = toks(2, 4)
                    kg_t = toks(2, 4)
                    v_t = toks(2, 4)
                    sog_t = toks(2, 4)
                    kgk_t = toks(4)
                    scm_t = toks(4)
                    Sbf_t = toks(32)
                    S32_t = toks(3)
                    Pp_t = toks(3)
                    ds_whg = [P.dsem() for _ in range(2)]
                    cnt = {"a": 0}
                    A_BANKS = (0, 1, 2)

                    def abank():
                        bk = A_BANKS[cnt["a"] % 3]
                        cnt["a"] += 1
                        return bk

                    MS("dve", rmask[:], 1.0, [rmask_t])
                    MS("dve", rmask[:, 0:512:64], 0.0, [rmask_t])
                    ACT(lnoml[:], oml[:], AF.Ln, [oml_t], [lnoml_t])

                    def load_whg(hh):
                        b = hh % 2
                        DMAG("pool", [(whg_sb[b][:, 0:2048], whg_d[hh, :, 0:2048], (), [whg_t[b]]),
                                      (whg_sb[b][:, 2048:4096], whg_d[hh, :, 2048:4096], (), [whg_t[b]])], ds_whg[b])

                    def stage_a1(hh):
                        b = hh % 2
                        w = whg_sb[b]
                        wt = whg_t[b]

                        def wcol(s_, kc):
                            return w[:, (s_ * 8 + kc) * 128:(s_ * 8 + kc + 1) * 128]

                        for tt in range(4):
                            ts = tsl(tt)
                            fb = abank()
                            for kc in range(8):
                                MM(banks[fb][:], wcol(1, kc), xnT[:, kc, ts], kc == 0, kc == 7, [wt, xn_t[kc][tt]], [bt[fb]])
                            ACT(T1[:], banks[fb][:], AF.Exp, [bt[fb]], [T1_t])
                            if KNOBS["hg_a1_dense"] == 0:
                                yield
                            ACT(T1[:], T1[:], AF.Ln, [T1_t], [T1_t], bias=1.0)
                            if KNOBS["hg_a1_dense"] == 0:
                                yield
                            ACT(T2[:], T1[:], AF.Exp, [T1_t, lnoml_t], [T2_t], scale=-1.0, bias=lnoml[:, hh:hh + 1])
                            if KNOBS["hg_a1_dense"] == 0:
                                yield
                            ACT(T3[:], T2[:], AF.Ln, [T2_t], [T3_t], scale=-1.0, bias=1.0)
                            if KNOBS["hg_a1_dense"] == 0:
                                yield
                            P.op("dve", lambda h: h.tensor_tensor_scan(out=T4[:], data0=rmask[:], data1=T3[:], initial=0.0,
                                                                        op0=ALU.mult, op1=ALU.add),
                                 [rmask_t, T3_t], [T4_t])
                            qb = abank()
                            for kc in range(8):
                                MM(banks[qb][:], wcol(0, kc), xnT[:, kc, ts], kc == 0, kc == 7, [wt, xn_t[kc][tt]], [bt[qb]])
                            if KNOBS["hg_a1_dense"] == 0:
                                yield
                            ACT(T3[:], T4[:], AF.Exp, [T4_t], [T3_t])
                            if KNOBS["hg_a1_dense"] == 0:
                                yield
                            ACT(T1[:], T4[:], AF.Exp, [T4_t], [T1_t], scale=-1.0)
                            if KNOBS["hg_a1_dense"] == 0:
                                yield
                            TT("dve", qgT[b][:, ts], banks[qb][:], T3[:], ALU.mult, [bt[qb], T3_t], [qg_t[b][tt]])
                            CP("dve", dcs[b][:, tt * 8:(tt + 1) * 8], T3[:, 63:512:64], [T3_t], [dcs_t[b][tt]])
                            if KNOBS["hg_a1_dense"] == 0:
                                yield
                            TT(KNOBS["hg_mul"], kgT[b][:, ts], T2[:], T1[:], ALU.mult, [T2_t, T1_t], [kg_t[b][tt]])
                            yield

                    def stage_a2(hh):
                        b = hh % 2
                        w = whg_sb[b]
                        wt = whg_t[b]

                        def wcol(s_, kc):
                            return w[:, (s_ * 8 + kc) * 128:(s_ * 8 + kc + 1) * 128]

                        for i4 in range(4):
                            vb = abank()
                            for q4 in range(4):
                                i = i4 * 4 + q4
                                for kc in range(8):
                                    MM(banks[vb][:, q4 * 128:(q4 + 1) * 128], xnT[:, kc, i * 128:(i + 1) * 128], wcol(2, kc), kc == 0, kc == 7,
                                       [wt, xn_t[kc][i4]], [bt[vb]])
                                if KNOBS["hg_a2_dense"] == 0:
                                    yield
                            CP("dve", v_tok[b][:, i4 * 512:(i4 + 1) * 512], banks[vb][:], [bt[vb]], [v_t[b][i4]])
                            if KNOBS["hg_a2_dense"] == 0:
                                yield
                        for tt in range(4):
                            ts = tsl(tt)
                            gb = abank()
                            for kc in range(8):
                                MM(banks[gb][:], wcol(3, kc), xnT[:, kc, ts], kc == 0, kc == 7, [wt, xn_t[kc][tt]], [bt[gb]])
                            CP("dve", T7[:], banks[gb][:], [bt[gb]], [T7_t])
                            if KNOBS["hg_a2_dense"] == 0:
                                yield
                            ACT(T5[:], T7[:], AF.Exp, [T7_t], [T5_t], scale=-1.0)
                            if KNOBS["hg_a2_dense"] == 0:
                                yield
                            ACT(T5[:], T5[:], AF.Ln, [T5_t], [T5_t], bias=1.0)
                            yield
                            ACT(T5[:], T5[:], AF.Exp, [T5_t], [T5_t], scale=-1.0)
                            if KNOBS["hg_a2_dense"] == 0:
                                yield
                            TT(KNOBS["hg_mul"], sog[b][:, ts], T5[:], T7[:], ALU.mult, [T5_t, T7_t], [sog_t[b][tt]])
                            yield

                    def stage_b(hh):
                        b = hh % 2
                        bankbf3 = banks[3].bitcast(BF16)
                        for i4 in range(4):
                            tt = i4
                            for q4 in range(4):
                                i = i4 * 4 + q4
                                TR(bankbf3[:, q4 * 128:(q4 + 1) * 128], kgT[b][:, i * 128:(i + 1) * 128], ident, [kg_t[b][tt], cst_t], [bt[3]])
                            CP("dve", kg_tok[:, i4 * 512:(i4 + 1) * 512], bankbf3[:, 0:512], [bt[3]], [kgk_t[i4]])
                            if KNOBS["hg_dense"] < 3:
                                yield
                            for q4 in range(4):
                                i = i4 * 4 + q4
                                tl = slice(i * 128, (i + 1) * 128)
                                MM(banks[4][:, q4 * 128:(q4 + 1) * 128], kgT[b][:, tl], qgT[b][:, tl], True, True, [kg_t[b][tt], qg_t[b][tt]], [bt[4]])
                            TT("dve", scm[:, i4 * 512:(i4 + 1) * 512], banks[4][:], mask4, ALU.mult, [bt[4], cst_t], [scm_t[i4]])
                            if KNOBS["hg_dense"] < 3:
                                yield
                            pbs = (5, 6)
                            for q4 in range(4):
                                i = i4 * 4 + q4
                                for half in range(2):
                                    hs = slice(half * 64, (half + 1) * 64)
                                    MM(banks[pbs[half]][:, q4 * 128:(q4 + 1) * 128], kg_tok[hs, i * 128:(i + 1) * 128], v_tok[b][hs, i * 128:(i + 1) * 128],
                                       True, True, [kgk_t[i4], v_t[b][i4]], [bt[pbs[half]]])
                            if KNOBS["hg_dense"] < 3:
                                yield
                            for half in range(2):
                                if KNOBS["hg_pst"] == "act":
                                    ACT(Pst[half][:], banks[pbs[half]][:], AF.Copy, [bt[pbs[half]]], [Pst_t[half]])
                                else:
                                    CP("dve", Pst[half][:], banks[pbs[half]][:], [bt[pbs[half]]], [Pst_t[half]])
                            if KNOBS["hg_dense"] < 3:
                                yield
                            for q4 in range(4):
                                i = i4 * 4 + q4
                                for half in range(2):
                                    c = 2 * i + half
                                    if c >= 31:
                                        continue
                                    pc = Pst[half][:, q4 * 128:(q4 + 1) * 128]
                                    if c == 0:
                                        CP("dve", S32[0][:], pc, [Pst_t[half]], [S32_t[0]])
                                    else:
                                        STT(S32[c % 3][:], S32[(c - 1) % 3][:], dcs[b][:, c - 1:c], pc, ALU.mult, ALU.add,
                                            [S32_t[(c - 1) % 3], Pst_t[half], dcs_t[b][(c - 1) // 8]], [S32_t[c % 3]])
                                    if KNOBS["hg_dense"] < 2:
                                        yield
                                    if KNOBS["hg_snap"] == "pool":
                                        TS("pool", S_bf[:, c, :], S32[c % 3][:], dcs[b][:, c:c + 1], 1.0, ALU.mult, ALU.mult,
                                           [S32_t[c % 3], dcs_t[b][c // 8]], [Sbf_t[c]])
                                    elif KNOBS["hg_snap"] == "act":
                                        ACT(S_bf[:, c, :], S32[c % 3][:], AF.Copy, [S32_t[c % 3], dcs_t[b][c // 8]], [Sbf_t[c]], scale=dcs[b][:, c:c + 1])
                                    else:
                                        TS("dve", S_bf[:, c, :], S32[c % 3][:], dcs[b][:, c:c + 1], None, ALU.mult, ALU.bypass,
                                           [S32_t[c % 3], dcs_t[b][c // 8]], [Sbf_t[c]])
                                    if KNOBS["hg_dense"] < 1:
                                        yield
                            if KNOBS["hg_dense"] >= 1:
                                yield
                        for tt in range(4):
                            ts = tsl(tt)
                            ob = 7
                            for q4 in range(4):
                                i = tt * 4 + q4
                                oreg = banks[ob][:, q4 * 128:(q4 + 1) * 128]
                                c0, c1 = 2 * i, 2 * i + 1
                                MM(oreg, v_tok[b][:, i * 128:(i + 1) * 128], scm[:, i * 128:(i + 1) * 128], True, False, [v_t[b][tt], scm_t[tt]], [bt[ob]])
                                if c0 >= 1:
                                    MM(banks[ob][:, q4 * 128:q4 * 128 + 64], S_bf[:, c0 - 1, :], qgT[b][:, c0 * 64:(c0 + 1) * 64], False, False,
                                       [Sbf_t[c0 - 1], qg_t[b][tt]], [bt[ob]])
                                MM(banks[ob][:, q4 * 128 + 64:(q4 + 1) * 128], S_bf[:, c1 - 1, :], qgT[b][:, c1 * 64:(c1 + 1) * 64], False, True,
                                   [Sbf_t[c1 - 1], qg_t[b][tt]], [bt[ob]])
                            ACT(osq[:], banks[ob][:], AF.Square, [bt[ob]], [osq_t])
                            yield
                            MM(banks[3][:], ones_bf[:], osq[:], True, True, [osq_t, ones_t], [bt[3]])
                            ACT(lnv[:], banks[3][:], AF.Ln, [bt[3]], [lnv_t], scale=1.0 / 128, bias=EPS)
                            yield
                            ACT(rstd[:], lnv[:], AF.Exp, [lnv_t], [rstd_t], scale=-0.5)
                            yield
                            TT("dve", t1[:], banks[ob][:], rstd[:], ALU.mult, [bt[ob], rstd_t], [t1_t])
                            yield
                            STT(y_hgT[:, hh, ts], t1[:], vecs[:, 32 + hh:33 + hh], sog[b][:, ts], ALU.mult, ALU.mult,
                                [t1_t, sog_t[b][tt], vecs_t], [yhg_t[hh][tt]])
                            yield

                    def run_rr(gens, weights=None):
                        gens = list(gens)
                        if KNOBS["hg_mode"] == "seq":
                            for g_ in gens[1:] + gens[:1]:
                                for _ in g_:
                                    pass
                            return
                        weights = list(weights) if weights else [1.0] * len(gens)
                        credit = [0.0] * len(gens)
                        alive = [True] * len(gens)
                        while any(alive):
                            for i_, g_ in enumerate(gens):
                                if not alive[i_]:
                                    continue
                                credit[i_] += weights[i_]
                                while credit[i_] >= 1.0 and alive[i_]:
                                    credit[i_] -= 1.0
                                    try:
                                        next(g_)
                                    except StopIteration:
                                        alive[i_] = False

                    load_whg(0)
                    load_whg(1)
                    norm(8, norm_to_xn)
                    run_rr([stage_a1(0), stage_a2(0)])
                    for hh in range(4):
                        gens = [stage_b(hh)]
                        wts = [KNOBS["hg_bw"]]
                        if hh + 1 < 4:
                            if KNOBS["hg_order"] == 0:
                                gens += [stage_a1(hh + 1), stage_a2(hh + 1)]
                            elif KNOBS["hg_order"] == 1:
                                gens = [stage_a1(hh + 1), stage_a2(hh + 1)] + gens
                            else:
                                gens = [stage_a1(hh + 1)] + gens + [stage_a2(hh + 1)]
                            wts = [1.0] * len(gens)
                        if hh + 2 < 4:
                            load_whg(hh + 2)
                        run_rr(gens, wts)
                    P.barrier()
                    if KNOBS["flush"]:
                        P.flush()
                if stage >= 3:
                    merge(wmA_d, y_hgT, yhg_t, "A")

        def attention():
            with ExitStack() as ph:
                y_attT = sbuf(ph, "y_attT", [128, 4, S], BF16)
                yatt_t = toks(4, 4)
                with ExitStack() as wk:
                    watt_sb = [sbuf(wk, "watt_sb%d" % i, [128, 3072], BF16) for i in range(2)]
                    bias_sb = [sbuf(wk, "bias_sb%d" % i, [128, 2, 256], F32) for i in range(2)]
                    qR = [sbuf(wk, "qR%d" % i, [128, S], BF16) for i in range(2)]
                    kR = [[sbuf(wk, "kR%d_%d" % (i, a), [128, S], BF16) for a in range(2)] for i in range(2)]
                    vT = sbuf(wk, "vT", [128, S], BF16)
                    vaug = [sbuf(wk, "vaug%d" % i, [128, 16, 2, 65], BF16) for i in range(2)]
                    acc = sbuf(wk, "acc", [65, 2, S], F32)
                    tmp = [sbuf(wk, "tmp%d" % i, [128, 512], F32) for i in range(2)]
                    pT = [sbuf(wk, "pT%d" % i, [128, 3968], BF16) for i in range(2)]
                    watt_t = toks(2)
                    bias_t = toks(2)
                    qT_t = toks(2, 4)
                    kT_t = toks(2, 4)
                    vT_t = toks(4)
                    kz_t, = toks(1)
                    vaug_t = toks(2, 16)
                    acc_t = toks(2)
                    tmp_t = toks(2)
                    pT_t = toks(2, 8)
                    vone_t = toks(2)
                    ds_watt = [P.dsem() for _ in range(2)]
                    ds_bias = [P.dsem() for _ in range(2)]
                    cnt = {"s": 0, "p": 0, "x": 0, "j": 0}
                    PROJ_BANKS = (0, 1, 4)

                    def pbank():
                        bk = PROJ_BANKS[cnt["j"] % 3]
                        cnt["j"] += 1
                        return bk

                    def load_watt(k):
                        hp, g = k // 3, k % 3
                        b = k % 2
                        DMAG("pool", [(watt_sb[b][:, 0:2048], watt_d[hp, g, :, 0:2048], (), [watt_t[b]]),
                                      (watt_sb[b][:, 2048:3072], watt_d[hp, g, :, 2048:3072], (), [watt_t[b]])], ds_watt[b])

                    def load_bias(k):
                        hp, g = k // 3, k % 3
                        DMA("sp", bias_sb[k % 2][:], biasT_d[hp, g], ds_bias[k % 2], writes=[bias_t[k % 2]])

                    def geom(g):
                        window, d = ATT_GROUPS[g]
                        nb = S // (128 * d)
                        units = [(r, nk) for r in range(d) for nk in range(nb)]
                        nq = [256 if nk + 1 < nb else 128 for (r, nk) in units]
                        col0 = [0] * 16
                        for u in range(1, 16):
                            col0[u] = col0[u - 1] + nq[u - 1]
                        groups = []
                        cur, w_ = [], 0
                        for u in range(16):
                            if w_ + nq[u] > 512:
                                groups.append(cur)
                                cur, w_ = [], 0
                            cur.append(u)
                            w_ += nq[u]
                        groups.append(cur)
                        return d, nb, units, nq, col0, groups

                    def res_out(t, d, tt):
                        L4 = 512 // d
                        return t.rearrange("p (r l) -> p r l", r=d)[:, :, tt * L4:(tt + 1) * L4]

                    def res_in(bank_ap, d):
                        return bank_ap.rearrange("p (l r) -> p r l", r=d)

                    def proj_gen(k):
                        hp, g = k // 3, k % 3
                        b = k % 2
                        if k + 1 < 12:
                            load_watt(k + 1)
                        w = watt_sb[b]
                        wt = watt_t[b]
                        d, nb, units, nq, col0, groups = geom(g)

                        def wcol(s_, kc):
                            return w[:, (s_ * 8 + kc) * 128:(s_ * 8 + kc + 1) * 128]

                        for tt in range(4):
                            ts = tsl(tt)
                            bk = pbank()
                            for kc in range(8):
                                MM(banks[bk][:], wcol(0, kc), xnT[:, kc, ts], kc == 0, kc == 7, [wt, xn_t[kc][tt]], [bt[bk]])
                            ACT(res_out(qR[b][:, :], d, tt), res_in(banks[bk][:, :], d), AF.Copy, [bt[bk]], [qT_t[b][tt]], scale=0.125)
                            yield
                            bk = pbank()
                            for kc in range(8):
                                MM(banks[bk][:], wcol(1, kc), xnT[:, kc, ts], kc == 0, kc == 7, [wt, xn_t[kc][tt]], [bt[bk]])
                            ACT(res_out(kR[b][0][0:64, :], d, tt), res_in(banks[bk][0:64, :], d), AF.Copy, [bt[bk], kz_t], [kT_t[b][tt]])
                            ACT(res_out(kR[b][1][64:128, :], d, tt), res_in(banks[bk][64:128, :], d), AF.Copy, [bt[bk], kz_t], [kT_t[b][tt]])
                            yield
                        for tt in range(4):
                            ts = tsl(tt)
                            bk = pbank()
                            for kc in range(8):
                                MM(banks[bk][:], wcol(2, kc), xnT[:, kc, ts], kc == 0, kc == 7, [wt, xn_t[kc][tt]], [bt[bk]])
                            ACT(res_out(vT[:, :], d, tt), res_in(banks[bk][:, :], d), AF.Copy, [bt[bk]], [vT_t[tt]])
                            yield
                        for u4 in range(4):
                            bk = pbank()
                            bkbf = banks[bk].bitcast(BF16)
                            for q4 in range(4):
                                u = u4 * 4 + q4
                                TR(bkbf[:, q4 * 128:(q4 + 1) * 128], vT[:, u * 128:(u + 1) * 128], ident, vT_t + [cst_t], [bt[bk]])
                            P.op("act", lambda h, bkbf=bkbf, u4=u4, b=b: h.activation(
                                out=vaug[b][:, u4 * 4:(u4 + 1) * 4, :, 0:64],
                                in_=bkbf[:, 0:512].rearrange("p (u a e) -> p u a e", u=4, a=2),
                                func=AF.Copy), [bt[bk]], vaug_t[b][u4 * 4:(u4 + 1) * 4])
                            yield

                    def units_gen(k):
                        hp, g = k // 3, k % 3
                        b = k % 2
                        bsb, bst = bias_sb[b], bias_t[b]
                        d, nb, units, nq, col0, groups = geom(g)
                        gof = {}
                        for m, grp in enumerate(groups):
                            for u in grp:
                                gof[u] = m
                        if g == 0:
                            MS("dve", acc[:], 0.0, acc_t)
                        if k + 1 < 12:
                            load_bias(k + 1)

                        def bias_ap(a, width):
                            base = bsb[:, a, 0:1]
                            pstep = base.ap[0][0]
                            if d == 16:
                                return bass.AP(bsb, base.offset, [[pstep, 128], [0, width // 128], [1, 128]])
                            assert width in (512, 384)
                            if width == 512:
                                return bass.AP(bsb, base.offset, [[pstep, 128], [0, 2], [1, 256]])
                            return None

                        def score_group(a, m):
                            grp = groups[m]
                            sbk = 5 + cnt["s"] % 3
                            x2 = cnt["x"] % 2
                            cnt["s"] += 1
                            cnt["x"] += 1
                            off = 0
                            for u in grp:
                                r, nk = units[u]
                                t0 = nk * 128 * d + r
                                c_ = u * 128
                                ktts = sorted(set([t0 // 512, (t0 + 127 * d) // 512]))
                                qtts = sorted(set(range(t0 // 512, (t0 + (nq[u] - 1) * d) // 512 + 1)))
                                MM(banks[sbk][:, off:off + nq[u]], kR[b][a][:, c_:c_ + 128], qR[b][:, c_:c_ + nq[u]], True, True,
                                   [kT_t[b][t] for t in ktts] + [qT_t[b][t] for t in qtts] + [kz_t], [bt[sbk]])
                                off += nq[u]
                            c0 = col0[grp[0]]
                            bap = bias_ap(a, off)
                            if bap is not None:
                                if d == 16:
                                    o_ap = tmp[x2][:, 0:off].rearrange("p (u i) -> p u i", i=128)
                                    i_ap = banks[sbk][:, 0:off].rearrange("p (u i) -> p u i", i=128)
                                else:
                                    o_ap = tmp[x2][:, 0:off].rearrange("p (u i) -> p u i", i=256)
                                    i_ap = banks[sbk][:, 0:off].rearrange("p (u i) -> p u i", i=256)
                                TT("dve", o_ap, i_ap, bap, ALU.add, [bt[sbk], bst], [tmp_t[x2]])
                            else:
                                TT("dve", tmp[x2][:, 0:256], banks[sbk][:, 0:256], bsb[:, a, 0:256], ALU.add, [bt[sbk], bst], [tmp_t[x2]])
                                TT("dve", tmp[x2][:, 256:384], banks[sbk][:, 256:384], bsb[:, a, 0:128], ALU.add, [bt[sbk], bst], [tmp_t[x2]])
                            ACT(pT[a][:, c0:c0 + off], tmp[x2][:, 0:off], AF.Exp, [tmp_t[x2]], [pT_t[a][m]])

                        def pv_group(a, j):
                            pvb = 2 + cnt["p"] % 2
                            cnt["p"] += 1
                            for q4 in range(4):
                                u = 4 * j + q4
                                r, n = units[u]
                                reg = banks[pvb][0:65, q4 * 128:(q4 + 1) * 128]
                                has_prev = n > 0
                                MM(reg, vaug[b][:, u, a, :], pT[a][:, col0[u]:col0[u] + 128], True, not has_prev,
                                   [vaug_t[b][u], vone_t[b], pT_t[a][gof[u]]], [bt[pvb]])
                                if has_prev:
                                    MM(reg, vaug[b][:, u - 1, a, :], pT[a][:, col0[u - 1] + 128:col0[u - 1] + 256], False, True,
                                       [vaug_t[b][u - 1], vone_t[b], pT_t[a][gof[u - 1]]], [bt[pvb]])
                            if d == 1:
                                dst = acc[:, a, j * 512:(j + 1) * 512]
                                src = banks[pvb][0:65, :]
                            elif d == 4:
                                dst = acc[:, a, j:j + 4 * 511 + 1:4]
                                src = banks[pvb][0:65, :]
                            else:
                                dst = acc[:, a, :].rearrange("p (i r) -> p r i", r=16)[:, 4 * j:4 * j + 4, :]
                                src = banks[pvb][0:65, :].rearrange("p (r i) -> p r i", r=4)
                            TT("dve", dst, dst, src, ALU.add, [bt[pvb], acc_t[a]], [acc_t[a]])

                        sched = []
                        emitted = -1
                        for j in range(4):
                            need = min(gof[4 * j + 3] + 1, len(groups) - 1)
                            for m in range(emitted + 1, need + 1):
                                sched.append(("S", m))
                            emitted = max(emitted, need)
                            sched.append(("P", j))
                        for kind, idx in sched:
                            for a in range(2):
                                if kind == "S":
                                    score_group(a, idx)
                                else:
                                    pv_group(a, idx)
                                yield
                        if g == 2:
                            for a in range(2):
                                for tt in range(4):
                                    ts = tsl(tt)
                                    nbk = 2 + (a * 4 + tt) % 2
                                    MM(banks[nbk][0:64, :], ones_f[64:65, 0:64], acc[64:65, a, ts], True, True, [acc_t[a], ones_t], [bt[nbk]])
                                    ACT(lnv[0:64, :], banks[nbk][0:64, :], AF.Ln, [bt[nbk]], [lnv_t])
                                    ACT(rstd[0:64, :], lnv[0:64, :], AF.Exp, [lnv_t], [rstd_t], scale=-1.0)
                                    TT("dve", y_attT[a * 64:(a + 1) * 64, hp, ts], acc[0:64, a, ts], rstd[0:64, :], ALU.mult, [acc_t[a], rstd_t], [yatt_t[hp][tt]])
                                    yield

                    for b in range(2):
                        MS("dve", vaug[b][:, :, :, 64:65], 1.0, vaug_t[b] + [vone_t[b]])
                        MS("dve", kR[b][0][64:128, :], 0.0, [kz_t])
                        MS("dve", kR[b][1][0:64, :], 0.0, [kz_t])
                    load_watt(0)
                    load_bias(0)
                    for _ in proj_gen(0):
                        pass
                    for k in range(12):
                        gu = units_gen(k)
                        gp = proj_gen(k + 1) if k + 1 < 12 else iter(())
                        alive_u, alive_p = True, True
                        cu, cp_ = 0.0, 0.0
                        while alive_u or alive_p:
                            cu += KNOBS["at_uw"]
                            while alive_u and cu >= 1.0:
                                cu -= 1.0
                                try:
                                    next(gu)
                                except StopIteration:
                                    alive_u = False
                            cp_ += KNOBS["at_pw"]
                            while alive_p and cp_ >= 1.0:
                                cp_ -= 1.0
                                try:
                                    next(gp)
                                except StopIteration:
                                    alive_p = False
                    P.barrier()
                    if KNOBS["flush"]:
                        P.flush()
                if stage >= 0:
                    merge(wmB_d, y_attT, yatt_t, "B")

        if stage >= 0:
            ffn(0, 0)
        if stage >= 2 or stage == -1:
            hgrn2()
        if stage == -2:
            norm(8, norm_to_xn)
        if stage >= 4 or stage == -2:
            attention()
        if stage >= 5:
            ffn(1, 16)
        if stage >= 6:
            norm(24, norm_inplace)
        ds_outs = [ds_out] + [P.dsem() for _ in range(3)]
        ev_out = [DMAG("sp", [(outT_d[c * 128:(c + 1) * 128, tsl(tt)], hT[:, c, tsl(tt)], [hT_t[c][tt]], ()) for c in range(8)], ds_outs[tt])
                  for tt in range(4)]
        P.wait_all("sp", ev_out)
        P.flush()
    return nc


def _const_tables():
    cst = np.zeros((128, 640), np.float32)
    cst[:, 0:128] = np.eye(128, dtype=np.float32)
    s = np.arange(128)[:, None]
    t = np.arange(128)[None, :]
    for q in range(4):
        cst[:, 128 + q * 128:256 + q * 128] = ((s <= t) & (s // 64 == t // 64)).astype(np.float32)
    n_heads = 24
    slopes = np.exp2(-8.0 * np.arange(1, n_heads + 1, dtype=np.float32) / n_heads).astype(np.float32)
    biasT = np.zeros((4, 3, 128, 2, 256), np.float32)
    j = np.arange(128)[:, None].astype(np.float32)
    i = np.arange(128)[None, :].astype(np.float32)
    for hp in range(4):
        for g, (window, d) in enumerate(ATT_GROUPS):
            for a in range(2):
                sl = slopes[g * 8 + hp * 2 + a]
                biasT[hp, g, :, a, 0:128] = np.where(i >= j, -sl * (d * (i - j)), NEG)
                biasT[hp, g, :, a, 128:256] = np.where(i <= j, -sl * (d * (i + 128.0 - j)), NEG)
    return cst, biasT.reshape(4, 3, 128, 512)


def _prep_weights(inp):
    f = lambda a: np.ascontiguousarray(a, dtype=np.float32)
    out = {}
    for tag, kgu, kd in (("1", "ffn1_w_gate_up", "ffn1_w_down"), ("2", "ffn2_w_gate_up", "ffn2_w_down")):
        W = np.asarray(inp[kgu])[0]
        W5 = W.reshape(8, 128, 2, NJ, 128)
        out["wgu" + tag] = f(W5.transpose(3, 1, 2, 0, 4).reshape(NJ, 128, 2048))
        out["wd" + tag] = f(np.asarray(inp[kd])[0].reshape(NJ, 128, 1024))
    Win = np.asarray(inp["w_in"])[0]
    Whg = Win[:, 0:2048].reshape(8, 128, 4, 4, 128)
    out["whg"] = f(Whg.transpose(3, 1, 2, 0, 4).reshape(4, 128, 4096))
    Watt = Win[:, 2048:6656].reshape(8, 128, 3, 3, 4, 128)
    out["watt"] = f(Watt.transpose(4, 2, 1, 3, 0, 5).reshape(4, 3, 128, 3072))
    for tag, c0, kb in (("A", 6656, "w_branch_hg"), ("B", 7680, "w_branch_att")):
        Wg = Win[:, c0:c0 + 1024].reshape(8, 128, 8, 128)
        Wb = np.asarray(inp[kb])[0].reshape(4, 128, 8, 128)
        wm = np.concatenate([Wg.transpose(2, 1, 0, 3).reshape(8, 128, 1024), Wb.transpose(2, 1, 0, 3).reshape(8, 128, 512)], axis=2)
        out["wm" + tag] = f(wm)
    Wo = np.asarray(inp["w_out"])[0].reshape(2, 4, 128, 1024)
    out["wo"] = f(Wo.transpose(0, 2, 1, 3).reshape(2, 128, 4096))
    vecs = np.zeros((128, 48), np.float32)
    vecs[:, 0:8] = np.asarray(inp["ffn1_norm"])[0].reshape(8, 128).T
    vecs[:, 8:16] = np.asarray(inp["mix_norm"])[0].reshape(8, 128).T
    vecs[:, 16:24] = np.asarray(inp["ffn2_norm"])[0].reshape(8, 128).T
    vecs[:, 24:32] = np.asarray(inp["final_norm"]).reshape(8, 128).T
    vecs[:, 32:36] = np.asarray(inp["hg_out_norm"])[0].reshape(4, 128).T
    lbs = np.asarray(inp["hg_lower_bounds"])
    vecs[:, 36:40] = lbs[0].reshape(4, 128).T
    vecs[:, 40:44] = lbs[1].reshape(4, 128).T
    out["vecs"] = vecs
    cst, biasT = _const_tables()
    out["cst"] = cst
    out["biasT"] = biasT
    return out


_NC_CACHE = {}


def _get_nc(stage=99):
    if stage not in _NC_CACHE:
        _NC_CACHE[stage] = build_program(stage)
    return _NC_CACHE[stage]


def kernel(**inputs):
    x = np.asarray(inputs["x"], dtype=np.float32)
    shared = _prep_weights(inputs)
    nc = _get_nc()
    in_maps = []
    for b in range(NCORES):
        m = dict(shared)
        m["xT"] = np.ascontiguousarray(x[b].T)
        in_maps.append(m)
    res = run_bass_kernel_spmd(nc, in_maps, core_ids=list(range(NCORES)))
    out = np.stack([np.ascontiguousarray(r["outT"].T) for r in res.results], axis=0)
    return out.astype(np.float32)
```

```python
import bisect
from contextlib import ExitStack

import numpy as np
import concourse.bass as bass
import concourse.mybir as mybir
from concourse.bass_utils import run_bass_kernel_spmd

F32 = mybir.dt.float32
BF16 = mybir.dt.bfloat16
AF = mybir.ActivationFunctionType
ALU = mybir.AluOpType

S = 2048
D = 1024
DFF = 2816
NJ = DFF // 128
EPS = 1e-6
NCORES = 8
ATT_GROUPS = ((128, 1), (512, 4), (2048, 16))
NEG = -30000.0
FFN_GROUPS = ((0, 1, 2, 3, 4, 5, 6, 7), (8, 9, 10, 11, 12, 13, 14), (15, 16, 17, 18, 19, 20, 21))
GMAX = 8
KNOBS = {"hg_mode": "rr", "hg_bw": 1.0, "hg_snap": "pool", "hg_dense": 2, "hg_a1_dense": 0, "hg_a2_dense": 0, "hg_pst": "act", "hg_order": 0, "hg_ob2": 1, "flush": 0, "hg_mul": "pool", "at_uw": 2.0, "at_pw": 1.0}


class Tok:
    __slots__ = ("w", "r")

    def __init__(self):
        self.w = None
        self.r = {}


def toks(*shape):
    if len(shape) == 1:
        return [Tok() for _ in range(shape[0])]
    return [toks(*shape[1:]) for _ in range(shape[0])]


class Group:
    def __init__(self, kids):
        self.kids = kids


def _expand(ts):
    out = []
    for t in ts:
        if isinstance(t, Group):
            out.extend(t.kids)
        else:
            out.append(t)
    return out


class DSem:
    def __init__(self, h):
        self.h = h
        self.count = 0


class Prog:
    ENG = ("pe", "act", "dve", "pool", "sp")

    def __init__(self, nc, stack):
        self.nc = nc
        self.stack = stack
        self.ops = {e: [] for e in self.ENG}
        self.needed = {e: set() for e in self.ENG}
        self.need_sorted = {e: [] for e in self.ENG}
        self.evval = {e: {} for e in self.ENG}
        self.ecount = {e: 0 for e in self.ENG}
        self.flushed = {e: 0 for e in self.ENG}
        self.waited = {e: {} for e in self.ENG}
        self.esem = {e: stack.enter_context(nc.semaphore("es_" + e)) for e in self.ENG}
        self.nds = 0

    def dsem(self):
        self.nds += 1
        return DSem(self.stack.enter_context(self.nc.semaphore("ds%d" % self.nds)))

    def _waits(self, eng, reads, writes):
        ws = []
        for t in reads:
            if t.w is not None:
                ws.append(t.w)
        for t in writes:
            if t.w is not None:
                ws.append(t.w)
            for k, ev in t.r.items():
                if k == eng and eng == "pe":
                    continue
                ws.append(ev)
        out = []
        for ev in ws:
            if ev[0] == "E":
                if ev[1] == eng and eng == "pe":
                    continue
                if ev[2] >= self.flushed[ev[1]]:
                    self.needed[ev[1]].add(ev[2])
            out.append(ev)
        return out

    def _mark(self, ev, key, reads, writes):
        for t in reads:
            t.r[key] = ev
        for t in writes:
            t.w = ev
            t.r = {}

    def op(self, eng, fn, reads=(), writes=()):
        reads, writes = _expand(reads), _expand(writes)
        waits = self._waits(eng, reads, writes)
        ev = ("E", eng, len(self.ops[eng]))
        self.ops[eng].append(dict(waits=waits, fn=fn, kind="c"))
        self._mark(ev, eng, reads, writes)
        return ev

    def dma(self, q, fn, ds, reads=(), writes=()):
        reads, writes = _expand(reads), _expand(writes)
        waits = self._waits("dma", reads, writes)
        ds.count += 16
        ev = ("D", ds, ds.count)
        self.ops[q].append(dict(waits=waits, fn=fn, kind="d", ds=ds))
        self._mark(ev, ("D", id(ds)), reads, writes)
        return ev

    def dma_group(self, q, items, ds):
        final = ("D", ds, ds.count + 16 * len(items))
        prepared = []
        for fn, reads, writes in items:
            reads, writes = _expand(reads), _expand(writes)
            prepared.append((fn, reads, writes, self._waits("dma", reads, writes)))
        for fn, reads, writes, waits in prepared:
            ds.count += 16
            self.ops[q].append(dict(waits=waits, fn=fn, kind="d", ds=ds))
            self._mark(final, ("D", id(ds)), reads, writes)
        return final

    def wait_all(self, eng, evs):
        for ev in evs:
            if ev[0] == "E" and ev[2] >= self.flushed[ev[1]]:
                self.needed[ev[1]].add(ev[2])
        self.ops[eng].append(dict(waits=list(evs), fn=None, kind="w"))

    def barrier(self):
        last = {}
        for e in self.ENG:
            for i in range(len(self.ops[e]) - 1, -1, -1):
                if self.ops[e][i]["kind"] == "c":
                    last[e] = ("E", e, i)
                    break
        for e in self.ENG:
            self.wait_all(e, [ev for k, ev in last.items() if k != e])

    def _resolve(self, ev):
        e, idx = ev[1], ev[2]
        ns = self.need_sorted[e]
        k = bisect.bisect_left(ns, idx)
        return self.evval[e][ns[k]]

    def flush(self):
        for e in self.ENG:
            n = len(self.ops[e])
            for i in range(n - 1, self.flushed[e] - 1, -1):
                if self.ops[e][i]["kind"] == "c":
                    self.needed[e].add(i)
                    break
            c = self.ecount[e]
            for i in range(self.flushed[e], n):
                if i in self.needed[e]:
                    c += 1
                    self.evval[e][i] = c
                    self.need_sorted[e].append(i)
            self.ecount[e] = c
        prog = self

        def run(e, h):
            waited = prog.waited[e]
            for i in range(prog.flushed[e], len(prog.ops[e])):
                o = prog.ops[e][i]
                for ev in o["waits"]:
                    if ev[0] == "E":
                        sem = prog.esem[ev[1]]
                        val = prog._resolve(ev)
                        key = ("E", ev[1])
                    else:
                        sem = ev[1].h
                        val = ev[2]
                        key = ("D", id(ev[1]))
                    if waited.get(key, 0) >= val:
                        continue
                    waited[key] = val
                    h.wait_ge(sem, val)
                if o["fn"] is None:
                    continue
                inst = o["fn"](h)
                if o["kind"] == "d":
                    inst.then_inc(o["ds"].h, 16)
                elif i in prog.needed[e]:
                    inst.then_inc(prog.esem[e], 1)

        with self.nc.Block() as block:
            @block.tensor
            def _(h):
                run("pe", h)

            @block.scalar
            def _(h):
                run("act", h)

            @block.vector
            def _(h):
                run("dve", h)

            @block.gpsimd
            def _(h):
                run("pool", h)

            @block.sync
            def _(h):
                run("sp", h)

        for e in self.ENG:
            self.flushed[e] = len(self.ops[e])


def build_program(stage=99):
    nc = bass.Bass("TRN2", target_bir_lowering=False)

    def dram(name, shape, kind="ExternalInput"):
        return nc.dram_tensor(name, list(shape), F32, kind=kind).ap()

    xT_d = dram("xT", [D, S])
    wgu_d = [dram("wgu1", [NJ, 128, 2048]), dram("wgu2", [NJ, 128, 2048])]
    wd_d = [dram("wd1", [NJ, 128, 1024]), dram("wd2", [NJ, 128, 1024])]
    whg_d = dram("whg", [4, 128, 4096])
    watt_d = dram("watt", [4, 3, 128, 3072])
    wmA_d = dram("wmA", [8, 128, 1536])
    wmB_d = dram("wmB", [8, 128, 1536])
    wo_d = dram("wo", [2, 128, 4096])
    vecs_d = dram("vecs", [128, 48])
    biasT_d = dram("biasT", [4, 3, 128, 512])
    cst_d = dram("cst", [128, 640])
    outT_d = dram("outT", [D, S], kind="ExternalOutput")

    with ExitStack() as st:
        P = Prog(nc, st)

        def sbuf(stack, name, shape, dt):
            return stack.enter_context(nc.sbuf_tensor("s_" + name, list(shape), dt))

        def MM(out, lhsT, rhs, start, stop, reads, writes):
            return P.op("pe", lambda h: h.matmul(out, lhsT=lhsT, rhs=rhs, start=start, stop=stop), reads, writes)

        def TR(out, in_, ident, reads, writes):
            return P.op("pe", lambda h: h.transpose(out, in_, ident), reads, writes)

        def ACT(out, in_, func, reads, writes, **kw):
            return P.op("act", lambda h: h.activation(out=out, in_=in_, func=func, **kw), reads, writes)

        def TT(eng, out, in0, in1, op, reads, writes):
            return P.op(eng, lambda h: h.tensor_tensor(out=out, in0=in0, in1=in1, op=op), reads, writes)

        def STT(out, in0, scalar, in1, op0, op1, reads, writes):
            return P.op("dve", lambda h: h.scalar_tensor_tensor(out=out, in0=in0, scalar=scalar, in1=in1, op0=op0, op1=op1), reads, writes)

        def TS(eng, out, in0, s1, s2, op0, op1, reads, writes):
            return P.op(eng, lambda h: h.tensor_scalar(out=out, in0=in0, scalar1=s1, scalar2=s2, op0=op0, op1=op1), reads, writes)

        def CP(eng, out, in_, reads, writes):
            return P.op(eng, lambda h: h.tensor_copy(out=out, in_=in_), reads, writes)

        def MS(eng, ap, val, writes):
            return P.op(eng, lambda h: h.memset(ap, val), (), writes)

        def DMA(q, out, in_, ds, reads=(), writes=()):
            return P.dma(q, lambda h: h.dma_start(out=out, in_=in_), ds, reads, writes)

        def _dfn(out, in_):
            return lambda h: h.dma_start(out=out, in_=in_)

        def DMAG(q, items, ds):
            return P.dma_group(q, [(_dfn(o, i), r, w) for (o, i, r, w) in items], ds)

        hT = sbuf(st, "hT", [128, 8, S], F32)
        xnT = sbuf(st, "xnT", [128, 8, S], BF16)
        vecs = sbuf(st, "vecs", [128, 48], F32)
        cstb = sbuf(st, "cstb", [128, 640], BF16)
        ones_bf = sbuf(st, "ones_bf", [128, 128], BF16)
        ones_f = sbuf(st, "ones_f", [128, 64], F32)
        oml = sbuf(st, "oml", [128, 4], F32)
        lbd = sbuf(st, "lbd", [128, 4], F32)
        sq = [sbuf(st, "sq%d" % i, [128, 512], BF16) for i in range(2)]
        lnv = sbuf(st, "lnv", [128, 512], F32)
        rstd = sbuf(st, "rstd", [128, 512], F32)
        banks = [st.enter_context(nc.psum_tensor("bank%d" % i, [128, 512], F32)) for i in range(8)]

        hT_t = toks(8, 4)
        xn_t = toks(8, 4)
        vecs_t, cst_t, ones_t, rmask_t, oml_t, lbd_t, lnv_t, rstd_t = toks(8)
        sq_t = toks(2)
        bq = toks(8, 4)
        bt = [Group(bq[i]) for i in range(8)]
        ident = cstb[:, 0:128]
        mask4 = cstb[:, 128:640]

        def tsl(tt):
            return slice(tt * 512, (tt + 1) * 512)

        ds_x = P.dsem()
        ds_c = P.dsem()
        ds_v = P.dsem()
        ds_out = P.dsem()

        DMA("sp", vecs[:], vecs_d, ds_v, writes=[vecs_t])
        ds_xs = [ds_x] + [P.dsem() for _ in range(3)]
        for tt in range(4):
            dep = [hT_t[0][tt - 1]] if tt > 0 else []
            DMAG("sp", [(hT[:, c, tsl(tt)], xT_d[c * 128:(c + 1) * 128, tsl(tt)], dep, [hT_t[c][tt]]) for c in range(8)], ds_xs[tt])
        DMA("pool", cstb[:], cst_d, ds_c, writes=[cst_t])
        MS("dve", ones_bf[:], 1.0, [ones_t])
        MS("dve", ones_f[:], 1.0, [ones_t])
        TT("dve", lbd[:], vecs[:, 40:44], vecs[:, 36:40], ALU.subtract, [vecs_t], [lbd_t])
        ACT(oml[:], lbd[:], AF.Sigmoid, [lbd_t], [oml_t])

        def norm(gcol, out_fn, nfeat=D):
            for tt in range(4):
                norm_tile(tt, gcol, out_fn, nfeat)

        def norm_tile(tt, gcol, out_fn, nfeat=D):
            if True:
                ts = tsl(tt)
                sb_, rb_ = (7, 6) if tt % 2 == 0 else (4, 5)
                for c in range(8):
                    ACT(sq[c % 2][:], hT[:, c, ts], AF.Square, [hT_t[c][tt]], [sq_t[c % 2]])
                    MM(banks[sb_][:], ones_bf[:], sq[c % 2][:], c == 0, c == 7, [sq_t[c % 2], ones_t], [bt[sb_]])
                ACT(lnv[:], banks[sb_][:], AF.Ln, [bt[sb_]], [lnv_t], scale=1.0 / nfeat, bias=EPS)
                ACT(banks[rb_][:], lnv[:], AF.Exp, [lnv_t], [bt[rb_]], scale=-0.5)
                for c in range(8):
                    out_fn(c, tt, ts, gcol, rb_)

        def norm_to_xn(c, tt, ts, gcol, rb_):
            STT(xnT[:, c, ts], hT[:, c, ts], vecs[:, gcol + c:gcol + c + 1], banks[rb_][:], ALU.mult, ALU.mult,
                [hT_t[c][tt], bt[rb_], vecs_t], [xn_t[c][tt]])

        def norm_inplace(c, tt, ts, gcol, rb_):
            STT(hT[:, c, ts], hT[:, c, ts], vecs[:, gcol + c:gcol + c + 1], banks[rb_][:], ALU.mult, ALU.mult,
                [hT_t[c][tt], bt[rb_], vecs_t], [hT_t[c][tt]])

        def ffn(which, gcol):
            wgu, wd = wgu_d[which], wd_d[which]
            with ExitStack() as ph:
                actT = sbuf(ph, "actT%d" % which, [128, GMAX, S], BF16)
                wgu_sb = [sbuf(ph, "wgu_sb%d_%d" % (which, i), [128, 2048], BF16) for i in range(3)]
                wd_sb = sbuf(ph, "wd_sb%d" % which, [128, GMAX, 1024], BF16)
                sg = [sbuf(ph, "sg%d_%d" % (which, i), [128, 512], F32) for i in range(2)]
                act_t = toks(GMAX, 4)
                wgu_t = toks(3)
                wd_t = toks(GMAX)
                sg_t = toks(2)
                ds_wgu = [P.dsem() for _ in range(3)]
                ds_wd = P.dsem()

                def load_wgu(j):
                    dep = [hT_t[0][0]] if (which == 0 and j < 3) else []
                    DMA("pool", wgu_sb[j % 3][:], wgu[j], ds_wgu[j % 3], reads=dep, writes=[wgu_t[j % 3]])

                load_wgu(0)
                load_wgu(1)
                for grp in FFN_GROUPS:
                    for jj, j in enumerate(grp):
                        if j + 2 < NJ:
                            load_wgu(j + 2)
                        if jj == 1:
                            DMAG("pool", [(wd_sb[:, jj2, :], wd[j2], (), [wd_t[jj2]]) for jj2, j2 in enumerate(grp)], ds_wd)
                        w = wgu_sb[j % 3]
                        for tt in range(4):
                            if j == 0:
                                norm_tile(tt, gcol, norm_to_xn)
                            ts = tsl(tt)
                            gb, ub = tt % 2, 2 + tt % 2
                            for kc in range(8):
                                MM(banks[gb][:], w[:, kc * 128:(kc + 1) * 128], xnT[:, kc, ts], kc == 0, kc == 7,
                                   [wgu_t[j % 3], xn_t[kc][tt]], [bt[gb]])
                            for kc in range(8):
                                MM(banks[ub][:], w[:, (8 + kc) * 128:(9 + kc) * 128], xnT[:, kc, ts], kc == 0, kc == 7,
                                   [wgu_t[j % 3], xn_t[kc][tt]], [bt[ub]])
                            ACT(sg[tt % 2][:], banks[gb][:], AF.Silu, [bt[gb]], [sg_t[tt % 2]])
                            TT("dve", actT[:, jj, ts], sg[tt % 2][:], banks[ub][:], ALU.mult, [sg_t[tt % 2], bt[ub]], [act_t[jj][tt]])
                    n = len(grp)
                    for m in range(8):
                        for tt in range(4):
                            ts = tsl(tt)
                            db = 4 + (m * 4 + tt) % 2
                            for jj in range(n):
                                MM(banks[db][:], wd_sb[:, jj, m * 128:(m + 1) * 128], actT[:, jj, ts], jj == 0, jj == n - 1,
                                   [wd_t[jj], act_t[jj][tt]], [bt[db]])
                            STT(hT[:, m, ts], banks[db][:], 0.5, hT[:, m, ts], ALU.mult, ALU.add,
                                [bt[db], hT_t[m][tt]], [hT_t[m][tt]])
                P.barrier()
                if KNOBS["flush"]:
                    P.flush()

        def merge(wm_d, yT, y_t, tag):
            with ExitStack() as ph:
                wm_sb = [sbuf(ph, "wm_sb%s%d" % (tag, i), [128, 1536], BF16) for i in range(2)]
                wo_sb = sbuf(ph, "wo_sb" + tag, [128, 4096], BF16)
                sA = [sbuf(ph, "sA%s%d" % (tag, i), [128, 512], F32) for i in range(2)]
                mT = sbuf(ph, "mT" + tag, [128, 4, S], BF16)
                wm_t = toks(2)
                wo_t, = toks(1)
                sA_t = toks(2)
                mT_t = toks(4, 4)
                ds_wm = [P.dsem() for _ in range(2)]
                ds_wo = P.dsem()

                def load_wm(dm):
                    DMA("pool", wm_sb[dm % 2][:], wm_d[dm], ds_wm[dm % 2], writes=[wm_t[dm % 2]])

                load_wm(0)
                for dmg in range(2):
                    for dmi in range(4):
                        dm = dmg * 4 + dmi
                        if dm + 1 < 8:
                            load_wm(dm + 1)
                        if dmi == 1:
                            DMAG("pool", [(wo_sb[:, 0:2048], wo_d[dmg, :, 0:2048], (), [wo_t]),
                                          (wo_sb[:, 2048:4096], wo_d[dmg, :, 2048:4096], (), [wo_t])], ds_wo)
                        w = wm_sb[dm % 2]
                        for tt in range(4):
                            ts = tsl(tt)
                            gb, bb = tt % 2, 2 + tt % 2
                            for kc in range(8):
                                MM(banks[gb][:], w[:, kc * 128:(kc + 1) * 128], xnT[:, kc, ts], kc == 0, kc == 7,
                                   [wm_t[dm % 2], xn_t[kc][tt]], [bt[gb]])
                            for c in range(4):
                                MM(banks[bb][:], w[:, (8 + c) * 128:(9 + c) * 128], yT[:, c, ts], c == 0, c == 3,
                                   [wm_t[dm % 2], y_t[c][tt]], [bt[bb]])
                            ACT(sA[tt % 2][:], banks[gb][:], AF.Sigmoid, [bt[gb]], [sA_t[tt % 2]])
                            TT("dve", mT[:, dmi, ts], sA[tt % 2][:], banks[bb][:], ALU.mult, [sA_t[tt % 2], bt[bb]], [mT_t[dmi][tt]])
                    for m in range(8):
                        for tt in range(4):
                            ts = tsl(tt)
                            db = 4 + (m * 4 + tt) % 2
                            for dmi in range(4):
                                MM(banks[db][:], wo_sb[:, dmi * 1024 + m * 128: dmi * 1024 + (m + 1) * 128], mT[:, dmi, ts],
                                   dmi == 0, dmi == 3, [wo_t, mT_t[dmi][tt]], [bt[db]])
                            TT("dve", hT[:, m, ts], banks[db][:], hT[:, m, ts], ALU.add, [bt[db], hT_t[m][tt]], [hT_t[m][tt]])
                P.barrier()
                if KNOBS["flush"]:
                    P.flush()

        def hgrn2():
            with ExitStack() as ph:
                y_hgT = sbuf(ph, "y_hgT", [128, 4, S], BF16)
                yhg_t = toks(4, 4)
                with ExitStack() as wk:
                    whg_sb = [sbuf(wk, "whg_sb%d" % i, [128, 4096], BF16) for i in range(2)]
                    rmask = sbuf(wk, "rmask", [128, 512], F32)
                    lnoml = sbuf(wk, "lnoml", [128, 4], F32)
                    T1 = sbuf(wk, "T1", [128, 512], F32)
                    T2 = sbuf(wk, "T2", [128, 512], F32)
                    T3 = sbuf(wk, "T3", [128, 512], F32)
                    T4 = sbuf(wk, "T4", [128, 512], F32)
                    T5 = sbuf(wk, "T5", [128, 512], F32)
                    T7 = sbuf(wk, "T7", [128, 512], F32)
                    dcs = [sbuf(wk, "dcs%d" % i, [128, 32], F32) for i in range(2)]
                    qgT = [sbuf(wk, "qgT%d" % i, [128, S], BF16) for i in range(2)]
                    kgT = [sbuf(wk, "kgT%d" % i, [128, S], BF16) for i in range(2)]
                    v_tok = [sbuf(wk, "v_tok%d" % i, [128, S], BF16) for i in range(2)]
                    sog = [sbuf(wk, "sog%d" % i, [128, S], BF16) for i in range(2)]
                    kg_tok = sbuf(wk, "kg_tok", [128, S], BF16)
                    scm = sbuf(wk, "scm", [128, S], BF16)
                    S_bf = sbuf(wk, "S_bf", [128, 32, 128], BF16)
                    S32 = [sbuf(wk, "S32_%d" % i, [128, 128], F32) for i in range(3)]
                    Pp = [sbuf(wk, "Pp%d" % i, [128, 128], F32) for i in range(3)]
                    Pst = [sbuf(wk, "Pst%d" % i, [128, 512], F32) for i in range(2)]
                    osq, t1 = sq[0], lnv
                    whg_t = toks(2)
                    T1_t, T2_t, T3_t, T4_t, T5_t, T7_t, lnoml_t = toks(7)
                    osq_t, t1_t = sq_t[0], lnv_t
                    Pst_t = toks(2)
                    dcs_t = toks(2, 4)
                    qg_t = toks(2, 4)
                    kg_t = toks(2, 4)
                    v_t = toks(2, 4)
                    sog_t = toks(2, 4)
                    kgk_t = toks(4)
                    scm_t = toks(4)
                    Sbf_t = toks(32)
                    S32_t = toks(3)
                    Pp_t = toks(3)
                    ds_whg = [P.dsem() for _ in range(2)]
                    cnt = {"a": 0}
                    A_BANKS = (0, 1, 2)

                    def abank():
                        bk = A_BANKS[cnt["a"] % 3]
                        cnt["a"] += 1
                        return bk

                    MS("dve", rmask[:], 1.0, [rmask_t])
                    MS("dve", rmask[:, 0:512:64], 0.0, [rmask_t])
                    ACT(lnoml[:], oml[:], AF.Ln, [oml_t], [lnoml_t])

                    def load_whg(hh):
                        b = hh % 2
                        DMAG("pool", [(whg_sb[b][:, 0:2048], whg_d[hh, :, 0:2048], (), [whg_t[b]]),
                                      (whg_sb[b][:, 2048:4096], whg_d[hh, :, 2048:4096], (), [whg_t[b]])], ds_whg[b])

                    def stage_a1(hh):
                        b = hh % 2
                        w = whg_sb[b]
                        wt = whg_t[b]

                        def wcol(s_, kc):
                            return w[:, (s_ * 8 + kc) * 128:(s_ * 8 + kc + 1) * 128]

                        for tt in range(4):
                            ts = tsl(tt)
                            fb = abank()
                            for kc in range(8):
                                MM(banks[fb][:], wcol(1, kc), xnT[:, kc, ts], kc == 0, kc == 7, [wt, xn_t[kc][tt]], [bt[fb]])
                            ACT(T1[:], banks[fb][:], AF.Exp, [bt[fb]], [T1_t])
                            if KNOBS["hg_a1_dense"] == 0:
                                yield
                            ACT(T1[:], T1[:], AF.Ln, [T1_t], [T1_t], bias=1.0)
                            if KNOBS["hg_a1_dense"] == 0:
                                yield
                            ACT(T2[:], T1[:], AF.Exp, [T1_t, lnoml_t], [T2_t], scale=-1.0, bias=lnoml[:, hh:hh + 1])
                            if KNOBS["hg_a1_dense"] == 0:
                                yield
                            ACT(T3[:], T2[:], AF.Ln, [T2_t], [T3_t], scale=-1.0, bias=1.0)
                            if KNOBS["hg_a1_dense"] == 0:
                                yield
                            P.op("dve", lambda h: h.tensor_tensor_scan(out=T4[:], data0=rmask[:], data1=T3[:], initial=0.0,
                                                                        op0=ALU.mult, op1=ALU.add),
                                 [rmask_t, T3_t], [T4_t])
                            qb = abank()
                            for kc in range(8):
                                MM(banks[qb][:], wcol(0, kc), xnT[:, kc, ts], kc == 0, kc == 7, [wt, xn_t[kc][tt]], [bt[qb]])
                            if KNOBS["hg_a1_dense"] == 0:
                                yield
                            ACT(T3[:], T4[:], AF.Exp, [T4_t], [T3_t])
                            if KNOBS["hg_a1_dense"] == 0:
                                yield
                            ACT(T1[:], T4[:], AF.Exp, [T4_t], [T1_t], scale=-1.0)
                            if KNOBS["hg_a1_dense"] == 0:
                                yield
                            TT("dve", qgT[b][:, ts], banks[qb][:], T3[:], ALU.mult, [bt[qb], T3_t], [qg_t[b][tt]])
                            CP("dve", dcs[b][:, tt * 8:(tt + 1) * 8], T3[:, 63:512:64], [T3_t], [dcs_t[b][tt]])
                            if KNOBS["hg_a1_dense"] == 0:
                                yield
                            TT(KNOBS["hg_mul"], kgT[b][:, ts], T2[:], T1[:], ALU.mult, [T2_t, T1_t], [kg_t[b][tt]])
                            yield

                    def stage_a2(hh):
                        b = hh % 2
                        w = whg_sb[b]
                        wt = whg_t[b]

                        def wcol(s_, kc):
                            return w[:, (s_ * 8 + kc) * 128:(s_ * 8 + kc + 1) * 128]

                        for i4 in range(4):
                            vb = abank()
                            for q4 in range(4):
                                i = i4 * 4 + q4
                                for kc in range(8):
                                    MM(banks[vb][:, q4 * 128:(q4 + 1) * 128], xnT[:, kc, i * 128:(i + 1) * 128], wcol(2, kc), kc == 0, kc == 7,
                                       [wt, xn_t[kc][i4]], [bt[vb]])
                                if KNOBS["hg_a2_dense"] == 0:
                                    yield
                            CP("dve", v_tok[b][:, i4 * 512:(i4 + 1) * 512], banks[vb][:], [bt[vb]], [v_t[b][i4]])
                            if KNOBS["hg_a2_dense"] == 0:
                                yield
                        for tt in range(4):
                            ts = tsl(tt)
                            gb = abank()
                            for kc in range(8):
                                MM(banks[gb][:], wcol(3, kc), xnT[:, kc, ts], kc == 0, kc == 7, [wt, xn_t[kc][tt]], [bt[gb]])
                            CP("dve", T7[:], banks[gb][:], [bt[gb]], [T7_t])
                            if KNOBS["hg_a2_dense"] == 0:
                                yield
                            ACT(T5[:], T7[:], AF.Exp, [T7_t], [T5_t], scale=-1.0)
                            if KNOBS["hg_a2_dense"] == 0:
                                yield
                            ACT(T5[:], T5[:], AF.Ln, [T5_t], [T5_t], bias=1.0)
                            yield
                            ACT(T5[:], T5[:], AF.Exp, [T5_t], [T5_t], scale=-1.0)
                            if KNOBS["hg_a2_dense"] == 0:
                                yield
                            TT(KNOBS["hg_mul"], sog[b][:, ts], T5[:], T7[:], ALU.mult, [T5_t, T7_t], [sog_t[b][tt]])
                            yield

                    def stage_b(hh):
                        b = hh % 2
                        bankbf3 = banks[3].bitcast(BF16)
                        for i4 in range(4):
                            tt = i4
                            for q4 in range(4):
                                i = i4 * 4 + q4
                                TR(bankbf3[:, q4 * 128:(q4 + 1) * 128], kgT[b][:, i * 128:(i + 1) * 128], ident, [kg_t[b][tt], cst_t], [bt[3]])
                            CP("dve", kg_tok[:, i4 * 512:(i4 + 1) * 512], bankbf3[:, 0:512], [bt[3]], [kgk_t[i4]])
                            if KNOBS["hg_dense"] < 3:
                                yield
                            for q4 in range(4):
                                i = i4 * 4 + q4
                                tl = slice(i * 128, (i + 1) * 128)
                                MM(banks[4][:, q4 * 128:(q4 + 1) * 128], kgT[b][:, tl], qgT[b][:, tl], True, True, [kg_t[b][tt], qg_t[b][tt]], [bt[4]])
                            TT("dve", scm[:, i4 * 512:(i4 + 1) * 512], banks[4][:], mask4, ALU.mult, [bt[4], cst_t], [scm_t[i4]])
                            if KNOBS["hg_dense"] < 3:
                                yield
                            pbs = (5, 6)
                            for q4 in range(4):
                                i = i4 * 4 + q4
                                for half in range(2):
                                    hs = slice(half * 64, (half + 1) * 64)
                                    MM(banks[pbs[half]][:, q4 * 128:(q4 + 1) * 128], kg_tok[hs, i * 128:(i + 1) * 128], v_tok[b][hs, i * 128:(i + 1) * 128],
                                       True, True, [kgk_t[i4], v_t[b][i4]], [bt[pbs[half]]])
                            if KNOBS["hg_dense"] < 3:
                                yield
                            for half in range(2):
                                if KNOBS["hg_pst"] == "act":
                                    ACT(Pst[half][:], banks[pbs[half]][:], AF.Copy, [bt[pbs[half]]], [Pst_t[half]])
                                else:
                                    CP("dve", Pst[half][:], banks[pbs[half]][:], [bt[pbs[half]]], [Pst_t[half]])
                            if KNOBS["hg_dense"] < 3:
                                yield
                            for q4 in range(4):
                                i = i4 * 4 + q4
                                for half in range(2):
                                    c = 2 * i + half
                                    if c >= 31:
                                        continue
                                    pc = Pst[half][:, q4 * 128:(q4 + 1) * 128]
                                    if c == 0:
                                        CP("dve", S32[0][:], pc, [Pst_t[half]], [S32_t[0]])
                                    else:
                                        STT(S32[c % 3][:], S32[(c - 1) % 3][:], dcs[b][:, c - 1:c], pc, ALU.mult, ALU.add,
                                            [S32_t[(c - 1) % 3], Pst_t[half], dcs_t[b][(c - 1) // 8]], [S32_t[c % 3]])
                                    if KNOBS["hg_dense"] < 2:
                                        yield
                                    if KNOBS["hg_snap"] == "pool":
                                        TS("pool", S_bf[:, c, :], S32[c % 3][:], dcs[b][:, c:c + 1], 1.0, ALU.mult, ALU.mult,
                                           [S32_t[c % 3], dcs_t[b][c // 8]], [Sbf_t[c]])
                                    elif KNOBS["hg_snap"] == "act":
                                        ACT(S_bf[:, c, :], S32[c % 3][:], AF.Copy, [S32_t[c % 3], dcs_t[b][c // 8]], [Sbf_t[c]], scale=dcs[b][:, c:c + 1])
                                    else:
                                        TS("dve", S_bf[:, c, :], S32[c % 3][:], dcs[b][:, c:c + 1], None, ALU.mult, ALU.bypass,
                                           [S32_t[c % 3], dcs_t[b][c // 8]], [Sbf_t[c]])
                                    if KNOBS["hg_dense"] < 1:
                                        yield
                            if KNOBS["hg_dense"] >= 1:
                                yield
                        def obank(tt):
                            alt = KNOBS["hg_ob2"] and tt % 2 == 1
                            return (4, 5) if alt else (7, 3)

                        def out_mms(tt):
                            ob = obank(tt)[0]
                            for q4 in range(4):
                                i = tt * 4 + q4
                                oreg = banks[ob][:, q4 * 128:(q4 + 1) * 128]
                                c0, c1 = 2 * i, 2 * i + 1
                                MM(oreg, v_tok[b][:, i * 128:(i + 1) * 128], scm[:, i * 128:(i + 1) * 128], True, False, [v_t[b][tt], scm_t[tt]], [bt[ob]])
                                if c0 >= 1:
                                    MM(banks[ob][:, q4 * 128:q4 * 128 + 64], S_bf[:, c0 - 1, :], qgT[b][:, c0 * 64:(c0 + 1) * 64], False, False,
                                       [Sbf_t[c0 - 1], qg_t[b][tt]], [bt[ob]])
                                MM(banks[ob][:, q4 * 128 + 64:(q4 + 1) * 128], S_bf[:, c1 - 1, :], qgT[b][:, c1 * 64:(c1 + 1) * 64], False, True,
                                   [Sbf_t[c1 - 1], qg_t[b][tt]], [bt[ob]])

                        out_mms(0)
                        for tt in range(4):
                            ts = tsl(tt)
                            ob, nb_ = obank(tt)
                            ACT(osq[:], banks[ob][:], AF.Square, [bt[ob]], [osq_t])
                            yield
                            if tt + 1 < 4 and KNOBS["hg_ob2"]:
                                out_mms(tt + 1)
                            MM(banks[nb_][:], ones_bf[:], osq[:], True, True, [osq_t, ones_t], [bt[nb_]])
                            ACT(lnv[:], banks[nb_][:], AF.Ln, [bt[nb_]], [lnv_t], scale=1.0 / 128, bias=EPS)
                            yield
                            ACT(rstd[:], lnv[:], AF.Exp, [lnv_t], [rstd_t], scale=-0.5)
                            yield
                            TT("dve", t1[:], banks[ob][:], rstd[:], ALU.mult, [bt[ob], rstd_t], [t1_t])
                            yield
                            STT(y_hgT[:, hh, ts], t1[:], vecs[:, 32 + hh:33 + hh], sog[b][:, ts], ALU.mult, ALU.mult,
                                [t1_t, sog_t[b][tt], vecs_t], [yhg_t[hh][tt]])
                            yield
                            if tt + 1 < 4 and not KNOBS["hg_ob2"]:
                                out_mms(tt + 1)

                    def run_rr(gens, weights=None):
                        gens = list(gens)
                        if KNOBS["hg_mode"] == "seq":
                            for g_ in gens[1:] + gens[:1]:
                                for _ in g_:
                                    pass
                            return
                        weights = list(weights) if weights else [1.0] * len(gens)
                        credit = [0.0] * len(gens)
                        alive = [True] * len(gens)
                        while any(alive):
                            for i_, g_ in enumerate(gens):
                                if not alive[i_]:
                                    continue
                                credit[i_] += weights[i_]
                                while credit[i_] >= 1.0 and alive[i_]:
                                    credit[i_] -= 1.0
                                    try:
                                        next(g_)
                                    except StopIteration:
                                        alive[i_] = False

                    load_whg(0)
                    load_whg(1)
                    norm(8, norm_to_xn)
                    run_rr([stage_a1(0), stage_a2(0)])
                    for hh in range(4):
                        gens = [stage_b(hh)]
                        wts = [KNOBS["hg_bw"]]
                        if hh + 1 < 4:
                            if KNOBS["hg_order"] == 0:
                                gens += [stage_a1(hh + 1), stage_a2(hh + 1)]
                            elif KNOBS["hg_order"] == 1:
                                gens = [stage_a1(hh + 1), stage_a2(hh + 1)] + gens
                            else:
                                gens = [stage_a1(hh + 1)] + gens + [stage_a2(hh + 1)]
                            wts = [1.0] * len(gens)
                        if hh + 2 < 4:
                            load_whg(hh + 2)
                        run_rr(gens, wts)
                    P.barrier()
                    if KNOBS["flush"]:
                        P.flush()
                if stage >= 3:
                    merge(wmA_d, y_hgT, yhg_t, "A")

        def attention():
            with ExitStack() as ph:
                y_attT = sbuf(ph, "y_attT", [128, 4, S], BF16)
                yatt_t = toks(4, 4)
                with ExitStack() as wk:
                    watt_sb = [sbuf(wk, "watt_sb%d" % i, [128, 3072], BF16) for i in range(2)]
                    bias_sb = [sbuf(wk, "bias_sb%d" % i, [128, 2, 256], F32) for i in range(2)]
                    qR = [sbuf(wk, "qR%d" % i, [128, S], BF16) for i in range(2)]
                    kR = [[sbuf(wk, "kR%d_%d" % (i, a), [128, S], BF16) for a in range(2)] for i in range(2)]
                    vT = sbuf(wk, "vT", [128, S], BF16)
                    vaug = [sbuf(wk, "vaug%d" % i, [128, 16, 2, 65], BF16) for i in range(2)]
                    acc = sbuf(wk, "acc", [65, 2, S], F32)
                    tmp = [sbuf(wk, "tmp%d" % i, [128, 512], F32) for i in range(2)]
                    pT = [sbuf(wk, "pT%d" % i, [128, 3968], BF16) for i in range(2)]
                    watt_t = toks(2)
                    bias_t = toks(2)
                    qT_t = toks(2, 4)
                    kT_t = toks(2, 4)
                    vT_t = toks(4)
                    kz_t, = toks(1)
                    vaug_t = toks(2, 16)
                    acc_t = toks(2)
                    tmp_t = toks(2)
                    pT_t = toks(2, 8)
                    vone_t = toks(2)
                    ds_watt = [P.dsem() for _ in range(2)]
                    ds_bias = [P.dsem() for _ in range(2)]
                    cnt = {"s": 0, "p": 0, "x": 0, "j": 0}
                    PROJ_BANKS = (0, 1, 4)

                    def pbank():
                        bk = PROJ_BANKS[cnt["j"] % 3]
                        cnt["j"] += 1
                        return bk

                    def load_watt(k):
                        hp, g = k // 3, k % 3
                        b = k % 2
                        DMAG("pool", [(watt_sb[b][:, 0:2048], watt_d[hp, g, :, 0:2048], (), [watt_t[b]]),
                                      (watt_sb[b][:, 2048:3072], watt_d[hp, g, :, 2048:3072], (), [watt_t[b]])], ds_watt[b])

                    def load_bias(k):
                        hp, g = k // 3, k % 3
                        DMA("sp", bias_sb[k % 2][:], biasT_d[hp, g], ds_bias[k % 2], writes=[bias_t[k % 2]])

                    def geom(g):
                        window, d = ATT_GROUPS[g]
                        nb = S // (128 * d)
                        units = [(r, nk) for r in range(d) for nk in range(nb)]
                        nq = [256 if nk + 1 < nb else 128 for (r, nk) in units]
                        col0 = [0] * 16
                        for u in range(1, 16):
                            col0[u] = col0[u - 1] + nq[u - 1]
                        groups = []
                        cur, w_ = [], 0
                        for u in range(16):
                            if w_ + nq[u] > 512:
                                groups.append(cur)
                                cur, w_ = [], 0
                            cur.append(u)
                            w_ += nq[u]
                        groups.append(cur)
                        return d, nb, units, nq, col0, groups

                    def res_out(t, d, tt):
                        L4 = 512 // d
                        return t.rearrange("p (r l) -> p r l", r=d)[:, :, tt * L4:(tt + 1) * L4]

                    def res_in(bank_ap, d):
                        return bank_ap.rearrange("p (l r) -> p r l", r=d)

                    def proj_gen(k):
                        hp, g = k // 3, k % 3
                        b = k % 2
                        if k + 1 < 12:
                            load_watt(k + 1)
                        w = watt_sb[b]
                        wt = watt_t[b]
                        d, nb, units, nq, col0, groups = geom(g)

                        def wcol(s_, kc):
                            return w[:, (s_ * 8 + kc) * 128:(s_ * 8 + kc + 1) * 128]

                        for tt in range(4):
                            ts = tsl(tt)
                            bk = pbank()
                            for kc in range(8):
                                MM(banks[bk][:], wcol(0, kc), xnT[:, kc, ts], kc == 0, kc == 7, [wt, xn_t[kc][tt]], [bt[bk]])
                            ACT(res_out(qR[b][:, :], d, tt), res_in(banks[bk][:, :], d), AF.Copy, [bt[bk]], [qT_t[b][tt]], scale=0.125)
                            yield
                            bk = pbank()
                            for kc in range(8):
                                MM(banks[bk][:], wcol(1, kc), xnT[:, kc, ts], kc == 0, kc == 7, [wt, xn_t[kc][tt]], [bt[bk]])
                            ACT(res_out(kR[b][0][0:64, :], d, tt), res_in(banks[bk][0:64, :], d), AF.Copy, [bt[bk], kz_t], [kT_t[b][tt]])
                            ACT(res_out(kR[b][1][64:128, :], d, tt), res_in(banks[bk][64:128, :], d), AF.Copy, [bt[bk], kz_t], [kT_t[b][tt]])
                            yield
                        for tt in range(4):
                            ts = tsl(tt)
                            bk = pbank()
                            for kc in range(8):
                                MM(banks[bk][:], wcol(2, kc), xnT[:, kc, ts], kc == 0, kc == 7, [wt, xn_t[kc][tt]], [bt[bk]])
                            ACT(res_out(vT[:, :], d, tt), res_in(banks[bk][:, :], d), AF.Copy, [bt[bk]], [vT_t[tt]])
                            yield
                        for u4 in range(4):
                            bk = pbank()
                            bkbf = banks[bk].bitcast(BF16)
                            for q4 in range(4):
                                u = u4 * 4 + q4
                                TR(bkbf[:, q4 * 128:(q4 + 1) * 128], vT[:, u * 128:(u + 1) * 128], ident, vT_t + [cst_t], [bt[bk]])
                            P.op("act", lambda h, bkbf=bkbf, u4=u4, b=b: h.activation(
                                out=vaug[b][:, u4 * 4:(u4 + 1) * 4, :, 0:64],
                                in_=bkbf[:, 0:512].rearrange("p (u a e) -> p u a e", u=4, a=2),
                                func=AF.Copy), [bt[bk]], vaug_t[b][u4 * 4:(u4 + 1) * 4])
                            yield

                    def units_gen(k):
                        hp, g = k // 3, k % 3
                        b = k % 2
                        bsb, bst = bias_sb[b], bias_t[b]
                        d, nb, units, nq, col0, groups = geom(g)
                        gof = {}
                        for m, grp in enumerate(groups):
                            for u in grp:
                                gof[u] = m
                        if g == 0:
                            MS("dve", acc[:], 0.0, acc_t)
                        if k + 1 < 12:
                            load_bias(k + 1)

                        def bias_ap(a, width):
                            base = bsb[:, a, 0:1]
                            pstep = base.ap[0][0]
                            if d == 16:
                                return bass.AP(bsb, base.offset, [[pstep, 128], [0, width // 128], [1, 128]])
                            assert width in (512, 384)
                            if width == 512:
                                return bass.AP(bsb, base.offset, [[pstep, 128], [0, 2], [1, 256]])
                            return None

                        def score_group(a, m):
                            grp = groups[m]
                            sbk = 5 + cnt["s"] % 3
                            x2 = cnt["x"] % 2
                            cnt["s"] += 1
                            cnt["x"] += 1
                            off = 0
                            for u in grp:
                                r, nk = units[u]
                                t0 = nk * 128 * d + r
                                c_ = u * 128
                                ktts = sorted(set([t0 // 512, (t0 + 127 * d) // 512]))
                                qtts = sorted(set(range(t0 // 512, (t0 + (nq[u] - 1) * d) // 512 + 1)))
                                MM(banks[sbk][:, off:off + nq[u]], kR[b][a][:, c_:c_ + 128], qR[b][:, c_:c_ + nq[u]], True, True,
                                   [kT_t[b][t] for t in ktts] + [qT_t[b][t] for t in qtts] + [kz_t], [bt[sbk]])
                                off += nq[u]
                            c0 = col0[grp[0]]
                            bap = bias_ap(a, off)
                            if bap is not None:
                                if d == 16:
                                    o_ap = tmp[x2][:, 0:off].rearrange("p (u i) -> p u i", i=128)
                                    i_ap = banks[sbk][:, 0:off].rearrange("p (u i) -> p u i", i=128)
                                else:
                                    o_ap = tmp[x2][:, 0:off].rearrange("p (u i) -> p u i", i=256)
                                    i_ap = banks[sbk][:, 0:off].rearrange("p (u i) -> p u i", i=256)
                                TT("dve", o_ap, i_ap, bap, ALU.add, [bt[sbk], bst], [tmp_t[x2]])
                            else:
                                TT("dve", tmp[x2][:, 0:256], banks[sbk][:, 0:256], bsb[:, a, 0:256], ALU.add, [bt[sbk], bst], [tmp_t[x2]])
                                TT("dve", tmp[x2][:, 256:384], banks[sbk][:, 256:384], bsb[:, a, 0:128], ALU.add, [bt[sbk], bst], [tmp_t[x2]])
                            ACT(pT[a][:, c0:c0 + off], tmp[x2][:, 0:off], AF.Exp, [tmp_t[x2]], [pT_t[a][m]])

                        def pv_group(a, j):
                            pvb = 2 + cnt["p"] % 2
                            cnt["p"] += 1
                            for q4 in range(4):
                                u = 4 * j + q4
                                r, n = units[u]
                                reg = banks[pvb][0:65, q4 * 128:(q4 + 1) * 128]
                                has_prev = n > 0
                                MM(reg, vaug[b][:, u, a, :], pT[a][:, col0[u]:col0[u] + 128], True, not has_prev,
                                   [vaug_t[b][u], vone_t[b], pT_t[a][gof[u]]], [bt[pvb]])
                                if has_prev:
                                    MM(reg, vaug[b][:, u - 1, a, :], pT[a][:, col0[u - 1] + 128:col0[u - 1] + 256], False, True,
                                       [vaug_t[b][u - 1], vone_t[b], pT_t[a][gof[u - 1]]], [bt[pvb]])
                            if d == 1:
                                dst = acc[:, a, j * 512:(j + 1) * 512]
                                src = banks[pvb][0:65, :]
                            elif d == 4:
                                dst = acc[:, a, j:j + 4 * 511 + 1:4]
                                src = banks[pvb][0:65, :]
                            else:
                                dst = acc[:, a, :].rearrange("p (i r) -> p r i", r=16)[:, 4 * j:4 * j + 4, :]
                                src = banks[pvb][0:65, :].rearrange("p (r i) -> p r i", r=4)
                            TT("dve", dst, dst, src, ALU.add, [bt[pvb], acc_t[a]], [acc_t[a]])

                        sched = []
                        emitted = -1
                        for j in range(4):
                            need = min(gof[4 * j + 3] + 1, len(groups) - 1)
                            for m in range(emitted + 1, need + 1):
                                sched.append(("S", m))
                            emitted = max(emitted, need)
                            sched.append(("P", j))
                        for kind, idx in sched:
                            for a in range(2):
                                if kind == "S":
                                    score_group(a, idx)
                                else:
                                    pv_group(a, idx)
                                yield
                        if g == 2:
                            for a in range(2):
                                for tt in range(4):
                                    ts = tsl(tt)
                                    nbk = 2 + (a * 4 + tt) % 2
                                    MM(banks[nbk][0:64, :], ones_f[64:65, 0:64], acc[64:65, a, ts], True, True, [acc_t[a], ones_t], [bt[nbk]])
                                    ACT(lnv[0:64, :], banks[nbk][0:64, :], AF.Ln, [bt[nbk]], [lnv_t])
                                    ACT(rstd[0:64, :], lnv[0:64, :], AF.Exp, [lnv_t], [rstd_t], scale=-1.0)
                                    TT("dve", y_attT[a * 64:(a + 1) * 64, hp, ts], acc[0:64, a, ts], rstd[0:64, :], ALU.mult, [acc_t[a], rstd_t], [yatt_t[hp][tt]])
                                    yield

                    for b in range(2):
                        MS("dve", vaug[b][:, :, :, 64:65], 1.0, vaug_t[b] + [vone_t[b]])
                        MS("dve", kR[b][0][64:128, :], 0.0, [kz_t])
                        MS("dve", kR[b][1][0:64, :], 0.0, [kz_t])
                    load_watt(0)
                    load_bias(0)
                    for _ in proj_gen(0):
                        pass
                    for k in range(12):
                        gu = units_gen(k)
                        gp = proj_gen(k + 1) if k + 1 < 12 else iter(())
                        alive_u, alive_p = True, True
                        cu, cp_ = 0.0, 0.0
                        while alive_u or alive_p:
                            cu += KNOBS["at_uw"]
                            while alive_u and cu >= 1.0:
                                cu -= 1.0
                                try:
                                    next(gu)
                                except StopIteration:
                                    alive_u = False
                            cp_ += KNOBS["at_pw"]
                            while alive_p and cp_ >= 1.0:
                                cp_ -= 1.0
                                try:
                                    next(gp)
                                except StopIteration:
                                    alive_p = False
                    P.barrier()
                    if KNOBS["flush"]:
                        P.flush()
                if stage >= 0:
                    merge(wmB_d, y_attT, yatt_t, "B")

        if stage >= 0:
            ffn(0, 0)
        if stage >= 2 or stage == -1:
            hgrn2()
        if stage == -2:
            norm(8, norm_to_xn)
        if stage >= 4 or stage == -2:
            attention()
        if stage >= 5:
            ffn(1, 16)
        if stage >= 6:
            norm(24, norm_inplace)
        ds_outs = [ds_out] + [P.dsem() for _ in range(3)]
        ev_out = [DMAG("sp", [(outT_d[c * 128:(c + 1) * 128, tsl(tt)], hT[:, c, tsl(tt)], [hT_t[c][tt]], ()) for c in range(8)], ds_outs[tt])
                  for tt in range(4)]
        P.wait_all("sp", ev_out)
        P.flush()
    return nc


def _const_tables():
    cst = np.zeros((128, 640), np.float32)
    cst[:, 0:128] = np.eye(128, dtype=np.float32)
    s = np.arange(128)[:, None]
    t = np.arange(128)[None, :]
    for q in range(4):
        cst[:, 128 + q * 128:256 + q * 128] = ((s <= t) & (s // 64 == t // 64)).astype(np.float32)
    n_heads = 24
    slopes = np.exp2(-8.0 * np.arange(1, n_heads + 1, dtype=np.float32) / n_heads).astype(np.float32)
    biasT = np.zeros((4, 3, 128, 2, 256), np.float32)
    j = np.arange(128)[:, None].astype(np.float32)
    i = np.arange(128)[None, :].astype(np.float32)
    for hp in range(4):
        for g, (window, d) in enumerate(ATT_GROUPS):
            for a in range(2):
                sl = slopes[g * 8 + hp * 2 + a]
                biasT[hp, g, :, a, 0:128] = np.where(i >= j, -sl * (d * (i - j)), NEG)
                biasT[hp, g, :, a, 128:256] = np.where(i <= j, -sl * (d * (i + 128.0 - j)), NEG)
    return cst, biasT.reshape(4, 3, 128, 512)


def _prep_weights(inp):
    f = lambda a: np.ascontiguousarray(a, dtype=np.float32)
    out = {}
    for tag, kgu, kd in (("1", "ffn1_w_gate_up", "ffn1_w_down"), ("2", "ffn2_w_gate_up", "ffn2_w_down")):
        W = np.asarray(inp[kgu])[0]
        W5 = W.reshape(8, 128, 2, NJ, 128)
        out["wgu" + tag] = f(W5.transpose(3, 1, 2, 0, 4).reshape(NJ, 128, 2048))
        out["wd" + tag] = f(np.asarray(inp[kd])[0].reshape(NJ, 128, 1024))
    Win = np.asarray(inp["w_in"])[0]
    Whg = Win[:, 0:2048].reshape(8, 128, 4, 4, 128)
    out["whg"] = f(Whg.transpose(3, 1, 2, 0, 4).reshape(4, 128, 4096))
    Watt = Win[:, 2048:6656].reshape(8, 128, 3, 3, 4, 128)
    out["watt"] = f(Watt.transpose(4, 2, 1, 3, 0, 5).reshape(4, 3, 128, 3072))
    for tag, c0, kb in (("A", 6656, "w_branch_hg"), ("B", 7680, "w_branch_att")):
        Wg = Win[:, c0:c0 + 1024].reshape(8, 128, 8, 128)
        Wb = np.asarray(inp[kb])[0].reshape(4, 128, 8, 128)
        wm = np.concatenate([Wg.transpose(2, 1, 0, 3).reshape(8, 128, 1024), Wb.transpose(2, 1, 0, 3).reshape(8, 128, 512)], axis=2)
        out["wm" + tag] = f(wm)
    Wo = np.asarray(inp["w_out"])[0].reshape(2, 4, 128, 1024)
    out["wo"] = f(Wo.transpose(0, 2, 1, 3).reshape(2, 128, 4096))
    vecs = np.zeros((128, 48), np.float32)
    vecs[:, 0:8] = np.asarray(inp["ffn1_norm"])[0].reshape(8, 128).T
    vecs[:, 8:16] = np.asarray(inp["mix_norm"])[0].reshape(8, 128).T
    vecs[:, 16:24] = np.asarray(inp["ffn2_norm"])[0].reshape(8, 128).T
    vecs[:, 24:32] = np.asarray(inp["final_norm"]).reshape(8, 128).T
    vecs[:, 32:36] = np.asarray(inp["hg_out_norm"])[0].reshape(4, 128).T
    lbs = np.asarray(inp["hg_lower_bounds"])
    vecs[:, 36:40] = lbs[0].reshape(4, 128).T
    vecs[:, 40:44] = lbs[1].reshape(4, 128).T
    out["vecs"] = vecs
    cst, biasT = _const_tables()
    out["cst"] = cst
    out["biasT"] = biasT
    return out


_NC_CACHE = {}


def _get_nc(stage=99):
    if stage not in _NC_CACHE:
        _NC_CACHE[stage] = build_program(stage)
    return _NC_CACHE[stage]


def kernel(**inputs):
    x = np.asarray(inputs["x"], dtype=np.float32)
    shared = _prep_weights(inputs)
    nc = _get_nc()
    in_maps = []
    for b in range(NCORES):
        m = dict(shared)
        m["xT"] = np.ascontiguousarray(x[b].T)
        in_maps.append(m)
    res = run_bass_kernel_spmd(nc, in_maps, core_ids=list(range(NCORES)))
    out = np.stack([np.ascontiguousarray(r["outT"].T) for r in res.results], axis=0)
    return out.astype(np.float32)
```

```python
import bisect
from contextlib import ExitStack

import numpy as np
import concourse.bass as bass
import concourse.mybir as mybir
from concourse.bass_utils import run_bass_kernel_spmd

F32 = mybir.dt.float32
BF16 = mybir.dt.bfloat16
AF = mybir.ActivationFunctionType
ALU = mybir.AluOpType

S = 2048
D = 1024
DFF = 2816
NJ = DFF // 128
EPS = 1e-6
NCORES = 8
ATT_GROUPS = ((128, 1), (512, 4), (2048, 16))
NEG = -30000.0
FFN_GROUPS = ((0, 1, 2, 3, 4, 5, 6, 7), (8, 9, 10, 11, 12, 13, 14), (15, 16, 17, 18, 19, 20, 21))
GMAX = 8
KNOBS = {"hg_mode": "rr", "hg_bw": 1.0, "hg_snap": "pool", "hg_dense": 2, "hg_a1_dense": 0, "hg_a2_dense": 0, "hg_pst": "act", "hg_order": 0, "hg_ob2": 1, "flush": 0, "hg_mul": "pool", "at_uw": 2.0, "at_pw": 1.0}


class Tok:
    __slots__ = ("w", "r")

    def __init__(self):
        self.w = None
        self.r = {}


def toks(*shape):
    if len(shape) == 1:
        return [Tok() for _ in range(shape[0])]
    return [toks(*shape[1:]) for _ in range(shape[0])]


class Group:
    def __init__(self, kids):
        self.kids = kids


def _expand(ts):
    out = []
    for t in ts:
        if isinstance(t, Group):
            out.extend(t.kids)
        else:
            out.append(t)
    return out


class DSem:
    def __init__(self, h):
        self.h = h
        self.count = 0


class Prog:
    ENG = ("pe", "act", "dve", "pool", "sp")

    def __init__(self, nc, stack):
        self.nc = nc
        self.stack = stack
        self.ops = {e: [] for e in self.ENG}
        self.needed = {e: set() for e in self.ENG}
        self.need_sorted = {e: [] for e in self.ENG}
        self.evval = {e: {} for e in self.ENG}
        self.ecount = {e: 0 for e in self.ENG}
        self.flushed = {e: 0 for e in self.ENG}
        self.waited = {e: {} for e in self.ENG}
        self.esem = {e: stack.enter_context(nc.semaphore("es_" + e)) for e in self.ENG}
        self.nds = 0

    def dsem(self):
        self.nds += 1
        return DSem(self.stack.enter_context(self.nc.semaphore("ds%d" % self.nds)))

    def _waits(self, eng, reads, writes):
        ws = []
        for t in reads:
            if t.w is not None:
                ws.append(t.w)
        for t in writes:
            if t.w is not None:
                ws.append(t.w)
            for k, ev in t.r.items():
                if k == eng and eng == "pe":
                    continue
                ws.append(ev)
        out = []
        for ev in ws:
            if ev[0] == "E":
                if ev[1] == eng and eng == "pe":
                    continue
                if ev[2] >= self.flushed[ev[1]]:
                    self.needed[ev[1]].add(ev[2])
            out.append(ev)
        return out

    def _mark(self, ev, key, reads, writes):
        for t in reads:
            t.r[key] = ev
        for t in writes:
            t.w = ev
            t.r = {}

    def op(self, eng, fn, reads=(), writes=()):
        reads, writes = _expand(reads), _expand(writes)
        waits = self._waits(eng, reads, writes)
        ev = ("E", eng, len(self.ops[eng]))
        self.ops[eng].append(dict(waits=waits, fn=fn, kind="c"))
        self._mark(ev, eng, reads, writes)
        return ev

    def dma(self, q, fn, ds, reads=(), writes=()):
        reads, writes = _expand(reads), _expand(writes)
        waits = self._waits("dma", reads, writes)
        ds.count += 16
        ev = ("D", ds, ds.count)
        self.ops[q].append(dict(waits=waits, fn=fn, kind="d", ds=ds))
        self._mark(ev, ("D", id(ds)), reads, writes)
        return ev

    def dma_group(self, q, items, ds):
        final = ("D", ds, ds.count + 16 * len(items))
        prepared = []
        for fn, reads, writes in items:
            reads, writes = _expand(reads), _expand(writes)
            prepared.append((fn, reads, writes, self._waits("dma", reads, writes)))
        for fn, reads, writes, waits in prepared:
            ds.count += 16
            self.ops[q].append(dict(waits=waits, fn=fn, kind="d", ds=ds))
            self._mark(final, ("D", id(ds)), reads, writes)
        return final

    def wait_all(self, eng, evs):
        for ev in evs:
            if ev[0] == "E" and ev[2] >= self.flushed[ev[1]]:
                self.needed[ev[1]].add(ev[2])
        self.ops[eng].append(dict(waits=list(evs), fn=None, kind="w"))

    def barrier(self):
        last = {}
        for e in self.ENG:
            for i in range(len(self.ops[e]) - 1, -1, -1):
                if self.ops[e][i]["kind"] == "c":
                    last[e] = ("E", e, i)
                    break
        for e in self.ENG:
            self.wait_all(e, [ev for k, ev in last.items() if k != e])

    def _resolve(self, ev):
        e, idx = ev[1], ev[2]
        ns = self.need_sorted[e]
        k = bisect.bisect_left(ns, idx)
        return self.evval[e][ns[k]]

    def flush(self):
        for e in self.ENG:
            n = len(self.ops[e])
            for i in range(n - 1, self.flushed[e] - 1, -1):
                if self.ops[e][i]["kind"] == "c":
                    self.needed[e].add(i)
                    break
            c = self.ecount[e]
            for i in range(self.flushed[e], n):
                if i in self.needed[e]:
                    c += 1
                    self.evval[e][i] = c
                    self.need_sorted[e].append(i)
            self.ecount[e] = c
        prog = self

        def run(e, h):
            waited = prog.waited[e]
            for i in range(prog.flushed[e], len(prog.ops[e])):
                o = prog.ops[e][i]
                for ev in o["waits"]:
                    if ev[0] == "E":
                        sem = prog.esem[ev[1]]
                        val = prog._resolve(ev)
                        key = ("E", ev[1])
                    else:
                        sem = ev[1].h
                        val = ev[2]
                        key = ("D", id(ev[1]))
                    if waited.get(key, 0) >= val:
                        continue
                    waited[key] = val
                    h.wait_ge(sem, val)
                if o["fn"] is None:
                    continue
                inst = o["fn"](h)
                if o["kind"] == "d":
                    inst.then_inc(o["ds"].h, 16)
                elif i in prog.needed[e]:
                    inst.then_inc(prog.esem[e], 1)

        with self.nc.Block() as block:
            @block.tensor
            def _(h):
                run("pe", h)

            @block.scalar
            def _(h):
                run("act", h)

            @block.vector
            def _(h):
                run("dve", h)

            @block.gpsimd
            def _(h):
                run("pool", h)

            @block.sync
            def _(h):
                run("sp", h)

        for e in self.ENG:
            self.flushed[e] = len(self.ops[e])


def build_program(stage=99):
    nc = bass.Bass("TRN2", target_bir_lowering=False)

    def dram(name, shape, kind="ExternalInput"):
        return nc.dram_tensor(name, list(shape), F32, kind=kind).ap()

    xT_d = dram("xT", [D, S])
    wgu_d = [dram("wgu1", [NJ, 128, 2048]), dram("wgu2", [NJ, 128, 2048])]
    wd_d = [dram("wd1", [NJ, 128, 1024]), dram("wd2", [NJ, 128, 1024])]
    whg_d = dram("whg", [4, 128, 4096])
    watt_d = dram("watt", [4, 3, 128, 3072])
    wmA_d = dram("wmA", [8, 128, 1536])
    wmB_d = dram("wmB", [8, 128, 1536])
    wo_d = dram("wo", [2, 128, 4096])
    vecs_d = dram("vecs", [128, 48])
    biasT_d = dram("biasT", [4, 3, 128, 512])
    cst_d = dram("cst", [128, 640])
    outT_d = dram("outT", [D, S], kind="ExternalOutput")

    with ExitStack() as st:
        P = Prog(nc, st)

        def sbuf(stack, name, shape, dt):
            return stack.enter_context(nc.sbuf_tensor("s_" + name, list(shape), dt))

        def MM(out, lhsT, rhs, start, stop, reads, writes):
            return P.op("pe", lambda h: h.matmul(out, lhsT=lhsT, rhs=rhs, start=start, stop=stop), reads, writes)

        def TR(out, in_, ident, reads, writes):
            return P.op("pe", lambda h: h.transpose(out, in_, ident), reads, writes)

        def ACT(out, in_, func, reads, writes, **kw):
            return P.op("act", lambda h: h.activation(out=out, in_=in_, func=func, **kw), reads, writes)

        def TT(eng, out, in0, in1, op, reads, writes):
            return P.op(eng, lambda h: h.tensor_tensor(out=out, in0=in0, in1=in1, op=op), reads, writes)

        def STT(out, in0, scalar, in1, op0, op1, reads, writes):
            return P.op("dve", lambda h: h.scalar_tensor_tensor(out=out, in0=in0, scalar=scalar, in1=in1, op0=op0, op1=op1), reads, writes)

        def TS(eng, out, in0, s1, s2, op0, op1, reads, writes):
            return P.op(eng, lambda h: h.tensor_scalar(out=out, in0=in0, scalar1=s1, scalar2=s2, op0=op0, op1=op1), reads, writes)

        def CP(eng, out, in_, reads, writes):
            return P.op(eng, lambda h: h.tensor_copy(out=out, in_=in_), reads, writes)

        def MS(eng, ap, val, writes):
            return P.op(eng, lambda h: h.memset(ap, val), (), writes)

        def DMA(q, out, in_, ds, reads=(), writes=()):
            return P.dma(q, lambda h: h.dma_start(out=out, in_=in_), ds, reads, writes)

        def _dfn(out, in_):
            return lambda h: h.dma_start(out=out, in_=in_)

        def DMAG(q, items, ds):
            return P.dma_group(q, [(_dfn(o, i), r, w) for (o, i, r, w) in items], ds)

        hT = sbuf(st, "hT", [128, 8, S], F32)
        xnT = sbuf(st, "xnT", [128, 8, S], BF16)
        vecs = sbuf(st, "vecs", [128, 48], F32)
        cstb = sbuf(st, "cstb", [128, 640], BF16)
        ones_bf = sbuf(st, "ones_bf", [128, 128], BF16)
        ones_f = sbuf(st, "ones_f", [128, 64], F32)
        oml = sbuf(st, "oml", [128, 4], F32)
        lbd = sbuf(st, "lbd", [128, 4], F32)
        sq = [sbuf(st, "sq%d" % i, [128, 512], BF16) for i in range(2)]
        lnv = sbuf(st, "lnv", [128, 512], F32)
        rstd = sbuf(st, "rstd", [128, 512], F32)
        banks = [st.enter_context(nc.psum_tensor("bank%d" % i, [128, 512], F32)) for i in range(8)]

        hT_t = toks(8, 4)
        xn_t = toks(8, 4)
        vecs_t, cst_t, ones_t, rmask_t, oml_t, lbd_t, lnv_t, rstd_t = toks(8)
        sq_t = toks(2)
        bq = toks(8, 4)
        bt = [Group(bq[i]) for i in range(8)]
        ident = cstb[:, 0:128]
        mask4 = cstb[:, 128:640]

        def tsl(tt):
            return slice(tt * 512, (tt + 1) * 512)

        ds_x = P.dsem()
        ds_c = P.dsem()
        ds_v = P.dsem()
        ds_out = P.dsem()

        DMA("sp", vecs[:], vecs_d, ds_v, writes=[vecs_t])
        ds_xs = [ds_x] + [P.dsem() for _ in range(3)]
        for tt in range(4):
            dep = [hT_t[0][tt - 1]] if tt > 0 else []
            DMAG("sp", [(hT[:, c, tsl(tt)], xT_d[c * 128:(c + 1) * 128, tsl(tt)], dep, [hT_t[c][tt]]) for c in range(8)], ds_xs[tt])
        DMA("pool", cstb[:], cst_d, ds_c, writes=[cst_t])
        MS("dve", ones_bf[:], 1.0, [ones_t])
        MS("dve", ones_f[:], 1.0, [ones_t])
        TT("dve", lbd[:], vecs[:, 40:44], vecs[:, 36:40], ALU.subtract, [vecs_t], [lbd_t])
        ACT(oml[:], lbd[:], AF.Sigmoid, [lbd_t], [oml_t])

        def norm(gcol, out_fn, nfeat=D):
            for tt in range(4):
                norm_tile(tt, gcol, out_fn, nfeat)

        def norm_tile(tt, gcol, out_fn, nfeat=D):
            if True:
                ts = tsl(tt)
                sb_, rb_ = (7, 6) if tt % 2 == 0 else (4, 5)
                for c in range(8):
                    ACT(sq[c % 2][:], hT[:, c, ts], AF.Square, [hT_t[c][tt]], [sq_t[c % 2]])
                    MM(banks[sb_][:], ones_bf[:], sq[c % 2][:], c == 0, c == 7, [sq_t[c % 2], ones_t], [bt[sb_]])
                ACT(lnv[:], banks[sb_][:], AF.Ln, [bt[sb_]], [lnv_t], scale=1.0 / nfeat, bias=EPS)
                ACT(banks[rb_][:], lnv[:], AF.Exp, [lnv_t], [bt[rb_]], scale=-0.5)
                for c in range(8):
                    out_fn(c, tt, ts, gcol, rb_)

        def norm_to_xn(c, tt, ts, gcol, rb_):
            STT(xnT[:, c, ts], hT[:, c, ts], vecs[:, gcol + c:gcol + c + 1], banks[rb_][:], ALU.mult, ALU.mult,
                [hT_t[c][tt], bt[rb_], vecs_t], [xn_t[c][tt]])

        def norm_inplace(c, tt, ts, gcol, rb_):
            STT(hT[:, c, ts], hT[:, c, ts], vecs[:, gcol + c:gcol + c + 1], banks[rb_][:], ALU.mult, ALU.mult,
                [hT_t[c][tt], bt[rb_], vecs_t], [hT_t[c][tt]])

        def ffn(which, gcol):
            wgu, wd = wgu_d[which], wd_d[which]
            with ExitStack() as ph:
                actT = sbuf(ph, "actT%d" % which, [128, GMAX, S], BF16)
                wgu_sb = [sbuf(ph, "wgu_sb%d_%d" % (which, i), [128, 2048], BF16) for i in range(3)]
                wd_sb = sbuf(ph, "wd_sb%d" % which, [128, GMAX, 1024], BF16)
                sg = [sbuf(ph, "sg%d_%d" % (which, i), [128, 512], F32) for i in range(2)]
                act_t = toks(GMAX, 4)
                wgu_t = toks(3)
                wd_t = toks(GMAX)
                sg_t = toks(2)
                ds_wgu = [P.dsem() for _ in range(3)]
                ds_wd = P.dsem()

                def load_wgu(j):
                    dep = [hT_t[0][0]] if (which == 0 and j < 3) else []
                    DMA("pool", wgu_sb[j % 3][:], wgu[j], ds_wgu[j % 3], reads=dep, writes=[wgu_t[j % 3]])

                load_wgu(0)
                load_wgu(1)
                for grp in FFN_GROUPS:
                    for jj, j in enumerate(grp):
                        if j + 2 < NJ:
                            load_wgu(j + 2)
                        if jj == 1:
                            DMAG("pool", [(wd_sb[:, jj2, :], wd[j2], (), [wd_t[jj2]]) for jj2, j2 in enumerate(grp)], ds_wd)
                        w = wgu_sb[j % 3]
                        for tt in range(4):
                            if j == 0:
                                norm_tile(tt, gcol, norm_to_xn)
                            ts = tsl(tt)
                            gb, ub = tt % 2, 2 + tt % 2
                            for kc in range(8):
                                MM(banks[gb][:], w[:, kc * 128:(kc + 1) * 128], xnT[:, kc, ts], kc == 0, kc == 7,
                                   [wgu_t[j % 3], xn_t[kc][tt]], [bt[gb]])
                            for kc in range(8):
                                MM(banks[ub][:], w[:, (8 + kc) * 128:(9 + kc) * 128], xnT[:, kc, ts], kc == 0, kc == 7,
                                   [wgu_t[j % 3], xn_t[kc][tt]], [bt[ub]])
                            if j == 0:
                                ACT(sg[tt % 2][:], banks[gb][:], AF.Exp, [bt[gb]], [sg_t[tt % 2]], scale=-1.0)
                                ACT(sg[tt % 2][:], sg[tt % 2][:], AF.Ln, [sg_t[tt % 2]], [sg_t[tt % 2]], bias=1.0)
                                ACT(sg[tt % 2][:], sg[tt % 2][:], AF.Exp, [sg_t[tt % 2]], [sg_t[tt % 2]], scale=-1.0)
                                TT("dve", sg[tt % 2][:], sg[tt % 2][:], banks[gb][:], ALU.mult, [sg_t[tt % 2], bt[gb]], [sg_t[tt % 2]])
                            else:
                                ACT(sg[tt % 2][:], banks[gb][:], AF.Silu, [bt[gb]], [sg_t[tt % 2]])
                            TT("dve", actT[:, jj, ts], sg[tt % 2][:], banks[ub][:], ALU.mult, [sg_t[tt % 2], bt[ub]], [act_t[jj][tt]])
                    n = len(grp)
                    for m in range(8):
                        for tt in range(4):
                            ts = tsl(tt)
                            db = 4 + (m * 4 + tt) % 2
                            for jj in range(n):
                                MM(banks[db][:], wd_sb[:, jj, m * 128:(m + 1) * 128], actT[:, jj, ts], jj == 0, jj == n - 1,
                                   [wd_t[jj], act_t[jj][tt]], [bt[db]])
                            STT(hT[:, m, ts], banks[db][:], 0.5, hT[:, m, ts], ALU.mult, ALU.add,
                                [bt[db], hT_t[m][tt]], [hT_t[m][tt]])
                P.barrier()
                if KNOBS["flush"]:
                    P.flush()

        def merge(wm_d, yT, y_t, tag):
            with ExitStack() as ph:
                wm_sb = [sbuf(ph, "wm_sb%s%d" % (tag, i), [128, 1536], BF16) for i in range(2)]
                wo_sb = sbuf(ph, "wo_sb" + tag, [128, 4096], BF16)
                sA = [sbuf(ph, "sA%s%d" % (tag, i), [128, 512], F32) for i in range(2)]
                mT = sbuf(ph, "mT" + tag, [128, 4, S], BF16)
                wm_t = toks(2)
                wo_t, = toks(1)
                sA_t = toks(2)
                mT_t = toks(4, 4)
                ds_wm = [P.dsem() for _ in range(2)]
                ds_wo = P.dsem()

                def load_wm(dm):
                    DMA("pool", wm_sb[dm % 2][:], wm_d[dm], ds_wm[dm % 2], writes=[wm_t[dm % 2]])

                load_wm(0)
                for dmg in range(2):
                    for dmi in range(4):
                        dm = dmg * 4 + dmi
                        if dm + 1 < 8:
                            load_wm(dm + 1)
                        if dmi == 1:
                            DMAG("pool", [(wo_sb[:, 0:2048], wo_d[dmg, :, 0:2048], (), [wo_t]),
                                          (wo_sb[:, 2048:4096], wo_d[dmg, :, 2048:4096], (), [wo_t])], ds_wo)
                        w = wm_sb[dm % 2]
                        for tt in range(4):
                            ts = tsl(tt)
                            gb, bb = tt % 2, 2 + tt % 2
                            for kc in range(8):
                                MM(banks[gb][:], w[:, kc * 128:(kc + 1) * 128], xnT[:, kc, ts], kc == 0, kc == 7,
                                   [wm_t[dm % 2], xn_t[kc][tt]], [bt[gb]])
                            for c in range(4):
                                MM(banks[bb][:], w[:, (8 + c) * 128:(9 + c) * 128], yT[:, c, ts], c == 0, c == 3,
                                   [wm_t[dm % 2], y_t[c][tt]], [bt[bb]])
                            ACT(sA[tt % 2][:], banks[gb][:], AF.Sigmoid, [bt[gb]], [sA_t[tt % 2]])
                            TT("dve", mT[:, dmi, ts], sA[tt % 2][:], banks[bb][:], ALU.mult, [sA_t[tt % 2], bt[bb]], [mT_t[dmi][tt]])
                    for m in range(8):
                        for tt in range(4):
                            ts = tsl(tt)
                            db = 4 + (m * 4 + tt) % 2
                            for dmi in range(4):
                                MM(banks[db][:], wo_sb[:, dmi * 1024 + m * 128: dmi * 1024 + (m + 1) * 128], mT[:, dmi, ts],
                                   dmi == 0, dmi == 3, [wo_t, mT_t[dmi][tt]], [bt[db]])
                            TT("dve", hT[:, m, ts], banks[db][:], hT[:, m, ts], ALU.add, [bt[db], hT_t[m][tt]], [hT_t[m][tt]])
                P.barrier()
                if KNOBS["flush"]:
                    P.flush()

        def hgrn2():
            with ExitStack() as ph:
                y_hgT = sbuf(ph, "y_hgT", [128, 4, S], BF16)
                yhg_t = toks(4, 4)
                with ExitStack() as wk:
                    whg_sb = [sbuf(wk, "whg_sb%d" % i, [128, 4096], BF16) for i in range(2)]
                    rmask = sbuf(wk, "rmask", [128, 512], F32)
                    lnoml = sbuf(wk, "lnoml", [128, 4], F32)
                    T1 = sbuf(wk, "T1", [128, 512], F32)
                    T2 = sbuf(wk, "T2", [128, 512], F32)
                    T3 = sbuf(wk, "T3", [128, 512], F32)
                    T4 = sbuf(wk, "T4", [128, 512], F32)
                    T5 = sbuf(wk, "T5", [128, 512], F32)
                    T7 = sbuf(wk, "T7", [128, 512], F32)
                    dcs = [sbuf(wk, "dcs%d" % i, [128, 32], F32) for i in range(2)]
                    qgT = [sbuf(wk, "qgT%d" % i, [128, S], BF16) for i in range(2)]
                    kgT = [sbuf(wk, "kgT%d" % i, [128, S], BF16) for i in range(2)]
                    v_tok = [sbuf(wk, "v_tok%d" % i, [128, S], BF16) for i in range(2)]
                    sog = [sbuf(wk, "sog%d" % i, [128, S], BF16) for i in range(2)]
                    kg_tok = sbuf(wk, "kg_tok", [128, S], BF16)
                    scm = sbuf(wk, "scm", [128, S], BF16)
                    S_bf = sbuf(wk, "S_bf", [128, 32, 128], BF16)
                    S32 = [sbuf(wk, "S32_%d" % i, [128, 128], F32) for i in range(3)]
                    Pp = [sbuf(wk, "Pp%d" % i, [128, 128], F32) for i in range(3)]
                    Pst = [sbuf(wk, "Pst%d" % i, [128, 512], F32) for i in range(2)]
                    osq, t1 = sq[0], lnv
                    whg_t = toks(2)
                    T1_t, T2_t, T3_t, T4_t, T5_t, T7_t, lnoml_t = toks(7)
                    osq_t, t1_t = sq_t[0], lnv_t
                    Pst_t = toks(2)
                    dcs_t = toks(2, 4)
                    qg_t = toks(2, 4)
                    kg_t = toks(2, 4)
                    v_t = toks(2, 4)
                    sog_t = toks(2, 4)
                    kgk_t = toks(4)
                    scm_t = toks(4)
                    Sbf_t = toks(32)
                    S32_t = toks(3)
                    Pp_t = toks(3)
                    ds_whg = [P.dsem() for _ in range(2)]
                    cnt = {"a": 0}
                    A_BANKS = (0, 1, 2)

                    def abank():
                        bk = A_BANKS[cnt["a"] % 3]
                        cnt["a"] += 1
                        return bk

                    MS("dve", rmask[:], 1.0, [rmask_t])
                    MS("dve", rmask[:, 0:512:64], 0.0, [rmask_t])
                    ACT(lnoml[:], oml[:], AF.Ln, [oml_t], [lnoml_t])

                    def load_whg(hh):
                        b = hh % 2
                        DMAG("pool", [(whg_sb[b][:, 0:2048], whg_d[hh, :, 0:2048], (), [whg_t[b]]),
                                      (whg_sb[b][:, 2048:4096], whg_d[hh, :, 2048:4096], (), [whg_t[b]])], ds_whg[b])

                    def stage_a1(hh):
                        b = hh % 2
                        w = whg_sb[b]
                        wt = whg_t[b]

                        def wcol(s_, kc):
                            return w[:, (s_ * 8 + kc) * 128:(s_ * 8 + kc + 1) * 128]

                        for tt in range(4):
                            ts = tsl(tt)
                            fb = abank()
                            for kc in range(8):
                                MM(banks[fb][:], wcol(1, kc), xnT[:, kc, ts], kc == 0, kc == 7, [wt, xn_t[kc][tt]], [bt[fb]])
                            ACT(T1[:], banks[fb][:], AF.Exp, [bt[fb]], [T1_t])
                            if KNOBS["hg_a1_dense"] == 0:
                                yield
                            ACT(T1[:], T1[:], AF.Ln, [T1_t], [T1_t], bias=1.0)
                            if KNOBS["hg_a1_dense"] == 0:
                                yield
                            ACT(T2[:], T1[:], AF.Exp, [T1_t, lnoml_t], [T2_t], scale=-1.0, bias=lnoml[:, hh:hh + 1])
                            if KNOBS["hg_a1_dense"] == 0:
                                yield
                            ACT(T3[:], T2[:], AF.Ln, [T2_t], [T3_t], scale=-1.0, bias=1.0)
                            if KNOBS["hg_a1_dense"] == 0:
                                yield
                            P.op("dve", lambda h: h.tensor_tensor_scan(out=T4[:], data0=rmask[:], data1=T3[:], initial=0.0,
                                                                        op0=ALU.mult, op1=ALU.add),
                                 [rmask_t, T3_t], [T4_t])
                            qb = abank()
                            for kc in range(8):
                                MM(banks[qb][:], wcol(0, kc), xnT[:, kc, ts], kc == 0, kc == 7, [wt, xn_t[kc][tt]], [bt[qb]])
                            if KNOBS["hg_a1_dense"] == 0:
                                yield
                            ACT(T3[:], T4[:], AF.Exp, [T4_t], [T3_t])
                            if KNOBS["hg_a1_dense"] == 0:
                                yield
                            ACT(T1[:], T4[:], AF.Exp, [T4_t], [T1_t], scale=-1.0)
                            if KNOBS["hg_a1_dense"] == 0:
                                yield
                            TT("dve", qgT[b][:, ts], banks[qb][:], T3[:], ALU.mult, [bt[qb], T3_t], [qg_t[b][tt]])
                            CP("dve", dcs[b][:, tt * 8:(tt + 1) * 8], T3[:, 63:512:64], [T3_t], [dcs_t[b][tt]])
                            if KNOBS["hg_a1_dense"] == 0:
                                yield
                            TT(KNOBS["hg_mul"], kgT[b][:, ts], T2[:], T1[:], ALU.mult, [T2_t, T1_t], [kg_t[b][tt]])
                            yield

                    def stage_a2(hh):
                        b = hh % 2
                        w = whg_sb[b]
                        wt = whg_t[b]

                        def wcol(s_, kc):
                            return w[:, (s_ * 8 + kc) * 128:(s_ * 8 + kc + 1) * 128]

                        for i4 in range(4):
                            vb = abank()
                            for q4 in range(4):
                                i = i4 * 4 + q4
                                for kc in range(8):
                                    MM(banks[vb][:, q4 * 128:(q4 + 1) * 128], xnT[:, kc, i * 128:(i + 1) * 128], wcol(2, kc), kc == 0, kc == 7,
                                       [wt, xn_t[kc][i4]], [bt[vb]])
                                if KNOBS["hg_a2_dense"] == 0:
                                    yield
                            CP("dve", v_tok[b][:, i4 * 512:(i4 + 1) * 512], banks[vb][:], [bt[vb]], [v_t[b][i4]])
                            if KNOBS["hg_a2_dense"] == 0:
                                yield
                        for tt in range(4):
                            ts = tsl(tt)
                            gb = abank()
                            for kc in range(8):
                                MM(banks[gb][:], wcol(3, kc), xnT[:, kc, ts], kc == 0, kc == 7, [wt, xn_t[kc][tt]], [bt[gb]])
                            CP("dve", T7[:], banks[gb][:], [bt[gb]], [T7_t])
                            if KNOBS["hg_a2_dense"] == 0:
                                yield
                            ACT(T5[:], T7[:], AF.Exp, [T7_t], [T5_t], scale=-1.0)
                            if KNOBS["hg_a2_dense"] == 0:
                                yield
                            ACT(T5[:], T5[:], AF.Ln, [T5_t], [T5_t], bias=1.0)
                            yield
                            ACT(T5[:], T5[:], AF.Exp, [T5_t], [T5_t], scale=-1.0)
                            if KNOBS["hg_a2_dense"] == 0:
                                yield
                            TT(KNOBS["hg_mul"], sog[b][:, ts], T5[:], T7[:], ALU.mult, [T5_t, T7_t], [sog_t[b][tt]])
                            yield

                    def stage_b(hh):
                        b = hh % 2
                        bankbf3 = banks[3].bitcast(BF16)
                        for i4 in range(4):
                            tt = i4
                            for q4 in range(4):
                                i = i4 * 4 + q4
                                TR(bankbf3[:, q4 * 128:(q4 + 1) * 128], kgT[b][:, i * 128:(i + 1) * 128], ident, [kg_t[b][tt], cst_t], [bt[3]])
                            CP("dve", kg_tok[:, i4 * 512:(i4 + 1) * 512], bankbf3[:, 0:512], [bt[3]], [kgk_t[i4]])
                            if KNOBS["hg_dense"] < 3:
                                yield
                            for q4 in range(4):
                                i = i4 * 4 + q4
                                tl = slice(i * 128, (i + 1) * 128)
                                MM(banks[4][:, q4 * 128:(q4 + 1) * 128], kgT[b][:, tl], qgT[b][:, tl], True, True, [kg_t[b][tt], qg_t[b][tt]], [bt[4]])
                            TT("dve", scm[:, i4 * 512:(i4 + 1) * 512], banks[4][:], mask4, ALU.mult, [bt[4], cst_t], [scm_t[i4]])
                            if KNOBS["hg_dense"] < 3:
                                yield
                            pbs = (5, 6)
                            for q4 in range(4):
                                i = i4 * 4 + q4
                                for half in range(2):
                                    hs = slice(half * 64, (half + 1) * 64)
                                    MM(banks[pbs[half]][:, q4 * 128:(q4 + 1) * 128], kg_tok[hs, i * 128:(i + 1) * 128], v_tok[b][hs, i * 128:(i + 1) * 128],
                                       True, True, [kgk_t[i4], v_t[b][i4]], [bt[pbs[half]]])
                            if KNOBS["hg_dense"] < 3:
                                yield
                            for half in range(2):
                                if KNOBS["hg_pst"] == "act":
                                    ACT(Pst[half][:], banks[pbs[half]][:], AF.Copy, [bt[pbs[half]]], [Pst_t[half]])
                                else:
                                    CP("dve", Pst[half][:], banks[pbs[half]][:], [bt[pbs[half]]], [Pst_t[half]])
                            if KNOBS["hg_dense"] < 3:
                                yield
                            for q4 in range(4):
                                i = i4 * 4 + q4
                                for half in range(2):
                                    c = 2 * i + half
                                    if c >= 31:
                                        continue
                                    pc = Pst[half][:, q4 * 128:(q4 + 1) * 128]
                                    if c == 0:
                                        CP("dve", S32[0][:], pc, [Pst_t[half]], [S32_t[0]])
                                    else:
                                        STT(S32[c % 3][:], S32[(c - 1) % 3][:], dcs[b][:, c - 1:c], pc, ALU.mult, ALU.add,
                                            [S32_t[(c - 1) % 3], Pst_t[half], dcs_t[b][(c - 1) // 8]], [S32_t[c % 3]])
                                    if KNOBS["hg_dense"] < 2:
                                        yield
                                    if KNOBS["hg_snap"] == "pool":
                                        TS("pool", S_bf[:, c, :], S32[c % 3][:], dcs[b][:, c:c + 1], 1.0, ALU.mult, ALU.mult,
                                           [S32_t[c % 3], dcs_t[b][c // 8]], [Sbf_t[c]])
                                    elif KNOBS["hg_snap"] == "act":
                                        ACT(S_bf[:, c, :], S32[c % 3][:], AF.Copy, [S32_t[c % 3], dcs_t[b][c // 8]], [Sbf_t[c]], scale=dcs[b][:, c:c + 1])
                                    else:
                                        TS("dve", S_bf[:, c, :], S32[c % 3][:], dcs[b][:, c:c + 1], None, ALU.mult, ALU.bypass,
                                           [S32_t[c % 3], dcs_t[b][c // 8]], [Sbf_t[c]])
                                    if KNOBS["hg_dense"] < 1:
                                        yield
                            if KNOBS["hg_dense"] >= 1:
                                yield
                        def obank(tt):
                            alt = KNOBS["hg_ob2"] and tt % 2 == 1
                            return (4, 5) if alt else (7, 3)

                        def out_mms(tt):
                            ob = obank(tt)[0]
                            for q4 in range(4):
                                i = tt * 4 + q4
                                oreg = banks[ob][:, q4 * 128:(q4 + 1) * 128]
                                c0, c1 = 2 * i, 2 * i + 1
                                MM(oreg, v_tok[b][:, i * 128:(i + 1) * 128], scm[:, i * 128:(i + 1) * 128], True, False, [v_t[b][tt], scm_t[tt]], [bt[ob]])
                                if c0 >= 1:
                                    MM(banks[ob][:, q4 * 128:q4 * 128 + 64], S_bf[:, c0 - 1, :], qgT[b][:, c0 * 64:(c0 + 1) * 64], False, False,
                                       [Sbf_t[c0 - 1], qg_t[b][tt]], [bt[ob]])
                                MM(banks[ob][:, q4 * 128 + 64:(q4 + 1) * 128], S_bf[:, c1 - 1, :], qgT[b][:, c1 * 64:(c1 + 1) * 64], False, True,
                                   [Sbf_t[c1 - 1], qg_t[b][tt]], [bt[ob]])

                        out_mms(0)
                        for tt in range(4):
                            ts = tsl(tt)
                            ob, nb_ = obank(tt)
                            ACT(osq[:], banks[ob][:], AF.Square, [bt[ob]], [osq_t])
                            yield
                            if tt + 1 < 4 and KNOBS["hg_ob2"]:
                                out_mms(tt + 1)
                            MM(banks[nb_][:], ones_bf[:], osq[:], True, True, [osq_t, ones_t], [bt[nb_]])
                            ACT(lnv[:], banks[nb_][:], AF.Ln, [bt[nb_]], [lnv_t], scale=1.0 / 128, bias=EPS)
                            yield
                            ACT(rstd[:], lnv[:], AF.Exp, [lnv_t], [rstd_t], scale=-0.5)
                            yield
                            TT("dve", t1[:], banks[ob][:], rstd[:], ALU.mult, [bt[ob], rstd_t], [t1_t])
                            yield
                            STT(y_hgT[:, hh, ts], t1[:], vecs[:, 32 + hh:33 + hh], sog[b][:, ts], ALU.mult, ALU.mult,
                                [t1_t, sog_t[b][tt], vecs_t], [yhg_t[hh][tt]])
                            yield
                            if tt + 1 < 4 and not KNOBS["hg_ob2"]:
                                out_mms(tt + 1)

                    def run_rr(gens, weights=None):
                        gens = list(gens)
                        if KNOBS["hg_mode"] == "seq":
                            for g_ in gens[1:] + gens[:1]:
                                for _ in g_:
                                    pass
                            return
                        weights = list(weights) if weights else [1.0] * len(gens)
                        credit = [0.0] * len(gens)
                        alive = [True] * len(gens)
                        while any(alive):
                            for i_, g_ in enumerate(gens):
                                if not alive[i_]:
                                    continue
                                credit[i_] += weights[i_]
                                while credit[i_] >= 1.0 and alive[i_]:
                                    credit[i_] -= 1.0
                                    try:
                                        next(g_)
                                    except StopIteration:
                                        alive[i_] = False

                    load_whg(0)
                    load_whg(1)
                    norm(8, norm_to_xn)
                    run_rr([stage_a1(0), stage_a2(0)])
                    for hh in range(4):
                        gens = [stage_b(hh)]
                        wts = [KNOBS["hg_bw"]]
                        if hh + 1 < 4:
                            if KNOBS["hg_order"] == 0:
                                gens += [stage_a1(hh + 1), stage_a2(hh + 1)]
                            elif KNOBS["hg_order"] == 1:
                                gens = [stage_a1(hh + 1), stage_a2(hh + 1)] + gens
                            else:
                                gens = [stage_a1(hh + 1)] + gens + [stage_a2(hh + 1)]
                            wts = [1.0] * len(gens)
                        if hh + 2 < 4:
                            load_whg(hh + 2)
                        run_rr(gens, wts)
                    P.barrier()
                    if KNOBS["flush"]:
                        P.flush()
                if stage >= 3:
                    merge(wmA_d, y_hgT, yhg_t, "A")

        def attention():
            with ExitStack() as ph:
                y_attT = sbuf(ph, "y_attT", [128, 4, S], BF16)
                yatt_t = toks(4, 4)
                with ExitStack() as wk:
                    watt_sb = [sbuf(wk, "watt_sb%d" % i, [128, 3072], BF16) for i in range(2)]
                    bias_sb = [sbuf(wk, "bias_sb%d" % i, [128, 2, 256], F32) for i in range(2)]
                    qR = [sbuf(wk, "qR%d" % i, [128, S], BF16) for i in range(2)]
                    kR = [[sbuf(wk, "kR%d_%d" % (i, a), [128, S], BF16) for a in range(2)] for i in range(2)]
                    vT = sbuf(wk, "vT", [128, S], BF16)
                    vaug = [sbuf(wk, "vaug%d" % i, [128, 16, 2, 65], BF16) for i in range(2)]
                    acc = sbuf(wk, "acc", [65, 2, S], F32)
                    tmp = [sbuf(wk, "tmp%d" % i, [128, 512], F32) for i in range(2)]
                    pT = [sbuf(wk, "pT%d" % i, [128, 3968], BF16) for i in range(2)]
                    watt_t = toks(2)
                    bias_t = toks(2)
                    qT_t = toks(2, 4)
                    kT_t = toks(2, 4)
                    vT_t = toks(4)
                    kz_t, = toks(1)
                    vaug_t = toks(2, 16)
                    acc_t = toks(2)
                    tmp_t = toks(2)
                    pT_t = toks(2, 8)
                    vone_t = toks(2)
                    ds_watt = [P.dsem() for _ in range(2)]
                    ds_bias = [P.dsem() for _ in range(2)]
                    cnt = {"s": 0, "p": 0, "x": 0, "j": 0}
                    PROJ_BANKS = (0, 1, 4)

                    def pbank():
                        bk = PROJ_BANKS[cnt["j"] % 3]
                        cnt["j"] += 1
                        return bk

                    def load_watt(k):
                        hp, g = k // 3, k % 3
                        b = k % 2
                        DMAG("pool", [(watt_sb[b][:, 0:2048], watt_d[hp, g, :, 0:2048], (), [watt_t[b]]),
                                      (watt_sb[b][:, 2048:3072], watt_d[hp, g, :, 2048:3072], (), [watt_t[b]])], ds_watt[b])

                    def load_bias(k):
                        hp, g = k // 3, k % 3
                        DMA("sp", bias_sb[k % 2][:], biasT_d[hp, g], ds_bias[k % 2], writes=[bias_t[k % 2]])

                    def geom(g):
                        window, d = ATT_GROUPS[g]
                        nb = S // (128 * d)
                        units = [(r, nk) for r in range(d) for nk in range(nb)]
                        nq = [256 if nk + 1 < nb else 128 for (r, nk) in units]
                        col0 = [0] * 16
                        for u in range(1, 16):
                            col0[u] = col0[u - 1] + nq[u - 1]
                        groups = []
                        cur, w_ = [], 0
                        for u in range(16):
                            if w_ + nq[u] > 512:
                                groups.append(cur)
                                cur, w_ = [], 0
                            cur.append(u)
                            w_ += nq[u]
                        groups.append(cur)
                        return d, nb, units, nq, col0, groups

                    def res_out(t, d, tt):
                        L4 = 512 // d
                        return t.rearrange("p (r l) -> p r l", r=d)[:, :, tt * L4:(tt + 1) * L4]

                    def res_in(bank_ap, d):
                        return bank_ap.rearrange("p (l r) -> p r l", r=d)

                    def proj_gen(k):
                        hp, g = k // 3, k % 3
                        b = k % 2
                        if k + 1 < 12:
                            load_watt(k + 1)
                        w = watt_sb[b]
                        wt = watt_t[b]
                        d, nb, units, nq, col0, groups = geom(g)

                        def wcol(s_, kc):
                            return w[:, (s_ * 8 + kc) * 128:(s_ * 8 + kc + 1) * 128]

                        for tt in range(4):
                            ts = tsl(tt)
                            bk = pbank()
                            for kc in range(8):
                                MM(banks[bk][:], wcol(0, kc), xnT[:, kc, ts], kc == 0, kc == 7, [wt, xn_t[kc][tt]], [bt[bk]])
                            ACT(res_out(qR[b][:, :], d, tt), res_in(banks[bk][:, :], d), AF.Copy, [bt[bk]], [qT_t[b][tt]], scale=0.125)
                            yield
                            bk = pbank()
                            for kc in range(8):
                                MM(banks[bk][:], wcol(1, kc), xnT[:, kc, ts], kc == 0, kc == 7, [wt, xn_t[kc][tt]], [bt[bk]])
                            ACT(res_out(kR[b][0][0:64, :], d, tt), res_in(banks[bk][0:64, :], d), AF.Copy, [bt[bk], kz_t], [kT_t[b][tt]])
                            ACT(res_out(kR[b][1][64:128, :], d, tt), res_in(banks[bk][64:128, :], d), AF.Copy, [bt[bk], kz_t], [kT_t[b][tt]])
                            yield
                        for tt in range(4):
                            ts = tsl(tt)
                            bk = pbank()
                            for kc in range(8):
                                MM(banks[bk][:], wcol(2, kc), xnT[:, kc, ts], kc == 0, kc == 7, [wt, xn_t[kc][tt]], [bt[bk]])
                            ACT(res_out(vT[:, :], d, tt), res_in(banks[bk][:, :], d), AF.Copy, [bt[bk]], [vT_t[tt]])
                            yield
                        for u4 in range(4):
                            bk = pbank()
                            bkbf = banks[bk].bitcast(BF16)
                            for q4 in range(4):
                                u = u4 * 4 + q4
                                TR(bkbf[:, q4 * 128:(q4 + 1) * 128], vT[:, u * 128:(u + 1) * 128], ident, vT_t + [cst_t], [bt[bk]])
                            P.op("act", lambda h, bkbf=bkbf, u4=u4, b=b: h.activation(
                                out=vaug[b][:, u4 * 4:(u4 + 1) * 4, :, 0:64],
                                in_=bkbf[:, 0:512].rearrange("p (u a e) -> p u a e", u=4, a=2),
                                func=AF.Copy), [bt[bk]], vaug_t[b][u4 * 4:(u4 + 1) * 4])
                            yield

                    def units_gen(k):
                        hp, g = k // 3, k % 3
                        b = k % 2
                        bsb, bst = bias_sb[b], bias_t[b]
                        d, nb, units, nq, col0, groups = geom(g)
                        gof = {}
                        for m, grp in enumerate(groups):
                            for u in grp:
                                gof[u] = m
                        if g == 0:
                            MS("dve", acc[:], 0.0, acc_t)
                        if k + 1 < 12:
                            load_bias(k + 1)

                        def bias_ap(a, width):
                            base = bsb[:, a, 0:1]
                            pstep = base.ap[0][0]
                            if d == 16:
                                return bass.AP(bsb, base.offset, [[pstep, 128], [0, width // 128], [1, 128]])
                            assert width in (512, 384)
                            if width == 512:
                                return bass.AP(bsb, base.offset, [[pstep, 128], [0, 2], [1, 256]])
                            return None

                        def score_group(a, m):
                            grp = groups[m]
                            sbk = 5 + cnt["s"] % 3
                            x2 = cnt["x"] % 2
                            cnt["s"] += 1
                            cnt["x"] += 1
                            off = 0
                            for u in grp:
                                r, nk = units[u]
                                t0 = nk * 128 * d + r
                                c_ = u * 128
                                ktts = sorted(set([t0 // 512, (t0 + 127 * d) // 512]))
                                qtts = sorted(set(range(t0 // 512, (t0 + (nq[u] - 1) * d) // 512 + 1)))
                                MM(banks[sbk][:, off:off + nq[u]], kR[b][a][:, c_:c_ + 128], qR[b][:, c_:c_ + nq[u]], True, True,
                                   [kT_t[b][t] for t in ktts] + [qT_t[b][t] for t in qtts] + [kz_t], [bt[sbk]])
                                off += nq[u]
                            c0 = col0[grp[0]]
                            bap = bias_ap(a, off)
                            if bap is not None:
                                if d == 16:
                                    o_ap = tmp[x2][:, 0:off].rearrange("p (u i) -> p u i", i=128)
                                    i_ap = banks[sbk][:, 0:off].rearrange("p (u i) -> p u i", i=128)
                                else:
                                    o_ap = tmp[x2][:, 0:off].rearrange("p (u i) -> p u i", i=256)
                                    i_ap = banks[sbk][:, 0:off].rearrange("p (u i) -> p u i", i=256)
                                TT("dve", o_ap, i_ap, bap, ALU.add, [bt[sbk], bst], [tmp_t[x2]])
                            else:
                                TT("dve", tmp[x2][:, 0:256], banks[sbk][:, 0:256], bsb[:, a, 0:256], ALU.add, [bt[sbk], bst], [tmp_t[x2]])
                                TT("dve", tmp[x2][:, 256:384], banks[sbk][:, 256:384], bsb[:, a, 0:128], ALU.add, [bt[sbk], bst], [tmp_t[x2]])
                            ACT(pT[a][:, c0:c0 + off], tmp[x2][:, 0:off], AF.Exp, [tmp_t[x2]], [pT_t[a][m]])

                        def pv_group(a, j):
                            pvb = 2 + cnt["p"] % 2
                            cnt["p"] += 1
                            for q4 in range(4):
                                u = 4 * j + q4
                                r, n = units[u]
                                reg = banks[pvb][0:65, q4 * 128:(q4 + 1) * 128]
                                has_prev = n > 0
                                MM(reg, vaug[b][:, u, a, :], pT[a][:, col0[u]:col0[u] + 128], True, not has_prev,
                                   [vaug_t[b][u], vone_t[b], pT_t[a][gof[u]]], [bt[pvb]])
                                if has_prev:
                                    MM(reg, vaug[b][:, u - 1, a, :], pT[a][:, col0[u - 1] + 128:col0[u - 1] + 256], False, True,
                                       [vaug_t[b][u - 1], vone_t[b], pT_t[a][gof[u - 1]]], [bt[pvb]])
                            if d == 1:
                                dst = acc[:, a, j * 512:(j + 1) * 512]
                                src = banks[pvb][0:65, :]
                            elif d == 4:
                                dst = acc[:, a, j:j + 4 * 511 + 1:4]
                                src = banks[pvb][0:65, :]
                            else:
                                dst = acc[:, a, :].rearrange("p (i r) -> p r i", r=16)[:, 4 * j:4 * j + 4, :]
                                src = banks[pvb][0:65, :].rearrange("p (r i) -> p r i", r=4)
                            TT("dve", dst, dst, src, ALU.add, [bt[pvb], acc_t[a]], [acc_t[a]])

                        sched = []
                        emitted = -1
                        for j in range(4):
                            need = min(gof[4 * j + 3] + 1, len(groups) - 1)
                            for m in range(emitted + 1, need + 1):
                                sched.append(("S", m))
                            emitted = max(emitted, need)
                            sched.append(("P", j))
                        for kind, idx in sched:
                            for a in range(2):
                                if kind == "S":
                                    score_group(a, idx)
                                else:
                                    pv_group(a, idx)
                                yield
                        if g == 2:
                            for a in range(2):
                                for tt in range(4):
                                    ts = tsl(tt)
                                    nbk = 2 + (a * 4 + tt) % 2
                                    MM(banks[nbk][0:64, :], ones_f[64:65, 0:64], acc[64:65, a, ts], True, True, [acc_t[a], ones_t], [bt[nbk]])
                                    ACT(lnv[0:64, :], banks[nbk][0:64, :], AF.Ln, [bt[nbk]], [lnv_t])
                                    ACT(rstd[0:64, :], lnv[0:64, :], AF.Exp, [lnv_t], [rstd_t], scale=-1.0)
                                    TT("dve", y_attT[a * 64:(a + 1) * 64, hp, ts], acc[0:64, a, ts], rstd[0:64, :], ALU.mult, [acc_t[a], rstd_t], [yatt_t[hp][tt]])
                                    yield

                    for b in range(2):
                        MS("dve", vaug[b][:, :, :, 64:65], 1.0, vaug_t[b] + [vone_t[b]])
                        MS("dve", kR[b][0][64:128, :], 0.0, [kz_t])
                        MS("dve", kR[b][1][0:64, :], 0.0, [kz_t])
                    load_watt(0)
                    load_bias(0)
                    for _ in proj_gen(0):
                        pass
                    for k in range(12):
                        gu = units_gen(k)
                        gp = proj_gen(k + 1) if k + 1 < 12 else iter(())
                        alive_u, alive_p = True, True
                        cu, cp_ = 0.0, 0.0
                        while alive_u or alive_p:
                            cu += KNOBS["at_uw"]
                            while alive_u and cu >= 1.0:
                                cu -= 1.0
                                try:
                                    next(gu)
                                except StopIteration:
                                    alive_u = False
                            cp_ += KNOBS["at_pw"]
                            while alive_p and cp_ >= 1.0:
                                cp_ -= 1.0
                                try:
                                    next(gp)
                                except StopIteration:
                                    alive_p = False
                    P.barrier()
                    if KNOBS["flush"]:
                        P.flush()
                if stage >= 0:
                    merge(wmB_d, y_attT, yatt_t, "B")

        if stage >= 0:
            ffn(0, 0)
        if stage >= 2 or stage == -1:
            hgrn2()
        if stage == -2:
            norm(8, norm_to_xn)
        if stage >= 4 or stage == -2:
            attention()
        if stage >= 5:
            ffn(1, 16)
        if stage >= 6:
            norm(24, norm_inplace)
        ds_outs = [ds_out] + [P.dsem() for _ in range(3)]
        ev_out = [DMAG("sp", [(outT_d[c * 128:(c + 1) * 128, tsl(tt)], hT[:, c, tsl(tt)], [hT_t[c][tt]], ()) for c in range(8)], ds_outs[tt])
                  for tt in range(4)]
        P.wait_all("sp", ev_out)
        P.flush()
    return nc


def _const_tables():
    cst = np.zeros((128, 640), np.float32)
    cst[:, 0:128] = np.eye(128, dtype=np.float32)
    s = np.arange(128)[:, None]
    t = np.arange(128)[None, :]
    for q in range(4):
        cst[:, 128 + q * 128:256 + q * 128] = ((s <= t) & (s // 64 == t // 64)).astype(np.float32)
    n_heads = 24
    slopes = np.exp2(-8.0 * np.arange(1, n_heads + 1, dtype=np.float32) / n_heads).astype(np.float32)
    biasT = np.zeros((4, 3, 128, 2, 256), np.float32)
    j = np.arange(128)[:, None].astype(np.float32)
    i = np.arange(128)[None, :].astype(np.float32)
    for hp in range(4):
        for g, (window, d) in enumerate(ATT_GROUPS):
            for a in range(2):
                sl = slopes[g * 8 + hp * 2 + a]
                biasT[hp, g, :, a, 0:128] = np.where(i >= j, -sl * (d * (i - j)), NEG)
                biasT[hp, g, :, a, 128:256] = np.where(i <= j, -sl * (d * (i + 128.0 - j)), NEG)
    return cst, biasT.reshape(4, 3, 128, 512)


def _prep_weights(inp):
    f = lambda a: np.ascontiguousarray(a, dtype=np.float32)
    out = {}
    for tag, kgu, kd in (("1", "ffn1_w_gate_up", "ffn1_w_down"), ("2", "ffn2_w_gate_up", "ffn2_w_down")):
        W = np.asarray(inp[kgu])[0]
        W5 = W.reshape(8, 128, 2, NJ, 128)
        out["wgu" + tag] = f(W5.transpose(3, 1, 2, 0, 4).reshape(NJ, 128, 2048))
        out["wd" + tag] = f(np.asarray(inp[kd])[0].reshape(NJ, 128, 1024))
    Win = np.asarray(inp["w_in"])[0]
    Whg = Win[:, 0:2048].reshape(8, 128, 4, 4, 128)
    out["whg"] = f(Whg.transpose(3, 1, 2, 0, 4).reshape(4, 128, 4096))
    Watt = Win[:, 2048:6656].reshape(8, 128, 3, 3, 4, 128)
    out["watt"] = f(Watt.transpose(4, 2, 1, 3, 0, 5).reshape(4, 3, 128, 3072))
    for tag, c0, kb in (("A", 6656, "w_branch_hg"), ("B", 7680, "w_branch_att")):
        Wg = Win[:, c0:c0 + 1024].reshape(8, 128, 8, 128)
        Wb = np.asarray(inp[kb])[0].reshape(4, 128, 8, 128)
        wm = np.concatenate([Wg.transpose(2, 1, 0, 3).reshape(8, 128, 1024), Wb.transpose(2, 1, 0, 3).reshape(8, 128, 512)], axis=2)
        out["wm" + tag] = f(wm)
    Wo = np.asarray(inp["w_out"])[0].reshape(2, 4, 128, 1024)
    out["wo"] = f(Wo.transpose(0, 2, 1, 3).reshape(2, 128, 4096))
    vecs = np.zeros((128, 48), np.float32)
    vecs[:, 0:8] = np.asarray(inp["ffn1_norm"])[0].reshape(8, 128).T
    vecs[:, 8:16] = np.asarray(inp["mix_norm"])[0].reshape(8, 128).T
    vecs[:, 16:24] = np.asarray(inp["ffn2_norm"])[0].reshape(8, 128).T
    vecs[:, 24:32] = np.asarray(inp["final_norm"]).reshape(8, 128).T
    vecs[:, 32:36] = np.asarray(inp["hg_out_norm"])[0].reshape(4, 128).T
    lbs = np.asarray(inp["hg_lower_bounds"])
    vecs[:, 36:40] = lbs[0].reshape(4, 128).T
    vecs[:, 40:44] = lbs[1].reshape(4, 128).T
    out["vecs"] = vecs
    cst, biasT = _const_tables()
    out["cst"] = cst
    out["biasT"] = biasT
    return out


_NC_CACHE = {}


def _get_nc(stage=99):
    if stage not in _NC_CACHE:
        _NC_CACHE[stage] = build_program(stage)
    return _NC_CACHE[stage]


def kernel(**inputs):
    x = np.asarray(inputs["x"], dtype=np.float32)
    shared = _prep_weights(inputs)
    nc = _get_nc()
    in_maps = []
    for b in range(NCORES):
        m = dict(shared)
        m["xT"] = np.ascontiguousarray(x[b].T)
        in_maps.append(m)
    res = run_bass_kernel_spmd(nc, in_maps, core_ids=list(range(NCORES)))
    out = np.stack([np.ascontiguousarray(r["outT"].T) for r in res.results], axis=0)
    return out.astype(np.float32)
```
